# Optimizing a Trainium2 kernel written in Bass

```python
import math
import jax, jax.numpy as jnp
from jax import lax
import numpy as np

D_MODEL = 1024
BATCH = 16
SEQ = 4096
DEPTH = 4

MEM_LEN = 256
N_MEM_HEADS = 4
MEM_HEAD_DIM = D_MODEL // N_MEM_HEADS
D_FF = 2816
MIX_WIDTH = D_MODEL
GROUP_WIDTH = MIX_WIDTH // 4
MLSTM_HEADS = 4
MLSTM_HEAD_DIM = GROUP_WIDTH // MLSTM_HEADS
MLSTM_CHUNK = 64
CONV_WIDTH = 4
POOL_GROUPS = 4
POOL_WINDOWS = (2, 4, 8, 16)
POOL_GROUP_DIM = GROUP_WIDTH // POOL_GROUPS
DIL_HEADS = 4
DIL_HEAD_DIM = GROUP_WIDTH // DIL_HEADS
DIL_PATTERNS = ((128, 1), (512, 4), (2048, 16))
DIFF_HEADS = 4
DIFF_HEAD_DIM = GROUP_WIDTH // DIFF_HEADS
DIFF_QK_HALF = DIFF_HEAD_DIM // 2
ATTN_BLOCK = 128
T5_BUCKETS = 32
T5_MAX_DIST = 2048
N_BIAS_HEADS = DIL_HEADS + DIFF_HEADS
IN_WIDTHS = (GROUP_WIDTH, GROUP_WIDTH, GROUP_WIDTH, GROUP_WIDTH, MLSTM_HEADS, MLSTM_HEADS,
             GROUP_WIDTH,
             GROUP_WIDTH, GROUP_WIDTH, GROUP_WIDTH,
             GROUP_WIDTH, GROUP_WIDTH, GROUP_WIDTH)
IN_WIDTH = sum(IN_WIDTHS)
RMS_EPS = 1e-6
SUBLN_EPS = 1e-5

kernel_name = "hybrid_parallel_groups_mlstm_pool_dilated_diff"


def _rms_norm(x, g, eps=RMS_EPS):
    xf = x.astype(jnp.float32)
    y = xf * lax.rsqrt(jnp.mean(xf * xf, axis=-1, keepdims=True) + eps)
    return (y * g.astype(jnp.float32)).astype(x.dtype)


def _swiglu(u, w_gate, w_up, w_down):
    return (jax.nn.silu(u @ w_gate) * (u @ w_up)) @ w_down


def _split_cols(z, widths):
    out, off = [], 0
    for w in widths:
        out.append(z[..., off:off + w])
        off += w
    return out


def _t5_bucket(dist):
    max_exact = T5_BUCKETS // 2
    d = jnp.maximum(dist, 1).astype(jnp.float32)
    large = max_exact + (jnp.log(d / max_exact) / math.log(T5_MAX_DIST / max_exact)
                         * (T5_BUCKETS - max_exact)).astype(jnp.int32)
    large = jnp.minimum(large, T5_BUCKETS - 1)
    return jnp.where(dist < max_exact, dist, large)


def _causal_conv(u, w, b):
    K, S = w.shape[0], u.shape[1]
    up = jnp.pad(u, ((0, 0), (K - 1, 0), (0, 0)))
    out = b
    for j in range(K):
        out = out + up[:, j:j + S] * w[j]
    return out


def _mlstm(q, k, v, o, ig, fg, conv_w, conv_b, gate_b, norm_g):
    B, S, _ = q.shape
    H, dh, L = MLSTM_HEADS, MLSTM_HEAD_DIM, MLSTM_CHUNK
    f32 = jnp.float32
    qk = jax.nn.silu(_causal_conv(jnp.concatenate([q, k], axis=-1), conv_w, conv_b))
    q, k = qk[..., :GROUP_WIDTH], qk[..., GROUP_WIDTH:]
    nc = S // L

    def to_chunks(t):
        return t.astype(f32).reshape(B, nc, L, H, dh).transpose(1, 0, 3, 2, 4)

    def gate_chunks(t):
        return t.reshape(B, nc, L, H).transpose(1, 0, 3, 2)

    qc, kc, vc = to_chunks(q), to_chunks(k) * (dh ** -0.5), to_chunks(v)
    gb = gate_b.astype(f32)
    igc = gate_chunks(ig.astype(f32) + gb[:H])
    lfc = gate_chunks(jax.nn.log_sigmoid(fg.astype(f32) + gb[H:]))
    causal = jnp.tril(jnp.ones((L, L), dtype=bool))

    def step(carry, xs):
        C, n, m = carry
        qq, kk, vv, ii, lf = xs
        b = jnp.cumsum(lf, axis=-1)
        dmat = b[..., :, None] - b[..., None, :] + ii[..., None, :]
        dmat = jnp.where(causal, dmat, -jnp.inf)
        inter_log = b + m[..., None]
        m_t = jnp.maximum(inter_log, jnp.max(dmat, axis=-1))
        inter_w = jnp.exp(inter_log - m_t)
        sc = jnp.einsum('bhtd,bhsd->bhts', qq, kk) * jnp.exp(dmat - m_t[..., None])
        num = inter_w[..., None] * jnp.einsum('bhtd,bhde->bhte', qq, C) \
            + jnp.einsum('bhts,bhse->bhte', sc, vv)
        den = inter_w * jnp.einsum('bhtd,bhd->bht', qq, n) + jnp.sum(sc, axis=-1)
        hh = num / jnp.maximum(jnp.abs(den), jnp.exp(-m_t))[..., None]
        b_last = b[..., -1]
        m_new = m_t[..., -1]
        w_s = jnp.exp(b_last[..., None] - b + ii - m_new[..., None])
        carry_w = jnp.exp(b_last + m - m_new)
        C_new = carry_w[..., None, None] * C + jnp.einsum('bhs,bhsd,bhse->bhde', w_s, kk, vv)
        n_new = carry_w[..., None] * n + jnp.einsum('bhs,bhsd->bhd', w_s, kk)
        return (C_new, n_new, m_new), hh

    init = (jnp.zeros((B, H, dh, dh), f32), jnp.zeros((B, H, dh), f32), jnp.zeros((B, H), f32))
    _, hc = lax.scan(step, init, (qc, kc, vc, igc, lfc))
    hs = hc.transpose(1, 0, 3, 2, 4).reshape(B, S, H, dh)
    mu = jnp.mean(hs, axis=-1, keepdims=True)
    var = jnp.mean(jnp.square(hs - mu), axis=-1, keepdims=True)
    hn = (hs - mu) * lax.rsqrt(var + RMS_EPS) * norm_g.astype(f32).reshape(H, dh)
    y = hn.reshape(B, S, H * dh) * jax.nn.sigmoid(o.astype(f32))
    return y.astype(o.dtype)


def _multiscale_pool(u, w_grp, scale):
    B, S, _ = u.shape
    G, c = POOL_GROUPS, POOL_GROUP_DIM
    f32 = jnp.float32
    ug = u.astype(f32).reshape(B, S, G, c)
    cs = jnp.pad(jnp.cumsum(ug, axis=1), ((0, 0), (1, 0), (0, 0), (0, 0)))
    t = jnp.arange(S)[:, None]
    win = jnp.array(POOL_WINDOWS, dtype=jnp.int32)[None, :]
    lo = jnp.maximum(t + 1 - win, 0)
    g = jnp.arange(G)[None, :]
    total = cs[:, 1:] - cs[:, lo, g]
    mean = total / jnp.minimum(t + 1, win).astype(f32)[None, :, :, None]
    y = jnp.einsum('bsgc,gcd->bsgd', mean - ug, w_grp.astype(f32))
    return (y.reshape(B, S, G * c) * scale.astype(f32)).astype(u.dtype)


def _dilated_branch(q, k, v, bias, dil, n_back):
    B, S, H, dh = q.shape
    f32 = jnp.float32
    L = S // dil
    blk = n_back
    nb = -(-L // blk)
    Lp = nb * blk

    def to_blocks(t):
        t = t.reshape(B, L, dil, H, dh).transpose(0, 2, 3, 1, 4)
        t = jnp.pad(t, ((0, 0), (0, 0), (0, 0), (0, Lp - L), (0, 0)))
        return t.reshape(B, dil, H, nb, blk, dh)

    def with_prev(t):
        prev = jnp.pad(t, ((0, 0), (0, 0), (0, 0), (1, 0), (0, 0), (0, 0)))[:, :, :, :nb]
        return jnp.concatenate([prev, t], axis=4)

    qb = to_blocks(q)
    kb = with_prev(to_blocks(k))
    vb = with_prev(to_blocks(v)).astype(f32)
    s = jnp.einsum('brhnqc,brhnkc->brhnqk', qb, kb).astype(f32) * (dh ** -0.5)
    qi = jnp.arange(blk)[:, None]
    kj = jnp.arange(2 * blk)[None, :]
    delta = qi + blk - kj
    key_idx = jnp.arange(nb)[:, None, None] * blk - blk + kj[None]
    valid = ((delta >= 0) & (delta <= n_back))[None] & (key_idx >= 0)
    s = s + bias[:, jnp.clip(delta, 0, n_back)][None, None, :, None]
    s = jnp.where(valid[None, None, None], s, -jnp.inf)
    m = jnp.max(s, axis=-1, keepdims=True)
    p = jnp.exp(s - m)
    l = jnp.sum(p, axis=-1)
    o = jnp.einsum('brhnqk,brhnkc->brhnqc', p, vb) / l[..., None]
    lse = m[..., 0] + jnp.log(l)
    o = o.reshape(B, dil, H, Lp, dh)[:, :, :, :L].transpose(0, 3, 1, 2, 4).reshape(B, S, H, dh)
    lse = lse.reshape(B, dil, H, Lp)[:, :, :, :L].transpose(0, 3, 1, 2).reshape(B, S, H)
    return o, lse


def _dilated_attention(q, k, v, dil_biases):
    B, S, H, dh = q.shape
    outs, lses = [], []
    for (w, d), b in zip(DIL_PATTERNS, dil_biases):
        o, lse = _dilated_branch(q, k, v, b, d, w // d)
        outs.append(o)
        lses.append(lse)
    wts = jax.nn.softmax(jnp.stack(lses), axis=0)
    o = jnp.sum(wts[..., None] * jnp.stack(outs), axis=0)
    return o.reshape(B, S, H * dh).astype(q.dtype)


def _diff_attention(q, k, v, lam_vecs, subln_g, bias_by_dist, lam_init):
    B, S, H, dv = q.shape
    dk = DIFF_QK_HALF
    f32 = jnp.float32
    scale = dk ** -0.5
    lv = lam_vecs.astype(f32)
    lam = jnp.exp(jnp.sum(lv[0] * lv[1])) - jnp.exp(jnp.sum(lv[2] * lv[3])) + lam_init
    kh = k.transpose(0, 2, 1, 3)
    k1, k2 = kh[..., :dk], kh[..., dk:]
    vh = v.transpose(0, 2, 1, 3).astype(f32)
    nq = S // ATTN_BLOCK
    qb = q.transpose(0, 2, 1, 3).reshape(B, H, nq, ATTN_BLOCK, 2 * dk).transpose(2, 0, 1, 3, 4)
    key_pos = jnp.arange(S)

    def block(args):
        qblk, start = args
        dist = start + jnp.arange(ATTN_BLOCK)[:, None] - key_pos[None, :]
        causal = dist >= 0
        bias = bias_by_dist[:, jnp.clip(dist, 0, S - 1)]

        def attn_map(qq, kk):
            s = jnp.einsum('bhqd,bhkd->bhqk', qq, kk).astype(f32) * scale + bias
            return jax.nn.softmax(jnp.where(causal, s, -jnp.inf), axis=-1)

        a = attn_map(qblk[..., :dk], k1) - lam * attn_map(qblk[..., dk:], k2)
        return jnp.einsum('bhqk,bhkd->bhqd', a, vh)

    o = lax.map(block, (qb, jnp.arange(nq) * ATTN_BLOCK))
    o = o.transpose(1, 0, 3, 2, 4).reshape(B, S, H, dv)
    o = o * lax.rsqrt(jnp.mean(o * o, axis=-1, keepdims=True) + SUBLN_EPS) * subln_g.astype(f32)
    o = o * (1.0 - lam_init)
    return o.reshape(B, S, H * dv).astype(q.dtype)


def _hybrid_mixer(u, w_in, w_out, conv_w, conv_b, gate_b, mlstm_g, pool_w, pool_scale,
                  dil_biases, lam_vecs, subln_g, diff_bias, lam_init):
    B, S, _ = u.shape
    z = u @ w_in
    qa, ka, va, oa, ia, fa, pb, qc, kc, vc, qd, kd, vd = _split_cols(z, IN_WIDTHS)
    ya = _mlstm(qa, ka, va, oa, ia, fa, conv_w, conv_b, gate_b, mlstm_g)
    yb = _multiscale_pool(pb, pool_w, pool_scale)
    yc = _dilated_attention(qc.reshape(B, S, DIL_HEADS, DIL_HEAD_DIM),
                            kc.reshape(B, S, DIL_HEADS, DIL_HEAD_DIM),
                            vc.reshape(B, S, DIL_HEADS, DIL_HEAD_DIM), dil_biases)
    yd = _diff_attention(qd.reshape(B, S, DIFF_HEADS, DIFF_HEAD_DIM),
                         kd.reshape(B, S, DIFF_HEADS, DIFF_HEAD_DIM),
                         vd.reshape(B, S, DIFF_HEADS, DIFF_HEAD_DIM),
                         lam_vecs, subln_g, diff_bias, lam_init)
    y = jnp.concatenate([ya, yb, yc, yd], axis=-1)
    return y @ w_out


def _cross_attention(u, mem_n, wq, wkv, wo):
    B, S, D = u.shape
    M = mem_n.shape[1]
    q = (u @ wq).reshape(B, S, N_MEM_HEADS, MEM_HEAD_DIM)
    kv = mem_n @ wkv
    k = kv[..., :D].reshape(B, M, N_MEM_HEADS, MEM_HEAD_DIM)
    v = kv[..., D:].reshape(B, M, N_MEM_HEADS, MEM_HEAD_DIM)
    s = jnp.einsum('bshd,bmhd->bhsm', q, k).astype(jnp.float32) * (MEM_HEAD_DIM ** -0.5)
    p = jax.nn.softmax(s, axis=-1)
    o = jnp.einsum('bhsm,bmhd->bshd', p, v.astype(jnp.float32)).astype(u.dtype)
    return o.reshape(B, S, D) @ wo


def setup_inputs(seed: int = 0) -> dict:
    key = jax.random.key(seed)
    keys = jax.random.split(key, 40)
    counter = [0]
    f32 = jnp.float32
    Lr, D, F = DEPTH, D_MODEL, D_FF

    def next_key():
        kk = keys[counter[0]]
        counter[0] += 1
        return kk

    def nrm(shape, scale):
        return jax.random.normal(next_key(), shape, f32) * scale

    def gain(shape):
        return 1.0 + 0.05 * jax.random.normal(next_key(), shape, f32)

    fgate_init = jnp.broadcast_to(jnp.linspace(3.0, 6.0, MLSTM_HEADS, dtype=f32), (Lr, MLSTM_HEADS))
    return {
        "x": nrm((BATCH, SEQ, D), 1.0),
        "mem": nrm((BATCH, MEM_LEN, D), 1.0),
        "t5_bias": nrm((T5_BUCKETS, N_BIAS_HEADS), 0.5),
        "ffn1_norm": gain((Lr, D)),
        "ffn1_w_gate": nrm((Lr, D, F), D ** -0.5),
        "ffn1_w_up": nrm((Lr, D, F), D ** -0.5),
        "ffn1_w_down": nrm((Lr, F, D), F ** -0.5),
        "mix_norm": gain((Lr, D)),
        "w_in": nrm((Lr, D, IN_WIDTH), D ** -0.5),
        "mlstm_conv_w": nrm((Lr, CONV_WIDTH, 2 * GROUP_WIDTH), CONV_WIDTH ** -0.5),
        "mlstm_conv_b": nrm((Lr, 2 * GROUP_WIDTH), 0.02),
        "mlstm_gate_b": jnp.concatenate([nrm((Lr, MLSTM_HEADS), 0.1),
                                         fgate_init + nrm((Lr, MLSTM_HEADS), 0.1)], axis=-1),
        "mlstm_norm": gain((Lr, GROUP_WIDTH)),
        "pool_w": nrm((Lr, POOL_GROUPS, POOL_GROUP_DIM, POOL_GROUP_DIM), POOL_GROUP_DIM ** -0.5),
        "pool_scale": gain((Lr, GROUP_WIDTH)),
        "diff_lambda": nrm((Lr, 4, DIFF_QK_HALF), 0.1),
        "diff_subln": gain((Lr, DIFF_HEAD_DIM)),
        "w_out": nrm((Lr, MIX_WIDTH, D), MIX_WIDTH ** -0.5),
        "xattn_norm": gain((Lr, D)),
        "mem_norm": gain((Lr, D)),
        "xattn_wq": nrm((Lr, D, D), D ** -0.5),
        "xattn_wkv": nrm((Lr, D, 2 * D), D ** -0.5),
        "xattn_wo": nrm((Lr, D, D), D ** -0.5),
        "ffn2_norm": gain((Lr, D)),
        "ffn2_w_gate": nrm((Lr, D, F), D ** -0.5),
        "ffn2_w_up": nrm((Lr, D, F), D ** -0.5),
        "ffn2_w_down": nrm((Lr, F, D), F ** -0.5),
        "final_norm": gain((D,)),
    }


def reference(x, mem, t5_bias, ffn1_norm, ffn1_w_gate, ffn1_w_up, ffn1_w_down, mix_norm, w_in,
              mlstm_conv_w, mlstm_conv_b, mlstm_gate_b, mlstm_norm, pool_w, pool_scale,
              diff_lambda, diff_subln, w_out, xattn_norm, mem_norm, xattn_wq, xattn_wkv, xattn_wo,
              ffn2_norm, ffn2_w_gate, ffn2_w_up, ffn2_w_down, final_norm):
    S = x.shape[1]
    dil_biases = [t5_bias[_t5_bucket(jnp.arange(w // d + 1) * d), :DIL_HEADS].T
                  for (w, d) in DIL_PATTERNS]
    diff_bias = t5_bias[_t5_bucket(jnp.arange(S)), DIL_HEADS:].T
    h = x
    for l in range(DEPTH):
        lam_init = 0.8 - 0.6 * math.exp(-0.3 * l)
        h = h + 0.5 * _swiglu(_rms_norm(h, ffn1_norm[l]), ffn1_w_gate[l], ffn1_w_up[l], ffn1_w_down[l])
        h = h + _hybrid_mixer(_rms_norm(h, mix_norm[l]), w_in[l], w_out[l],
                              mlstm_conv_w[l], mlstm_conv_b[l], mlstm_gate_b[l], mlstm_norm[l],
                              pool_w[l], pool_scale[l], dil_biases,
                              diff_lambda[l], diff_subln[l], diff_bias, lam_init)
        h = h + _cross_attention(_rms_norm(h, xattn_norm[l]), _rms_norm(mem, mem_norm[l]),
                                 xattn_wq[l], xattn_wkv[l], xattn_wo[l])
        h = h + 0.5 * _swiglu(_rms_norm(h, ffn2_norm[l]), ffn2_w_gate[l], ffn2_w_up[l], ffn2_w_down[l])
    return _rms_norm(h, final_norm)
```

```python
import math
import os
MIXSTAGE = int(os.environ.get('MIXSTAGE', '9'))
MLSTAGE = int(os.environ.get('MLSTAGE', '9'))
from contextlib import ExitStack
import numpy as np
import concourse.bass as bass
import concourse.mybir as mybir
from concourse.bass_utils import run_bass_kernel_spmd

F32 = mybir.dt.float32
BF16 = mybir.dt.bfloat16
AF = mybir.ActivationFunctionType
ALU = mybir.AluOpType
AX = mybir.AxisListType

D = 1024
DFF = 2816
NFF = DFF // 128
INW = 2824
RMS_EPS = 1e-6
N_DMA_SEMS = 24


class Buf:
    __slots__ = ("w", "r")

    def __init__(self):
        self.w = None
        self.r = {}


class Sched:
    def __init__(self, nc, st):
        self.nc = nc
        self.eng = {"pe": nc.tensor, "act": nc.scalar, "dve": nc.vector, "pool": nc.gpsimd, "sp": nc.sync}
        self.sem = {}
        for e in self.eng:
            self.sem[e] = st.enter_context(nc.semaphore("s_" + e))
        for i in range(N_DMA_SEMS):
            self.sem[("d", i)] = st.enter_context(nc.semaphore("s_d%d" % i))
        self.cnt = {e: 0 for e in self.eng}
        self.dval = [0] * N_DMA_SEMS
        self.rr = 0
        self.known = {e: {} for e in self.eng}
        self.uid = 0
        self.ninst = 0

    def name(self, p):
        self.uid += 1
        return "%s_%d" % (p, self.uid)

    def _wait(self, e, deps):
        kn = self.known[e]
        for k, v in deps.items():
            if k == e and e == "pe":
                continue
            if kn.get(k, 0) >= v:
                continue
            self.eng[e].wait_ge(self.sem[k], v)
            self.ninst += 1
            kn[k] = v

    @staticmethod
    def _deps(reads, writes, deps):
        def add(ev):
            if ev is not None and deps.get(ev[0], 0) < ev[1]:
                deps[ev[0]] = ev[1]
        for b in reads:
            add(b.w)
        for b in writes:
            add(b.w)
            for k, v in b.r.items():
                add((k, v))

    @staticmethod
    def _mark(ev, reads, writes):
        for b in reads:
            if b.r.get(ev[0], 0) < ev[1]:
                b.r[ev[0]] = ev[1]
        for b in writes:
            b.w = ev
            b.r = {}

    def op(self, e, fn, reads=(), writes=()):
        deps = {}
        self._deps(reads, writes, deps)
        self._wait(e, deps)
        inst = fn(self.eng[e])
        self.cnt[e] += 1
        self.ninst += 1
        inst.then_inc(self.sem[e], 1)
        self._mark((e, self.cnt[e]), reads, writes)

    def dma(self, e, out, in_, reads=(), writes=(), **kw):
        i = self.rr
        self.rr = (i + 1) % N_DMA_SEMS
        k = ("d", i)
        deps = {}
        if self.dval[i] > 0:
            deps[k] = self.dval[i]
        self._deps(reads, writes, deps)
        self._wait(e, deps)
        inst = self.eng[e].dma_start(out=out, in_=in_, **kw)
        self.ninst += 1
        self.dval[i] += 16
        inst.then_inc(self.sem[k], 16)
        self._mark((k, self.dval[i]), reads, writes)

    def barrier(self):
        deps = {e: c for e, c in self.cnt.items() if c > 0}
        for i in range(N_DMA_SEMS):
            if self.dval[i] > 0:
                deps[("d", i)] = self.dval[i]
        for e in self.eng:
            self._wait(e, dict(deps))


class Ctx:
    def __init__(self, S, st):
        self.S = S
        self.st = st
        self.nc = S.nc

    def sb(self, shape, dt, name="t"):
        t = self.st.enter_context(self.nc.sbuf_tensor(self.S.name(name), list(shape), dt))
        return t, Buf()

    def ps(self, shape, dt, name="p"):
        t = self.st.enter_context(self.nc.psum_tensor(self.S.name(name), list(shape), dt))
        return t, Buf()


def bcast_row(handle_ap, nparts):
    a = handle_ap
    return bass.AP(a.tensor, a.offset, [[0, nparts]] + [list(x) for x in a.ap])


def load_w_bf16(S, dst, dbuf, src, kchunks, ncols):
    for c in range(kchunks):
        S.dma("pool", dst[:, c, :], src[c * 128:(c + 1) * 128, :], writes=[dbuf], max_dma_last_dim=4096)


def norm_pre(S, C, h_src, r0, ntile, gain, gbuf, res, lq="sp"):
    hn, u, ss = res["hn"], res["u"], res["ss"]
    res["i"] = res.get("i", 0) + 1
    sst, ssb = ss[res["i"] % len(ss)]
    tiles = []
    us = []
    for i in range(ntile):
        res["j"] = res.get("j", 0) + 1
        ht, hb = hn[res["j"] % len(hn)]
        res["k"] = res.get("k", 0) + 1
        ut, ub = u[res["k"] % len(u)]
        rows = slice(r0 + i * 128, r0 + (i + 1) * 128)
        S.dma(lq, ht[:], h_src[rows, :], writes=[hb])
        S.op("act", lambda e: e.activation(out=ut[:], in_=ht[:], func=AF.Square, accum_out=sst[:, i:i + 1]),
             reads=[hb], writes=[ub, ssb])
        tiles.append((ht, hb))
        us.append((ut, ub))
    S.op("dve", lambda e: e.tensor_scalar(out=sst[:, 4:4 + ntile], in0=sst[:, 0:ntile], scalar1=1.0 / D, scalar2=RMS_EPS,
                                          op0=ALU.mult, op1=ALU.add), reads=[ssb], writes=[ssb])
    S.op("act", lambda e: e.activation(out=sst[:, 8:8 + ntile], in_=sst[:, 4:4 + ntile], func=AF.Sqrt), reads=[ssb], writes=[ssb])
    S.op("dve", lambda e: e.reciprocal(out=sst[:, 12:12 + ntile], in_=sst[:, 8:8 + ntile]), reads=[ssb], writes=[ssb])
    for i in range(ntile):
        ht, hb = tiles[i]
        ut, ub = us[i]
        S.op("dve", lambda e: e.scalar_tensor_tensor(out=ut[:], in0=ht[:], scalar=sst[:, 12 + i:13 + i], in1=gain[:],
                                                     op0=ALU.mult, op1=ALU.mult),
             reads=[hb, ssb, gbuf], writes=[ub])
    return us


def norm_post(S, us, ident, ibuf, uT, uTbuf, res):
    ptp = res["ptp"]
    for i, (ut, ub) in enumerate(us):
        res["m"] = res.get("m", 0) + 1
        pt, pb = ptp[res["m"] % len(ptp)]
        for c in range(8):
            S.op("pe", lambda e, c=c: e.transpose(out=pt[:, c * 128:(c + 1) * 128], in_=ut[:, c * 128:(c + 1) * 128],
                                                  identity=ident[:]), reads=[ub, ibuf], writes=[pb])
        S.op("act", lambda e: e.copy(out=uT[:, 0:4, i * 128:(i + 1) * 128],
                                     in_=pt[:, 0:512].rearrange("p (c t) -> p c t", c=4)),
             reads=[pb], writes=[uTbuf])
        S.op("dve", lambda e: e.tensor_copy(out=uT[:, 4:8, i * 128:(i + 1) * 128],
                                            in_=pt[:, 512:1024].rearrange("p (c t) -> p c t", c=4)),
             reads=[pb], writes=[uTbuf])


def norm_block(S, C, h_src, r0, ntile, gain, gbuf, ident, ibuf, uT, uTbuf, res, lq="sp"):
    us = norm_pre(S, C, h_src, r0, ntile, gain, gbuf, res, lq=lq)
    norm_post(S, us, ident, ibuf, uT, uTbuf, res)


def phase_ffn(S, W, pre, l, h_in, h_out, ntok, ident, ibuf):
    nc = S.nc
    NB = ntok // 512
    with ExitStack() as st:
        C = Ctx(S, st)
        wg, wgb = C.sb([128, 8, DFF], BF16, "wg")
        wu, wub = C.sb([128, 8, DFF], BF16, "wu")
        wd, wdb = C.sb([128, NFF, D], BF16, "wd")
        gain, gb = C.sb([128, D], F32, "gain")
        uT, uTb = C.sb([128, 8, 512], BF16, "uT")
        aT, aTb = C.sb([128, NFF, 512], BF16, "aT")
        res = norm_res(C)
        hr = [C.sb([128, D], F32, "hr") for _ in range(2)]
        sg = [C.sb([128, 512], F32, "sg") for _ in range(2)]
        psg = [C.ps([128, 512], F32, "psg") for _ in range(2)]
        psu = [C.ps([128, 512], F32, "psu") for _ in range(2)]
        psd = [C.ps([128, 512], F32, "psd") for _ in range(2)]

        S.dma("sp", gain[:], bcast_row(W[pre + "_norm"][l], 128), writes=[gb])
        fgroups = [(0, 6), (6, 12), (12, 17), (17, 22)]
        wgbs = [Buf() for _ in fgroups]
        wubs = [Buf() for _ in fgroups]
        fgrp = {}
        for gi, (f0, f1) in enumerate(fgroups):
            for f in range(f0, f1):
                fgrp[f] = gi
            for src_, dst_, bufs_ in ((W[pre + "_w_gate"][l], wg, wgbs), (W[pre + "_w_up"][l], wu, wubs)):
                for c in range(8):
                    S.dma("pool", dst_[:, c, f0 * 128:f1 * 128], src_[c * 128:(c + 1) * 128, f0 * 128:f1 * 128],
                          writes=[bufs_[gi]], max_dma_last_dim=4096)
        load_w_bf16(S, wd, wdb, W[pre + "_w_down"][l], NFF, D)

        def npre(b):
            return norm_pre(S, C, h_in, b * 512, 4, gain, gb, res)

        def npost(us):
            norm_post(S, us, ident, ibuf, uT, uTb, res)

        def gateup(b):
            for f in range(NFF):
                pg, pgb = psg[f % 2]
                pu, pub = psu[f % 2]
                sgt, sgb = sg[f % 2]
                for k in range(8):
                    S.op("pe", lambda e, k=k: e.matmul(pg[:], lhsT=wg[:, k, f * 128:(f + 1) * 128], rhs=uT[:, k, :],
                                                       start=(k == 0), stop=(k == 7)),
                         reads=[wgbs[fgrp[f]], uTb], writes=[pgb])
                for k in range(8):
                    S.op("pe", lambda e, k=k: e.matmul(pu[:], lhsT=wu[:, k, f * 128:(f + 1) * 128], rhs=uT[:, k, :],
                                                       start=(k == 0), stop=(k == 7)),
                         reads=[wubs[fgrp[f]], uTb], writes=[pub])
                S.op("act", lambda e: e.activation(out=sgt[:], in_=pg[:], func=AF.Silu), reads=[pgb], writes=[sgb])
                S.op("dve", lambda e: e.tensor_tensor(out=aT[:, f, :], in0=sgt[:], in1=pu[:], op=ALU.mult),
                     reads=[sgb, pub], writes=[aTb])

        def down(b):
            for i in range(4):
                ht, hb = hr[i % 2]
                rows = slice(b * 512 + i * 128, b * 512 + (i + 1) * 128)
                S.dma("sp", ht[:], h_in[rows, :], writes=[hb])
                for n in range(2):
                    pd, pdb = psd[n]
                    for f in range(NFF):
                        S.op("pe", lambda e, f=f: e.matmul(pd[:], lhsT=aT[:, f, i * 128:(i + 1) * 128],
                                                           rhs=wd[:, f, n * 512:(n + 1) * 512],
                                                           start=(f == 0), stop=(f == NFF - 1)),
                             reads=[aTb, wdb], writes=[pdb])
                    S.op("dve", lambda e: e.scalar_tensor_tensor(out=ht[:, n * 512:(n + 1) * 512], in0=pd[:],
                                                                 scalar=0.5, in1=ht[:, n * 512:(n + 1) * 512],
                                                                 op0=ALU.mult, op1=ALU.add),
                         reads=[pdb, hb], writes=[hb])
                S.dma("pool", h_out[rows, :], ht[:], reads=[hb])

        npost(npre(0))
        for b in range(NB):
            us_next = npre(b + 1) if b + 1 < NB else None
            gateup(b)
            if us_next is not None:
                npost(us_next)
            down(b)
        S.barrier()


NEG = -30000.0
NG = 4608
DIL = ((128, 1), (512, 4), (2048, 16))


def t5_bucket_np(dist):
    dist = np.asarray(dist, dtype=np.int64)
    d = np.maximum(dist, 1).astype(np.float32)
    large = 16 + (np.log(d / np.float32(16)) / np.float32(math.log(2048 / 16)) * np.float32(16)).astype(np.int32)
    large = np.minimum(large, 31)
    return np.where(dist < 16, dist, large)


def host_consts():
    c = {}
    s = np.arange(128)
    c["c_tri"] = (s[:, None] <= s[None, :]).astype(np.float32)
    sel = np.zeros((128, 128), np.float32)
    sel[127, :] = 1
    c["c_sel"] = sel
    c["c_maskT"] = c["c_tri"] * np.float32(0.125)
    c["c_jrev"] = np.ascontiguousarray(np.eye(128, dtype=np.float32)[::-1])
    oh = np.zeros((33, NG), np.float32)
    for p, (w, d) in enumerate(DIL):
        jx = np.arange(384)
        dl = jx - 127
        valid = (dl >= 0) & (dl <= 128)
        bk = t5_bucket_np(np.clip(dl, 0, 128) * d)
        cols = p * 384 + jx
        oh[bk[valid], cols[valid]] = 1
        oh[32, cols[~valid]] = NEG
    jx = np.arange(NG - 1152)
    dl = jx - 511
    valid = dl >= 0
    bk = t5_bucket_np(np.clip(dl, 0, None))
    cols = 1152 + jx
    oh[bk[valid], cols[valid]] = 1
    oh[32, cols[~valid]] = NEG
    c["c_oh"] = oh
    wins = [2, 4, 8, 16]
    invc = np.zeros((128, 2, 2, 512), np.float32)
    t = np.arange(512)
    for cc in range(2):
        for half in range(2):
            w = wins[cc * 2 + half]
            rows = slice(half * 64, half * 64 + 64)
            invc[rows, 1, cc, :] = 1.0 / w
            invc[rows, 0, cc, :] = 1.0 / np.minimum(t + 1, w)
    c["c_invc"] = invc
    return c


CONST_SHAPES = {"c_tri": (128, 128), "c_sel": (128, 128), "c_maskT": (128, 128), "c_jrev": (128, 128),
                "c_oh": (33, NG), "c_invc": (128, 2, 2, 512)}


class Rot:
    def __init__(self, items):
        self.items = items
        self.i = 0

    def next(self):
        x = self.items[self.i % len(self.items)]
        self.i += 1
        return x


def col1(ap1d):
    return ap1d.rearrange("(p o) -> p o", o=1)


def load_tok(S, dst, dbuf, src, nch):
    for n0 in range(0, nch, 8):
        n1 = min(nch, n0 + 8)
        S.dma("sp", dst[:, n0:n1, :], src[n0 * 128:n1 * 128, :].rearrange("(n p) c -> p n c", p=128), writes=[dbuf])


def setup_consts(S, C, W, CS, SC):
    nc = S.nc
    idf, idfb = C.sb([128, 128], F32, "identf")
    ident, ibuf = C.sb([128, 128], BF16, "ident")
    jrev, jb = C.sb([128, 128], BF16, "jrev")
    S.op("pool", lambda e: e.memset(idf[:], 1.0), writes=[idfb])
    S.op("pool", lambda e: e.affine_select(out=idf[:], in_=idf[:], pattern=[[-1, 128]], compare_op=ALU.is_equal,
                                           fill=0.0, base=0, channel_multiplier=1), reads=[idfb], writes=[idfb])
    S.op("dve", lambda e: e.tensor_copy(out=ident[:], in_=idf[:]), reads=[idfb], writes=[ibuf])
    S.dma("pool", jrev[:], CS["c_jrev"], writes=[jb])
    with ExitStack() as st:
        C2 = Ctx(S, st)
        t5x, t5b = C2.sb([33, 8], F32, "t5x")
        oh, ohb = C2.sb([33, NG], F32, "oh")
        gsb, gsbb = C2.sb([8, NG], BF16, "gsb")
        pg = [C2.ps([128, 512], F32, "pgv") for _ in range(2)]
        S.op("dve", lambda e: e.memset(t5x[:], 1.0), writes=[t5b])
        S.dma("sp", t5x[0:32, :], W["t5_bias"], writes=[t5b])
        S.dma("sp", oh[:], CS["c_oh"], writes=[ohb])
        for n in range(NG // 512):
            p, pb = pg[n % 2]
            S.op("pe", lambda e: e.matmul(p[0:8, :], lhsT=t5x[:], rhs=oh[:, n * 512:(n + 1) * 512], start=True, stop=True),
                 reads=[t5b, ohb], writes=[pb])
            S.op("act", lambda e: e.copy(out=gsb[:, n * 512:(n + 1) * 512], in_=p[0:8, :]), reads=[pb], writes=[gsbb])
        S.dma("sp", SC["gvec"], gsb[:], reads=[gsbb])
        S.barrier()
    return ident, ibuf, jrev, jb


def norm_res(C):
    return {
        "hn": [C.sb([128, D], F32, "hn") for _ in range(4)],
        "u": [C.sb([128, D], BF16, "u") for _ in range(4)],
        "ss": [C.sb([128, 16], F32, "ss") for _ in range(2)],
        "ptp": [C.ps([128, 1024], BF16, "ptp") for _ in range(2)],
    }


def phase_mixproj(S, W, CS, SC, l, NSEQ, T, h, ident, ibuf):
    NB = T // 512
    with ExitStack() as st:
        C = Ctx(S, st)
        win, winb = C.sb([128, 8, INW], BF16, "win")
        gain, gb = C.sb([128, D], F32, "gain")
        uTs = [C.sb([128, 8, 512], BF16, "uT") for _ in range(2)]
        res = norm_res(C)
        cw, cwb = C.sb([128, 4, 4], F32, "cw")
        cbias, cbb = C.sb([128, 4], F32, "cb")
        gateb, gtb = C.sb([128, 8], F32, "gateb")
        pscale, pscb = C.sb([128, 256], F32, "pscale")
        wblkf, wfb = C.sb([128, 2, 128], F32, "wblkf")
        wblk, wkb = C.sb([128, 2, 128], BF16, "wblk")
        invc, invb = C.sb([128, 2, 2, 512], F32, "invc")
        tri, trib = C.sb([128, 128], BF16, "tri")
        sel, selb = C.sb([128, 128], BF16, "sel")
        gbr = Rot([C.sb([128, 4, 16], BF16, "gbb") for _ in range(2)])
        Xm = [C.sb([128, 515], F32, "Xm") for _ in range(4)]
        Xp = [C.sb([128, 528], F32, "Xp") for _ in range(2)]
        acc = Rot([C.sb([128, 512], F32, "acc") for _ in range(2)])
        stg = Rot([C.sb([128, 512], BF16, "stg") for _ in range(4)])
        ksg = [C.sb([128, 512], BF16, "ksg") for _ in range(2)]
        ssum = [C.sb([128, 528], F32, "ssum") for _ in range(4)]
        ptmp = Rot([C.sb([128, 512], F32, "ptmp") for _ in range(2)])
        dmT = [C.sb([128, 512], BF16, "dmT") for _ in range(2)]
        mvst = Rot([C.sb([128, 4, 65], BF16, "mvst") for _ in range(2)])
        dvst = Rot([C.sb([128, 4, 65], BF16, "dvst") for _ in range(2)])
        fvst = Rot([C.sb([128, 4, 65], BF16, "fvst") for _ in range(2)])
        ogst = Rot([C.sb([128, 256], BF16, "ogst") for _ in range(2)])
        kgst = Rot([C.sb([128, 256], BF16, "kgst") for _ in range(2)])
        ybst = Rot([C.sb([128, 256], BF16, "ybst") for _ in range(2)])
        agst = Rot([C.sb([128, 4, 8], F32, "agst") for _ in range(2)])
        ebst = Rot([C.sb([128, 4, 4], F32, "ebst") for _ in range(2)])
        gsr = Rot([C.sb([128, 4, 32], F32, "gs") for _ in range(2)])
        pgen = Rot([C.ps([128, 512], F32, "pgen") for _ in range(4)])
        pgp = C.ps([128, 512], F32, "pgp")
        ptk = C.ps([128, 1024], BF16, "ptk")

        load_w_bf16(S, win, winb, W["w_in"][l], 8, INW)
        S.dma("sp", gain[:], bcast_row(W["mix_norm"][l], 128), writes=[gb])
        S.dma("sp", gateb[:], bcast_row(W["mlstm_gate_b"][l], 128), writes=[gtb])
        S.dma("sp", pscale[:], bcast_row(W["pool_scale"][l], 128), writes=[pscb])
        for c in range(4):
            for j in range(4):
                S.dma("sp", cw[:, c, j:j + 1], col1(W["mlstm_conv_w"][l, j, c * 128:(c + 1) * 128]), writes=[cwb])
            S.dma("sp", cbias[:, c:c + 1], col1(W["mlstm_conv_b"][l, c * 128:(c + 1) * 128]), writes=[cbb])
        S.op("dve", lambda e: e.memset(wblkf[:], 0.0), writes=[wfb])
        for g in range(4):
            r0 = (g % 2) * 64
            S.dma("sp", wblkf[r0:r0 + 64, g // 2, r0:r0 + 64], W["pool_w"][l, g], writes=[wfb])
        S.op("dve", lambda e: e.tensor_copy(out=wblk[:], in_=wblkf[:]), reads=[wfb], writes=[wkb])
        S.dma("sp", invc[:], CS["c_invc"], writes=[invb])
        S.dma("pool", tri[:], CS["c_tri"], writes=[trib])
        S.dma("pool", sel[:], CS["c_sel"], writes=[selb])
        for r in (mvst, dvst, fvst):
            for t_, b_ in r.items:
                S.op("dve", lambda e: e.memset(t_[:], 1.0), writes=[b_])

        if MIXSTAGE <= 0:
            S.barrier()
            return
        fm_specs = [("m", 0, 0), ("m", 1, 128), ("m", 2, 256), ("m", 3, 384), ("p", 0, 1032), ("p", 1, 1160),
                    ("d", 4, 1288), ("d", 5, 1416), ("d", 6, 1544), ("d", 7, 1672),
                    ("d", 8, 2056), ("d", 9, 2184), ("d", 10, 2312), ("d", 11, 2440)]
        dscale = {4: 0.125, 5: 0.125, 6: 1.0, 7: 1.0, 8: 32 ** -0.5, 9: 32 ** -0.5, 10: 1.0, 11: 1.0}

        blocks = [(sq, bb) for sq in range(NSEQ) for bb in range(NB)]
        norm_block(S, C, h, 0, 4, gain, gb, ident, ibuf, uTs[0][0], uTs[0][1], res, lq="pool")
        for bi, (seq, b) in enumerate(blocks):
            if True:
                uT, uTb = uTs[bi % 2]
                us_next = None
                if bi + 1 < len(blocks):
                    nsq, nbb = blocks[bi + 1]
                    us_next = norm_pre(S, C, h, nsq * T + nbb * 512, 4, gain, gb, res, lq="pool")
                tok0 = seq * T + b * 512
                tsl = slice(b * 512, (b + 1) * 512)
                if b == 0:
                    for c in range(4):
                        S.op("dve", lambda e: e.memset(Xm[c][0][:, 0:3], 0.0), writes=[Xm[c][1]])
                    for c in range(2):
                        S.op("dve", lambda e: e.memset(Xp[c][0][:, 0:16], 0.0), writes=[Xp[c][1]])
                gs, _ = gsr.next()
                gbt, _ = gbr.next()
                pg, _ = pgp
                gsbs = [Buf() for _ in range(4)]
                gbbs = [Buf() for _ in range(4)]
                pgbs = [pgp[1]] * 4
                agts = [agst.next()]

                gsb_, gbtb, pgb = gsbs[0], gbbs[0], pgbs[0]
                ag_t, ag_b = agts[0]
                PV4 = pg[:, 0:64].rearrange("p (i c) -> p i c", c=16)

                def gateA():
                    for i in range(4):
                        tl = slice(i * 128, (i + 1) * 128)
                        for k in range(8):
                            S.op("pe", lambda e: e.matmul(pg[:, i * 16:i * 16 + 8], lhsT=uT[:, k, tl], rhs=win[:, k, 1024:1032],
                                                          start=(k == 0), stop=(k == 7)), reads=[uTb, winb], writes=[pgb])
                    S.op("dve", lambda e: e.tensor_tensor(out=gs[:, :, 0:8], in0=PV4[:, :, 0:8],
                                                          in1=gateb[:].unsqueeze(1).broadcast_to([128, 4, 8]), op=ALU.add),
                         reads=[pgb, gtb], writes=[gsb_])
                    S.op("act", lambda e: e.activation(out=gs[:, :, 8:12], in_=gs[:, :, 4:8], func=AF.Sigmoid),
                         reads=[gsb_], writes=[gsb_])
                    S.op("act", lambda e: e.activation(out=gs[:, :, 12:16], in_=gs[:, :, 8:12], func=AF.Ln),
                         reads=[gsb_], writes=[gsb_])
                    S.op("dve", lambda e: e.tensor_copy(out=gbt[:, :, 0:4], in_=gs[:, :, 12:16]), reads=[gsb_], writes=[gbtb])
                    S.op("dve", lambda e: e.tensor_copy(out=gs[:, :, 28:32], in_=gbt[:, :, 0:4]), reads=[gbtb], writes=[gsb_])
                    S.op("dve", lambda e: e.tensor_tensor(out=gbt[:, :, 4:8], in0=gs[:, :, 12:16], in1=gs[:, :, 28:32],
                                                          op=ALU.subtract), reads=[gsb_, gbtb], writes=[gbtb])

                def gateB():
                    for i in range(4):
                        S.op("pe", lambda e: e.matmul(pg[:, i * 16 + 8:i * 16 + 12], lhsT=tri[:], rhs=gbt[:, i, 0:4],
                                                      start=True, stop=False), reads=[trib, gbtb], writes=[pgb])
                        S.op("pe", lambda e: e.matmul(pg[:, i * 16 + 8:i * 16 + 12], lhsT=tri[:], rhs=gbt[:, i, 4:8],
                                                      start=False, stop=True), reads=[trib, gbtb], writes=[pgb])
                    S.op("act", lambda e: e.copy(out=gs[:, :, 16:20], in_=PV4[:, :, 8:12]), reads=[pgb], writes=[gsb_])
                    S.op("act", lambda e: e.activation(out=ag_t[:, :, 0:4], in_=gs[:, :, 16:20], func=AF.Exp),
                         reads=[gsb_], writes=[ag_b])
                    S.op("dve", lambda e: e.tensor_tensor(out=gs[:, :, 20:24], in0=gs[:, :, 0:4], in1=gs[:, :, 16:20],
                                                          op=ALU.subtract), reads=[gsb_], writes=[gsb_])
                    S.op("act", lambda e: e.activation(out=ag_t[:, :, 4:8], in_=gs[:, :, 20:24], func=AF.Exp),
                         reads=[gsb_], writes=[ag_b])
                    S.op("dve", lambda e: e.tensor_scalar(out=gs[:, :, 24:28], in0=ag_t[:, :, 4:8], scalar1=0.125, scalar2=None,
                                                          op0=ALU.mult), reads=[ag_b], writes=[gsb_])
                    S.op("dve", lambda e: e.tensor_copy(out=gbt[:, :, 8:12], in_=gs[:, :, 16:20]), reads=[gsb_], writes=[gbtb])
                    S.op("dve", lambda e: e.tensor_copy(out=gs[:, :, 28:32], in_=gbt[:, :, 8:12]), reads=[gbtb], writes=[gsb_])
                    S.op("dve", lambda e: e.tensor_tensor(out=gbt[:, :, 12:16], in0=gs[:, :, 16:20], in1=gs[:, :, 28:32],
                                                          op=ALU.subtract), reads=[gsb_, gbtb], writes=[gbtb])

                def gateC():
                    for i in range(4):
                        S.op("pe", lambda e: e.matmul(pg[:, i * 16 + 12:i * 16 + 16], lhsT=sel[:], rhs=gbt[:, i, 8:12],
                                                      start=True, stop=False), reads=[selb, gbtb], writes=[pgb])
                        S.op("pe", lambda e: e.matmul(pg[:, i * 16 + 12:i * 16 + 16], lhsT=sel[:], rhs=gbt[:, i, 12:16],
                                                      start=False, stop=True), reads=[selb, gbtb], writes=[pgb])
                    eb_t, eb_b = ebst.next()
                    S.op("act", lambda e: e.activation(out=eb_t[:], in_=PV4[:, :, 12:16], func=AF.Exp),
                         reads=[pgb], writes=[eb_b])
                    S.dma("sp", SC["ag"][seq, :, b * 4:(b + 1) * 4, :], ag_t[:], reads=[ag_b])
                    S.dma("sp", SC["eb"][seq, :, b * 4:(b + 1) * 4, :], eb_t[:], reads=[eb_b])

                for fi, (kind, ci, col) in enumerate(fm_specs):
                    if fi == 0:
                        gateA()
                    elif fi == 4:
                        gateB()
                    elif fi == 8:
                        gateC()
                    p, pb = pgen.next()
                    for k in range(8):
                        S.op("pe", lambda e: e.matmul(p[:], lhsT=win[:, k, col:col + 128], rhs=uT[:, k, :],
                                                      start=(k == 0), stop=(k == 7)), reads=[winb, uTb], writes=[pb])
                    if kind == "m":
                        X, Xb = Xm[ci]
                        S.op("act", lambda e: e.copy(out=X[:, 3:515], in_=p[:]), reads=[pb], writes=[Xb])
                        a_, ab_ = acc.next()
                        S.op("dve", lambda e: e.tensor_scalar(out=a_[:], in0=X[:, 3:515], scalar1=cw[:, ci, 3:4],
                                                              scalar2=cbias[:, ci:ci + 1], op0=ALU.mult, op1=ALU.add),
                             reads=[Xb, cwb, cbb], writes=[ab_])
                        for j in range(3):
                            S.op("dve", lambda e: e.scalar_tensor_tensor(out=a_[:], in0=X[:, j:j + 512],
                                                                         scalar=cw[:, ci, j:j + 1], in1=a_[:],
                                                                         op0=ALU.mult, op1=ALU.add),
                                 reads=[Xb, cwb, ab_], writes=[ab_])
                        S.op("dve", lambda e: e.tensor_copy(out=X[:, 0:3], in_=X[:, 512:515]), reads=[Xb], writes=[Xb])
                        if ci < 2:
                            s_, sb_ = stg.next()
                        else:
                            s_, sb_ = ksg[ci - 2]
                        S.op("act", lambda e: e.activation(out=s_[:], in_=a_[:], func=AF.Silu), reads=[ab_], writes=[sb_])
                        S.dma("sp", SC["qkT"][seq, ci * 128:(ci + 1) * 128, tsl], s_[:], reads=[sb_])
                        if ci >= 2:
                            pk, pkb = ptk
                            for i in range(4):
                                o0 = i * 256 + (ci - 2) * 128
                                S.op("pe", lambda e: e.transpose(out=pk[:, o0:o0 + 128], in_=s_[:, i * 128:(i + 1) * 128],
                                                                 identity=ident[:]), reads=[sb_, ibuf], writes=[pkb])
                    elif kind == "p":
                        X, Xb = Xp[ci]
                        S.op("act", lambda e: e.copy(out=X[:, 16:528], in_=p[:]), reads=[pb], writes=[Xb])
                        prev, prevb = X, Xb
                        sh = 1
                        nlev = 2 if ci == 0 else 4
                        levels = []
                        for lev in range(nlev):
                            s_, sb_ = ssum[lev]
                            lo = 2 * sh - 1
                            S.op("dve", lambda e: e.tensor_tensor(out=s_[:, lo:528], in0=prev[:, lo:528],
                                                                  in1=prev[:, lo - sh:528 - sh], op=ALU.add),
                                 reads=[prevb], writes=[sb_])
                            levels.append((s_, sb_))
                            prev, prevb = s_, sb_
                            sh *= 2
                        d_, db_ = dmT[ci]
                        for half in range(2):
                            s_, sb_ = levels[(0 if ci == 0 else 2) + half]
                            rs = slice(half * 64, half * 64 + 64)
                            if b == 0:
                                t_, tb_ = ptmp.next()
                                S.op("dve", lambda e: e.tensor_tensor(out=t_[rs, :], in0=s_[rs, 16:528],
                                                                      in1=invc[rs, 0, ci, :], op=ALU.mult),
                                     reads=[sb_, invb], writes=[tb_])
                                S.op("dve", lambda e: e.tensor_tensor(out=d_[rs, :], in0=t_[rs, :], in1=X[rs, 16:528],
                                                                      op=ALU.subtract), reads=[tb_, Xb], writes=[db_])
                            else:
                                S.op("dve", lambda e: e.scalar_tensor_tensor(out=d_[rs, :], in0=s_[rs, 16:528],
                                                                             scalar=invc[rs, 1, ci, 0:1], in1=X[rs, 16:528],
                                                                             op0=ALU.mult, op1=ALU.subtract),
                                     reads=[sb_, invb, Xb], writes=[db_])
                        S.op("dve", lambda e: e.tensor_copy(out=X[:, 0:16], in_=X[:, 512:528]), reads=[Xb], writes=[Xb])
                    else:
                        s_, sb_ = stg.next()
                        S.op("act", lambda e: e.activation(out=s_[:], in_=p[:], func=AF.Copy, scale=float(dscale[ci])),
                             reads=[pb], writes=[sb_])
                        S.dma("sp", SC["qkT"][seq, ci * 128:(ci + 1) * 128, tsl], s_[:], reads=[sb_])
                for i in range(4):
                    if i == 2 and us_next is not None:
                        norm_post(S, us_next, ident, ibuf, uTs[(bi + 1) % 2][0], uTs[(bi + 1) % 2][1], res)
                    tl = slice(i * 128, (i + 1) * 128)
                    rows = slice(b * 512 + i * 128, b * 512 + (i + 1) * 128)
                    G = gs[:, i, :]
                    k_, kb_ = kgst.next()
                    pk, pkb = ptk
                    S.op("dve", lambda e: e.tensor_tensor(
                        out=k_[:].rearrange("p (h d) -> p h d", h=4),
                        in0=pk[:, i * 256:(i + 1) * 256].rearrange("p (h d) -> p h d", h=4),
                        in1=G[:, 24:28].unsqueeze(2).broadcast_to([128, 4, 64]), op=ALU.mult),
                        reads=[pkb, gsbs[0]], writes=[kb_])
                    S.dma("sp", SC["kg"][seq, rows, :], k_[:], reads=[kb_])
                    pp, ppb = pgen.next()
                    for cc in range(2):
                        S.op("pe", lambda e: e.matmul(pp[:, cc * 128:(cc + 1) * 128], lhsT=dmT[cc][0][:, tl],
                                                      rhs=wblk[:, cc, :], start=True, stop=True),
                             reads=[dmT[cc][1], wkb], writes=[ppb])
                    y_, yb_ = ybst.next()
                    S.op("dve", lambda e: e.tensor_tensor(out=y_[:], in0=pp[:, 0:256], in1=pscale[:], op=ALU.mult),
                         reads=[ppb, pscb], writes=[yb_])
                    S.dma("sp", SC["y"][seq, rows, 256:512], y_[:], reads=[yb_])
                    p1, p1b = pgen.next()
                    for k in range(8):
                        S.op("pe", lambda e: e.matmul(p1[:], lhsT=uT[:, k, tl], rhs=win[:, k, 512:1024],
                                                      start=(k == 0), stop=(k == 7)), reads=[uTb, winb], writes=[p1b])
                    v_, vb_ = mvst.next()
                    S.op("act", lambda e: e.copy(out=v_[:, :, 0:64], in_=p1[:, 0:256].rearrange("p (h d) -> p h d", h=4)),
                         reads=[p1b], writes=[vb_])
                    o_, ob_ = ogst.next()
                    S.op("act", lambda e: e.activation(out=o_[:], in_=p1[:, 256:512], func=AF.Sigmoid),
                         reads=[p1b], writes=[ob_])
                    S.dma("sp", SC["mv1"][seq, rows, :], v_[:].rearrange("p h d -> p (h d)"), reads=[vb_])
                    S.dma("sp", SC["og"][seq, rows, :], o_[:], reads=[ob_])
                    p2, p2b = pgen.next()
                    for gi, col in ((0, 1800), (1, 2568)):
                        for k in range(8):
                            S.op("pe", lambda e: e.matmul(p2[:, gi * 256:(gi + 1) * 256], lhsT=uT[:, k, tl],
                                                          rhs=win[:, k, col:col + 256], start=(k == 0), stop=(k == 7)),
                                 reads=[uTb, winb], writes=[p2b])
                    dv_, dvb_ = dvst.next()
                    fv_, fvb_ = fvst.next()
                    S.op("act", lambda e: e.copy(out=dv_[:, :, 0:64], in_=p2[:, 0:256].rearrange("p (h d) -> p h d", h=4)),
                         reads=[p2b], writes=[dvb_])
                    S.op("act", lambda e: e.copy(out=fv_[:, :, 0:64], in_=p2[:, 256:512].rearrange("p (h d) -> p h d", h=4)),
                         reads=[p2b], writes=[fvb_])
                    S.dma("sp", SC["dv1"][seq, rows, :], dv_[:].rearrange("p h d -> p (h d)"), reads=[dvb_])
                    S.dma("sp", SC["fv1"][seq, rows, :], fv_[:].rearrange("p h d -> p (h d)"), reads=[fvb_])
        S.barrier()


def phase_mlstm(S, W, CS, SC, l, seq, T):
    NCH = T // 128
    with ExitStack() as st:
        C = Ctx(S, st)
        qk, qkb = C.sb([128, 4, T], BF16, "mqk")
        v1, v1b = C.sb([128, NCH, 260], BF16, "mv1")
        kg, kgb = C.sb([128, NCH, 256], BF16, "mkg")
        og, ogb = C.sb([128, NCH, 256], BF16, "mog")
        ag, agb = C.sb([128, NCH, 8], F32, "mag")
        eb, ebb = C.sb([128, NCH, 4], F32, "meb")
        ebp, ebpb = C.sb([128, NCH, 2], F32, "mebp")
        maskT, mkb = C.sb([128, 128], F32, "maskT")
        ng, ngb = C.sb([128, 256], F32, "ng")
        Cf, Cfb = C.sb([128, 2, 130], F32, "Cf")
        Cb, Cbb = C.sb([128, 2, 130], BF16, "Cb")
        tmpU = Rot([C.sb([128, 2, 130], F32, "tmpU") for _ in range(2)])
        PT = Rot([C.sb([128, 128], BF16, "PT") for _ in range(4)])
        nd = Rot([C.sb([128, 4, 65], F32, "nd") for _ in range(2)])
        hh = Rot([C.sb([128, 4, 64], F32, "hh") for _ in range(2)])
        sq = Rot([C.sb([128, 4, 64], F32, "sq") for _ in range(2)])
        stt = Rot([C.sb([128, 32], F32, "stt") for _ in range(2)])
        yst = Rot([C.sb([128, 256], BF16, "yst") for _ in range(2)])
        psc = Rot([C.ps([128, 512], F32, "psc") for _ in range(2)])
        pU = Rot([C.ps([128, 512], F32, "pU") for _ in range(2)])
        po = Rot([C.ps([128, 512], F32, "po") for _ in range(2)])

        for c in range(4):
            S.dma("sp", qk[:, c, :], SC["qkT"][seq, c * 128:(c + 1) * 128, :], writes=[qkb])
        load_tok(S, v1, v1b, SC["mv1"][seq], NCH)
        load_tok(S, kg, kgb, SC["kg"][seq], NCH)
        load_tok(S, og, ogb, SC["og"][seq], NCH)
        S.dma("sp", ag[:], SC["ag"][seq], writes=[agb])
        S.dma("sp", eb[:], SC["eb"][seq], writes=[ebb])
        S.dma("sp", maskT[:], CS["c_maskT"], writes=[mkb])
        S.dma("sp", ng[:], bcast_row(W["mlstm_norm"][l], 128), writes=[ngb])
        for pr in range(2):
            S.op("dve", lambda e: e.tensor_copy(out=ebp[0:64, :, pr], in_=eb[0:64, :, 2 * pr]), reads=[ebb], writes=[ebpb])
            S.op("dve", lambda e: e.tensor_copy(out=ebp[64:128, :, pr], in_=eb[64:128, :, 2 * pr + 1]),
                 reads=[ebb], writes=[ebpb])
        S.op("dve", lambda e: e.memset(Cf[:], 0.0), writes=[Cfb])
        S.op("dve", lambda e: e.memset(Cb[:], 0.0), writes=[Cbb])

        for c in range(NCH):
            if MLSTAGE <= 0:
                break
            cols = slice(c * 128, (c + 1) * 128)
            psA, psB = psc.items
            pts = []
            for hd in range(4):
                pr, hh_ = hd // 2, hd % 2
                rs = slice(hh_ * 64, hh_ * 64 + 64)
                ps_, psb_ = (psA, psB)[hh_]
                S.op("pe", lambda e: e.matmul(ps_[:, pr * 128:(pr + 1) * 128], lhsT=qk[rs, 2 + pr, cols], rhs=qk[rs, pr, cols],
                                              start=True, stop=True), reads=[qkb], writes=[psb_])
            if MLSTAGE <= 1:
                continue
            pu_, pub_ = pU.next()
            for pr in range(2):
                S.op("pe", lambda e: e.matmul(pu_[:, pr * 130:(pr + 1) * 130], lhsT=kg[:, c, pr * 128:(pr + 1) * 128],
                                              rhs=v1[:, c, pr * 130:(pr + 1) * 130], start=True, stop=True),
                     reads=[kgb, v1b], writes=[pub_])
            for hd in range(4):
                p_, pb_ = PT.next()
                ps_, psb_ = (psA, psB)[hd % 2]
                S.op("dve", lambda e: e.scalar_tensor_tensor(out=p_[:], in0=ps_[:, (hd // 2) * 128:(hd // 2 + 1) * 128],
                                                             scalar=ag[:, c, 4 + hd:5 + hd], in1=maskT[:],
                                                             op0=ALU.mult, op1=ALU.mult),
                     reads=[psb_, agb, mkb], writes=[pb_])
                pts.append((p_, pb_))
            if MLSTAGE <= 2:
                continue
            po_, pob_ = po.next()
            for pr in range(2):
                S.op("pe", lambda e: e.matmul(po_[:, pr * 130:(pr + 1) * 130], lhsT=qk[:, pr, cols], rhs=Cb[:, pr, :],
                                              start=True, stop=False), reads=[qkb, Cbb], writes=[pob_])
                for hh_ in range(2):
                    hd = 2 * pr + hh_
                    p_, pb_ = pts[hd]
                    S.op("pe", lambda e: e.matmul(po_[:, hd * 65:(hd + 1) * 65], lhsT=p_[:], rhs=v1[:, c, hd * 65:(hd + 1) * 65],
                                                  start=False, stop=True), reads=[pb_, v1b], writes=[pob_])
            if MLSTAGE <= 3:
                continue
            tu, tub = tmpU.next()
            for pr in range(2):
                S.op("act", lambda e: e.activation(out=tu[:, pr, :], in_=pu_[:, pr * 130:(pr + 1) * 130], func=AF.Identity,
                                                   scale=ebp[:, c, pr:pr + 1]), reads=[pub_, ebpb], writes=[tub])
                S.op("dve", lambda e: e.scalar_tensor_tensor(out=Cf[:, pr, :], in0=Cf[:, pr, :], scalar=ebp[:, c, pr:pr + 1],
                                                             in1=tu[:, pr, :], op0=ALU.mult, op1=ALU.add),
                     reads=[Cfb, ebpb, tub], writes=[Cfb])
                S.op("act", lambda e: e.copy(out=Cb[0:64, pr, 0:65], in_=Cf[0:64, pr, 0:65]), reads=[Cfb], writes=[Cbb])
                S.op("act", lambda e: e.copy(out=Cb[64:128, pr, 65:130], in_=Cf[64:128, pr, 65:130]), reads=[Cfb], writes=[Cbb])
            if MLSTAGE <= 4:
                continue
            n_, nb_ = nd.next()
            S.op("dve", lambda e: e.tensor_tensor(out=n_[:], in0=po_[:, 0:260].rearrange("p (h d) -> p h d", h=4),
                                                  in1=ag[:, c, 0:4].unsqueeze(2).broadcast_to([128, 4, 65]), op=ALU.mult),
                 reads=[pob_, agb], writes=[nb_])
            s_, sb_ = stt.next()
            S.op("act", lambda e: e.activation(out=s_[:, 0:4].unsqueeze(2), in_=n_[:, :, 64:65], func=AF.Abs),
                 reads=[nb_], writes=[sb_])
            S.op("dve", lambda e: e.tensor_scalar(out=s_[:, 0:4], in0=s_[:, 0:4], scalar1=1.0, scalar2=None,
                                                  op0=ALU.max), reads=[sb_], writes=[sb_])
            S.op("dve", lambda e: e.reciprocal(out=s_[:, 4:8], in_=s_[:, 0:4]), reads=[sb_], writes=[sb_])
            h_, hb_ = hh.next()
            S.op("dve", lambda e: e.tensor_tensor(out=h_[:], in0=n_[:, :, 0:64],
                                                  in1=s_[:, 4:8].unsqueeze(2).broadcast_to([128, 4, 64]), op=ALU.mult),
                 reads=[nb_, sb_], writes=[hb_])
            q_, qb_ = sq.next()
            S.op("act", lambda e: e.activation(out=q_[:], in_=h_[:], func=AF.Square), reads=[hb_], writes=[qb_])
            S.op("dve", lambda e: e.tensor_reduce(out=s_[:, 8:12], in_=h_[:], axis=AX.X, op=ALU.add), reads=[hb_], writes=[sb_])
            S.op("dve", lambda e: e.tensor_reduce(out=s_[:, 12:16], in_=q_[:], axis=AX.X, op=ALU.add), reads=[qb_], writes=[sb_])
            S.op("dve", lambda e: e.tensor_scalar(out=s_[:, 16:20], in0=s_[:, 8:12], scalar1=1.0 / 64, scalar2=None,
                                                  op0=ALU.mult), reads=[sb_], writes=[sb_])
            S.op("dve", lambda e: e.tensor_tensor(out=s_[:, 20:24], in0=s_[:, 16:20], in1=s_[:, 16:20], op=ALU.mult),
                 reads=[sb_], writes=[sb_])
            S.op("dve", lambda e: e.scalar_tensor_tensor(out=s_[:, 24:28], in0=s_[:, 12:16], scalar=1.0 / 64,
                                                         in1=s_[:, 20:24], op0=ALU.mult, op1=ALU.subtract),
                 reads=[sb_], writes=[sb_])
            S.op("dve", lambda e: e.tensor_scalar(out=s_[:, 24:28], in0=s_[:, 24:28], scalar1=0.0, scalar2=RMS_EPS,
                                                  op0=ALU.max, op1=ALU.add), reads=[sb_], writes=[sb_])
            S.op("act", lambda e: e.activation(out=s_[:, 28:32], in_=s_[:, 24:28], func=AF.Sqrt), reads=[sb_], writes=[sb_])
            S.op("dve", lambda e: e.reciprocal(out=s_[:, 28:32], in_=s_[:, 28:32]), reads=[sb_], writes=[sb_])
            S.op("dve", lambda e: e.tensor_tensor(out=h_[:], in0=h_[:], in1=s_[:, 16:20].unsqueeze(2).broadcast_to([128, 4, 64]),
                                                  op=ALU.subtract), reads=[hb_, sb_], writes=[hb_])
            S.op("dve", lambda e: e.tensor_tensor(out=h_[:], in0=h_[:], in1=s_[:, 28:32].unsqueeze(2).broadcast_to([128, 4, 64]),
                                                  op=ALU.mult), reads=[hb_, sb_], writes=[hb_])
            hf = h_[:].rearrange("p h d -> p (h d)")
            S.op("dve", lambda e: e.tensor_tensor(out=hf, in0=hf, in1=ng[:], op=ALU.mult), reads=[hb_, ngb], writes=[hb_])
            y_, yb_ = yst.next()
            S.op("dve", lambda e: e.tensor_tensor(out=y_[:], in0=hf, in1=og[:, c, :], op=ALU.mult),
                 reads=[hb_, ogb], writes=[yb_])
            S.dma("pool", SC["y"][seq, cols, 0:256], y_[:], reads=[yb_])
        S.barrier()


def phase_dil(S, W, CS, SC, l, seq, T, jrev, jb):
    with ExitStack() as st:
        C = Ctx(S, st)
        qk, qkb = C.sb([128, 4, T], BF16, "dqk")
        Rd, Rdb = C.sb([128, 3, 4, 256], BF16, "Rd")
        vt = [C.sb([128, 260], BF16, "dvt") for _ in range(6)]
        PT = Rot([C.sb([128, 256], BF16, "dPT") for _ in range(6)])
        PTr = Rot([C.sb([128, 256], BF16, "dPTr") for _ in range(4)])
        Ed, Edb = C.sb([128, 3, 4, 256], BF16, "Ed")
        ost = Rot([C.sb([128, 260], F32, "dost") for _ in range(3)])
        pscs = [Rot([C.ps([128, 512], F32, "dpsc") for _ in range(2)]) for _ in range(2)]
        po = Rot([C.ps([128, 512], F32, "dpo") for _ in range(2)])
        for c in range(4):
            S.dma("sp", qk[:, c, :], SC["qkT"][seq, (4 + c) * 128:(5 + c) * 128, :], writes=[qkb])
        gv = SC["gvec"]
        for p in range(3):
            for hd in range(4):
                src = bass.AP(gv.tensor, gv.offset + hd * NG + p * 384, [[1, 128], [1, 256]])
                S.dma("sp", Rd[:, p, hd, :], src, writes=[Rdb])
        for p in range(3):
            for hd in range(4):
                ps_, psb_ = pscs[hd % 2].next()
                S.op("pe", lambda e: e.matmul(ps_[:, 0:256], lhsT=jrev[:], rhs=Rd[:, p, hd, :], start=True, stop=True),
                     reads=[jb, Rdb], writes=[psb_])
                S.op("act", lambda e: e.activation(out=Ed[:, p, hd, :], in_=ps_[:, 0:256], func=AF.Exp),
                     reads=[psb_], writes=[Edb])
        for p, (w, d) in enumerate(DIL):
            L = T // d
            ntl = L // 128
            for r in range(d):
                grp = {}

                def dscores(i, hd):
                    t0 = r + d * 128 * i
                    ks = [i, i - 1] if i > 0 else [i]
                    nk = len(ks)
                    if hd == 0:
                        for ii in ([0, 1, 2] if i == 0 else [i + 2]):
                            if ii < ntl:
                                v_, vb_ = vt[ii % 6]
                                tt = r + d * 128 * ii
                                S.dma("sp", v_[:], SC["dv1"][seq, tt:tt + d * 127 + 1:d, :], writes=[vb_])
                    ch, rs = hd // 2, slice((hd % 2) * 64, (hd % 2) * 64 + 64)
                    ps_, psb_ = pscs[hd % 2].next()
                    qsl = qk[rs, ch, t0:t0 + d * 127 + 1:d]
                    for jj, j in enumerate(ks):
                        k0 = r + d * 128 * j
                        S.op("pe", lambda e: e.matmul(ps_[:, jj * 128:(jj + 1) * 128], lhsT=qk[rs, 2 + ch, k0:k0 + d * 127 + 1:d],
                                                      rhs=qsl, start=True, stop=True), reads=[qkb], writes=[psb_])
                    pr_, prb_ = PTr.next()
                    S.op("act", lambda e: e.activation(out=pr_[:, 0:nk * 128], in_=ps_[:, 0:nk * 128], func=AF.Exp),
                         reads=[psb_], writes=[prb_])
                    p_, pb_ = PT.next()
                    S.op("dve", lambda e: e.tensor_tensor(out=p_[:, 0:nk * 128], in0=pr_[:, 0:nk * 128],
                                                          in1=Ed[:, p, hd, 0:nk * 128], op=ALU.mult),
                         reads=[prb_, Edb], writes=[pb_])
                    return p_, pb_

                def dpv(i, hd, p_, pb_):
                    t0 = r + d * 128 * i
                    ks = [i, i - 1] if i > 0 else [i]
                    nk = len(ks)
                    if hd == 0:
                        grp[i] = po.next()
                    po_, pob_ = grp[i]
                    for jj, j in enumerate(ks):
                        vj, vjb = vt[j % 6]
                        S.op("pe", lambda e: e.matmul(po_[:, hd * 65:(hd + 1) * 65], lhsT=p_[:, jj * 128:(jj + 1) * 128],
                                                      rhs=vj[:, hd * 65:(hd + 1) * 65], start=(jj == 0), stop=(jj == nk - 1)),
                             reads=[pb_, vjb], writes=[pob_])
                    if hd == 3:
                        o_, ob_ = ost.next()
                        S.op("dve", lambda e: e.tensor_copy(out=o_[:], in_=po_[:, 0:260]), reads=[pob_], writes=[ob_])
                        S.dma("pool", SC["dacc"][seq, p, t0:t0 + d * 127 + 1:d, :], o_[:], reads=[ob_])
                        del grp[i]

                units = [(i, hd) for i in range(ntl) for hd in range(4)]
                LA = 3
                pend = {}
                nxt = 0
                for u in range(len(units)):
                    while nxt < len(units) and nxt <= u + LA:
                        pend[nxt] = dscores(*units[nxt])
                        nxt += 1
                    dpv(*units[u], *pend.pop(u))
        S.barrier()
        ld = Rot([C.sb([128, 3, 260], F32, "dld") for _ in range(2)])
        yst = Rot([C.sb([128, 256], BF16, "dyst") for _ in range(2)])
        rd = Rot([C.sb([128, 4], F32, "drd") for _ in range(2)])
        for n in range(T // 128):
            rows = slice(n * 128, (n + 1) * 128)
            a_, ab_ = ld.next()
            S.dma("sp", a_[:], SC["dacc"][seq, :, rows, :].rearrange("t p c -> p t c"), writes=[ab_])
            S.op("dve", lambda e: e.tensor_tensor(out=a_[:, 0, :], in0=a_[:, 0, :], in1=a_[:, 1, :], op=ALU.add),
                 reads=[ab_], writes=[ab_])
            S.op("dve", lambda e: e.tensor_tensor(out=a_[:, 0, :], in0=a_[:, 0, :], in1=a_[:, 2, :], op=ALU.add),
                 reads=[ab_], writes=[ab_])
            r_, rb_ = rd.next()
            av = a_[:, 0, :].rearrange("p (h d) -> p h d", h=4)
            S.op("dve", lambda e: e.reciprocal(out=r_[:].unsqueeze(2), in_=av[:, :, 64:65]), reads=[ab_], writes=[rb_])
            y_, yb_ = yst.next()
            S.op("dve", lambda e: e.tensor_tensor(out=y_[:].rearrange("p (h d) -> p h d", h=4), in0=av[:, :, 0:64],
                                                  in1=r_[:].unsqueeze(2).broadcast_to([128, 4, 64]), op=ALU.mult),
                 reads=[ab_, rb_], writes=[yb_])
            S.dma("pool", SC["y"][seq, rows, 512:768], y_[:], reads=[yb_])
        S.barrier()


def phase_diff(S, W, CS, SC, l, seq, T, jrev, jb):
    NCH = T // 128
    lam_init = 0.8 - 0.6 * math.exp(-0.3 * l)
    with ExitStack() as st:
        C = Ctx(S, st)
        qk, qkb = C.sb([64, 8, T], BF16, "fqk")
        v1, v1b = C.sb([128, NCH, 260], BF16, "fv1")
        Rf, Rfb = C.sb([128, 4, 3072], BF16, "Rf")
        Ef, Efb = C.sb([128, 4, 3072], BF16, "Ef")
        PTr = Rot([C.sb([128, 2, 512], BF16, "fPTr") for _ in range(3)])
        lv, lvb = C.sb([1, 160], F32, "lv")
        ones1, o1b = C.sb([1, 128], F32, "ones1")
        nlam, nlb = C.sb([128, 1], F32, "nlam")
        subg, sgb = C.sb([128, 64], F32, "subg")
        zl, zlb = C.sb([1, 128], BF16, "zl")
        zr, zrb = C.sb([1, 260], BF16, "zr")
        PT = Rot([C.sb([128, 2, 512], BF16, "fPT") for _ in range(5)])
        om = [C.sb([128, 4, 65], F32, "om") for _ in range(2)]
        om2 = [C.sb([128, 4, 64], F32, "om2") for _ in range(2)]
        rdn = Rot([C.sb([128, 8], F32, "rdn") for _ in range(4)])
        od = Rot([C.sb([128, 4, 64], F32, "od") for _ in range(2)])
        sq = Rot([C.sb([128, 4, 64], F32, "fsq") for _ in range(2)])
        yst = Rot([C.sb([128, 4, 64], BF16, "fyst") for _ in range(2)])
        psc2 = Rot([C.ps([128, 2, 512], F32, "fpsc2") for _ in range(2)])
        pscs = [Rot([(t_[:, mm, :], b_) for (t_, b_) in psc2.items]) for mm in range(2)]
        po = Rot([C.ps([128, 512], F32, "fpo") for _ in range(4)])
        plam = po.items[0]

        for c in range(8):
            S.dma("sp", qk[:, c, :], SC["qkT"][seq, 1024 + c * 64:1024 + (c + 1) * 64, :], writes=[qkb])
        load_tok(S, v1, v1b, SC["fv1"][seq], NCH)
        gv = SC["gvec"]
        for hd in range(4):
            for part in range(2):
                src = bass.AP(gv.tensor, gv.offset + (4 + hd) * NG + 1152 + part * 1536, [[1, 128], [1, 1536]])
                S.dma("sp", Rf[:, hd, part * 1536:(part + 1) * 1536], src, writes=[Rfb])
        for hd in range(4):
            for n in range(6):
                pt2_, psb_ = psc2.next()
                S.op("pe", lambda e: e.matmul(pt2_[:, 0, :], lhsT=jrev[:], rhs=Rf[:, hd, n * 512:(n + 1) * 512], start=True, stop=True),
                     reads=[jb, Rfb], writes=[psb_])
                S.op("act", lambda e: e.activation(out=Ef[:, hd, n * 512:(n + 1) * 512], in_=pt2_[:, 0, :], func=AF.Exp),
                     reads=[psb_], writes=[Efb])
        S.dma("sp", lv[:, 0:128], W["diff_lambda"][l].rearrange("(o a) b -> o (a b)", o=1), writes=[lvb])
        S.op("dve", lambda e: e.tensor_tensor(out=lv[:, 128:160], in0=lv[:, 0:32], in1=lv[:, 32:64], op=ALU.mult),
             reads=[lvb], writes=[lvb])
        S.op("dve", lambda e: e.tensor_reduce(out=lv[:, 0:1], in_=lv[:, 128:160], axis=AX.X, op=ALU.add), reads=[lvb], writes=[lvb])
        S.op("dve", lambda e: e.tensor_tensor(out=lv[:, 128:160], in0=lv[:, 64:96], in1=lv[:, 96:128], op=ALU.mult),
             reads=[lvb], writes=[lvb])
        S.op("dve", lambda e: e.tensor_reduce(out=lv[:, 1:2], in_=lv[:, 128:160], axis=AX.X, op=ALU.add), reads=[lvb], writes=[lvb])
        S.op("act", lambda e: e.activation(out=lv[:, 2:4], in_=lv[:, 0:2], func=AF.Exp), reads=[lvb], writes=[lvb])
        S.op("dve", lambda e: e.scalar_tensor_tensor(out=lv[:, 4:5], in0=lv[:, 3:4], scalar=-float(lam_init), in1=lv[:, 2:3],
                                                     op0=ALU.add, op1=ALU.subtract), reads=[lvb], writes=[lvb])
        S.op("dve", lambda e: e.memset(ones1[:], 1.0), writes=[o1b])
        pl, plb = plam
        S.op("pe", lambda e: e.matmul(pl[:, 0:1], lhsT=ones1[:], rhs=lv[:, 4:5], start=True, stop=True),
             reads=[o1b, lvb], writes=[plb])
        S.op("act", lambda e: e.copy(out=nlam[:], in_=pl[:, 0:1]), reads=[plb], writes=[nlb])
        S.dma("sp", subg[:], bcast_row(W["diff_subln"][l], 128), writes=[sgb])
        S.op("dve", lambda e: e.tensor_scalar(out=subg[:], in0=subg[:], scalar1=float(1.0 - lam_init), scalar2=None,
                                              op0=ALU.mult), reads=[sgb], writes=[sgb])
        S.op("dve", lambda e: e.memset(zl[:], 0.0), writes=[zlb])
        S.op("dve", lambda e: e.memset(zr[:], 0.0), writes=[zrb])

        for i in range(T // 512):
            for hd in range(4):
                pos = [po.next(), po.next()]
                for m in range(2):
                    S.op("pe", lambda e: e.matmul(pos[m][0][:, 0:260], lhsT=zl[:], rhs=zr[:], start=True, stop=False),
                         reads=[zlb, zrb], writes=[pos[m][1]])
                nk = 4 * i + 4

                def scores(j):
                    jj = j - 4 * i
                    q0 = max(jj, 0) * 128
                    o = 512 * i - 128 * j
                    x0 = min(o, 2176) + 384 + q0
                    pt2_, psb_ = psc2.next()
                    for m in range(2):
                        rs = slice(32 * m, 32 * m + 32)
                        S.op("pe", lambda e: e.matmul(pt2_[:, m, q0:512], lhsT=qk[rs, 4 + hd, j * 128:(j + 1) * 128],
                                                      rhs=qk[rs, hd, i * 512 + q0:(i + 1) * 512], start=True, stop=True),
                             reads=[qkb], writes=[psb_])
                    pr_, prb_ = PTr.next()
                    S.op("act", lambda e: e.activation(out=pr_[:, :, q0:512], in_=pt2_[:, :, q0:512], func=AF.Exp),
                         reads=[psb_], writes=[prb_])
                    p_, pb_ = PT.next()
                    S.op("dve", lambda e: e.tensor_tensor(out=p_[:, :, q0:512], in0=pr_[:, :, q0:512],
                                                          in1=Ef[:, hd, x0:x0 + 512 - q0].unsqueeze(1).broadcast_to([128, 2, 512 - q0]),
                                                          op=ALU.mult),
                         reads=[prb_, Efb], writes=[pb_])
                    outs = [(p_[:, 0, :], pb_), (p_[:, 1, :], pb_)]
                    return outs, q0

                def pv(j, outs, q0):
                    for m in range(2):
                        p_, pb_ = outs[m]
                        po_, pob_ = pos[m]
                        for s in range(q0 // 128, 4):
                            S.op("pe", lambda e: e.matmul(po_[:, s * 65:(s + 1) * 65], lhsT=p_[:, s * 128:(s + 1) * 128],
                                                          rhs=v1[:, j, hd * 65:(hd + 1) * 65], start=False,
                                                          stop=(j == 4 * i + s)), reads=[pb_, v1b], writes=[pob_])

                LA = 2
                pend = {}
                nxt = 0
                for j in range(nk):
                    while nxt < nk and nxt <= j + LA:
                        pend[nxt] = scores(nxt)
                        nxt += 1
                    pv(j, *pend.pop(j))
                for m in range(2):
                    po_, pob_ = pos[m]
                    o_, ob_ = om[m]
                    S.op("act", lambda e: e.copy(out=o_[:], in_=po_[:, 0:260].rearrange("p (s d) -> p s d", s=4)),
                         reads=[pob_], writes=[ob_])
                    r_, rb_ = rdn.next()
                    S.op("dve", lambda e: e.reciprocal(out=r_[:, 0:4].unsqueeze(2), in_=o_[:, :, 64:65]), reads=[ob_], writes=[rb_])
                    o2_, o2b_ = om2[m]
                    S.op("dve", lambda e: e.tensor_tensor(out=o2_[:], in0=o_[:, :, 0:64],
                                                          in1=r_[:, 0:4].unsqueeze(2).broadcast_to([128, 4, 64]), op=ALU.mult),
                         reads=[ob_, rb_], writes=[o2b_])
                d_, db_ = od.next()
                S.op("dve", lambda e: e.scalar_tensor_tensor(out=d_[:], in0=om2[1][0][:], scalar=nlam[:, 0:1], in1=om2[0][0][:],
                                                             op0=ALU.mult, op1=ALU.add),
                     reads=[om2[0][1], om2[1][1], nlb], writes=[db_])
                q_, qb_ = sq.next()
                S.op("act", lambda e: e.activation(out=q_[:], in_=d_[:], func=AF.Square), reads=[db_], writes=[qb_])
                r_, rb_ = rdn.next()
                S.op("dve", lambda e: e.tensor_reduce(out=r_[:, 0:4], in_=q_[:], axis=AX.X, op=ALU.add), reads=[qb_], writes=[rb_])
                S.op("dve", lambda e: e.tensor_scalar(out=r_[:, 0:4], in0=r_[:, 0:4], scalar1=1.0 / 64, scalar2=1e-5,
                                                      op0=ALU.mult, op1=ALU.add), reads=[rb_], writes=[rb_])
                S.op("act", lambda e: e.activation(out=r_[:, 4:8], in_=r_[:, 0:4], func=AF.Sqrt), reads=[rb_], writes=[rb_])
                S.op("dve", lambda e: e.reciprocal(out=r_[:, 4:8], in_=r_[:, 4:8]), reads=[rb_], writes=[rb_])
                S.op("dve", lambda e: e.tensor_tensor(out=d_[:], in0=d_[:], in1=r_[:, 4:8].unsqueeze(2).broadcast_to([128, 4, 64]),
                                                      op=ALU.mult), reads=[db_, rb_], writes=[db_])
                y_, yb_ = yst.next()
                S.op("dve", lambda e: e.tensor_tensor(out=y_[:], in0=d_[:], in1=subg[:].unsqueeze(1).broadcast_to([128, 4, 64]),
                                                      op=ALU.mult), reads=[db_, sgb], writes=[yb_])
                S.dma("pool", SC["y"][seq, i * 512:(i + 1) * 512, 768 + hd * 64:832 + hd * 64].rearrange("(s p) c -> p s c", p=128),
                      y_[:], reads=[yb_])
        S.barrier()


def phase_outproj(S, W, SC, l, NSEQ, T, h, ident, ibuf):
    with ExitStack() as st:
        C = Ctx(S, st)
        wo, wob = C.sb([128, 8, D], BF16, "wout")
        load_w_bf16(S, wo, wob, W["w_out"][l], 8, D)
        yt = Rot([C.sb([128, D], BF16, "yt") for _ in range(2)])
        yT = Rot([C.sb([128, 8, 128], BF16, "yT") for _ in range(2)])
        ht = Rot([C.sb([128, D], F32, "oht") for _ in range(3)])
        ptp = Rot([C.ps([128, 1024], BF16, "optp") for _ in range(2)])
        pso = Rot([C.ps([128, 512], F32, "opso") for _ in range(4)])
        for seq in range(NSEQ):
            for n in range(T // 128):
                rows = slice(n * 128, (n + 1) * 128)
                hrows = slice(seq * T + n * 128, seq * T + (n + 1) * 128)
                y_, yb_ = yt.next()
                S.dma("sp", y_[:], SC["y"][seq, rows, :], writes=[yb_])
                h_, hb_ = ht.next()
                S.dma("sp", h_[:], h[hrows, :], writes=[hb_])
                pt, ptb = ptp.next()
                for c in range(8):
                    S.op("pe", lambda e: e.transpose(out=pt[:, c * 128:(c + 1) * 128], in_=y_[:, c * 128:(c + 1) * 128],
                                                     identity=ident[:]), reads=[yb_, ibuf], writes=[ptb])
                yT_, yTb_ = yT.next()
                S.op("act", lambda e: e.copy(out=yT_[:, 0:4, :], in_=pt[:, 0:512].rearrange("p (c t) -> p c t", c=4)),
                     reads=[ptb], writes=[yTb_])
                S.op("dve", lambda e: e.tensor_copy(out=yT_[:, 4:8, :], in_=pt[:, 512:1024].rearrange("p (c t) -> p c t", c=4)),
                     reads=[ptb], writes=[yTb_])
                for nn in range(2):
                    p_, pb_ = pso.next()
                    for k in range(8):
                        S.op("pe", lambda e: e.matmul(p_[:], lhsT=yT_[:, k, :], rhs=wo[:, k, nn * 512:(nn + 1) * 512],
                                                      start=(k == 0), stop=(k == 7)), reads=[yTb_, wob], writes=[pb_])
                    S.op("dve", lambda e: e.tensor_tensor(out=h_[:, nn * 512:(nn + 1) * 512], in0=p_[:],
                                                          in1=h_[:, nn * 512:(nn + 1) * 512], op=ALU.add),
                         reads=[pb_, hb_], writes=[hb_])
                S.dma("pool", h[hrows, :], h_[:], reads=[hb_])
        S.barrier()


def phase_xattn(S, W, l, NSEQ, T, MEM, h, mem, ident, ibuf, xw):
    NB = T // 512
    NMT = MEM // 128
    with ExitStack() as st:
        C = Ctx(S, st)
        (wq, wqb), (wkv, wkvb), (wo, wob) = xw
        xg, xgb = C.sb([128, D], F32, "xg")
        mg, mgb = C.sb([128, D], F32, "mg")
        res = norm_res(C)
        uTs = [C.sb([128, 8, 512], BF16, "xuT") for _ in range(2)]
        mT, mTb = C.sb([128, 8, MEM], BF16, "mT")
        KT, KTb = C.sb([128, 8, MEM], BF16, "KT")
        V1, V1b = C.sb([128, NMT, 4, 257], BF16, "V1")
        qT, qTb = C.sb([128, 8, 512], BF16, "qT")
        PT = Rot([C.sb([128, 512], BF16, "xPT") for _ in range(6)])
        osb, osbb = C.sb([128, 4, D], BF16, "osb")
        oT, oTb = C.sb([128, 8, 512], BF16, "oT")
        rd = Rot([C.sb([128, 1], F32, "xrd") for _ in range(4)])
        ht = Rot([C.sb([128, D], F32, "xht") for _ in range(8)])
        oTbs = [Buf() for _ in range(4)]
        psq = Rot([C.ps([128, 512], F32, "psq") for _ in range(2)])
        pov = Rot([C.ps([128, 512], F32, "pov") for _ in range(4)])
        pwo = pov
        S.dma("sp", xg[:], bcast_row(W["xattn_norm"][l], 128), writes=[xgb])
        S.dma("sp", mg[:], bcast_row(W["mem_norm"][l], 128), writes=[mgb])
        S.op("dve", lambda e: e.memset(V1[:], 1.0), writes=[V1b])
        for seq in range(NSEQ):
            norm_block(S, C, mem, seq * MEM, NMT, mg, mgb, ident, ibuf, mT, mTb, res)
            for c in range(8):
                p_, pb_ = psq.next()
                for k in range(8):
                    S.op("pe", lambda e: e.matmul(p_[:, 0:MEM], lhsT=wkv[:, k, c * 128:(c + 1) * 128], rhs=mT[:, k, :],
                                                  start=(k == 0), stop=(k == 7)), reads=[wkvb, mTb], writes=[pb_])
                S.op("act", lambda e: e.copy(out=KT[:, c, :], in_=p_[:, 0:MEM]), reads=[pb_], writes=[KTb])
            for j in range(NMT):
                for nn in range(2):
                    p_, pb_ = psq.next()
                    for k in range(8):
                        S.op("pe", lambda e: e.matmul(p_[:], lhsT=mT[:, k, j * 128:(j + 1) * 128],
                                                      rhs=wkv[:, k, D + nn * 512:D + (nn + 1) * 512],
                                                      start=(k == 0), stop=(k == 7)), reads=[wkvb, mTb], writes=[pb_])
                    S.op("act", lambda e: e.copy(out=V1[:, j, 2 * nn:2 * nn + 2, 0:256],
                                                 in_=p_[:].rearrange("p (h d) -> p h d", h=2)),
                         reads=[pb_], writes=[V1b])
            norm_block(S, C, h, seq * T, 4, xg, xgb, ident, ibuf, uTs[0][0], uTs[0][1], res)
            for b in range(NB):
                uT, uTb = uTs[b % 2]
                tok0 = seq * T + b * 512
                hts = [ht.next() for _ in range(4)]
                for s4 in range(4):
                    S.dma("sp", hts[s4][0][:], h[tok0 + s4 * 128:tok0 + (s4 + 1) * 128, :], writes=[hts[s4][1]])
                for c in range(8):
                    p_, pb_ = psq.next()
                    for k in range(8):
                        S.op("pe", lambda e: e.matmul(p_[:], lhsT=wq[:, k, c * 128:(c + 1) * 128], rhs=uT[:, k, :],
                                                      start=(k == 0), stop=(k == 7)), reads=[wqb, uTb], writes=[pb_])
                    S.op("act", lambda e: e.activation(out=qT[:, c, :], in_=p_[:], func=AF.Copy, scale=1.0 / 16),
                         reads=[pb_], writes=[qTb])
                def xscores(hd):
                    pts = []
                    for j in range(NMT):
                        p_, pb_ = psq.next()
                        for cc in range(2):
                            S.op("pe", lambda e: e.matmul(p_[:], lhsT=KT[:, 2 * hd + cc, j * 128:(j + 1) * 128],
                                                          rhs=qT[:, 2 * hd + cc, :], start=(cc == 0), stop=(cc == 1)),
                                 reads=[KTb, qTb], writes=[pb_])
                        t_, tb_ = PT.next()
                        S.op("act", lambda e: e.activation(out=t_[:], in_=p_[:], func=AF.Exp), reads=[pb_], writes=[tb_])
                        pts.append((t_, tb_))
                    return pts

                def xpv(hd, pts):
                    for s in range(4):
                        p_, pb_ = pov.next()
                        for j in range(NMT):
                            t_, tb_ = pts[j]
                            S.op("pe", lambda e: e.matmul(p_[:, 0:257], lhsT=t_[:, s * 128:(s + 1) * 128], rhs=V1[:, j, hd, :],
                                                          start=(j == 0), stop=(j == NMT - 1)), reads=[tb_, V1b], writes=[pb_])
                        r_, rb_ = rd.next()
                        S.op("dve", lambda e: e.reciprocal(out=r_[:], in_=p_[:, 256:257]), reads=[pb_], writes=[rb_])
                        S.op("act", lambda e: e.activation(out=osb[:, s, hd * 256:(hd + 1) * 256], in_=p_[:, 0:256],
                                                           func=AF.Identity, scale=r_[:, 0:1]),
                             reads=[pb_, rb_], writes=[osbb])

                prevp = xscores(0)
                for hd in range(1, 4):
                    curp = xscores(hd)
                    xpv(hd - 1, prevp)
                    prevp = curp
                xpv(3, prevp)
                if b + 1 < NB:
                    norm_block(S, C, h, tok0 + 512, 4, xg, xgb, ident, ibuf, uTs[(b + 1) % 2][0], uTs[(b + 1) % 2][1], res)
                for s in range(4):
                    pt, ptb = res["ptp"][s % 2]
                    for c in range(8):
                        S.op("pe", lambda e: e.transpose(out=pt[:, c * 128:(c + 1) * 128], in_=osb[:, s, c * 128:(c + 1) * 128],
                                                         identity=ident[:]), reads=[osbb, ibuf], writes=[ptb])
                    S.op("act", lambda e: e.copy(out=oT[:, 0:4, s * 128:(s + 1) * 128],
                                                 in_=pt[:, 0:512].rearrange("p (c t) -> p c t", c=4)), reads=[ptb], writes=[oTbs[s]])
                    S.op("dve", lambda e: e.tensor_copy(out=oT[:, 4:8, s * 128:(s + 1) * 128],
                                                        in_=pt[:, 512:1024].rearrange("p (c t) -> p c t", c=4)),
                         reads=[ptb], writes=[oTbs[s]])
                for s in range(4):
                    hrows = slice(tok0 + s * 128, tok0 + (s + 1) * 128)
                    h_, hb_ = hts[s]
                    for nn in range(2):
                        p_, pb_ = pwo.next()
                        for k in range(8):
                            S.op("pe", lambda e: e.matmul(p_[:], lhsT=oT[:, k, s * 128:(s + 1) * 128],
                                                          rhs=wo[:, k, nn * 512:(nn + 1) * 512], start=(k == 0), stop=(k == 7)),
                                 reads=[oTbs[s], wob], writes=[pb_])
                        S.op("dve", lambda e: e.tensor_tensor(out=h_[:, nn * 512:(nn + 1) * 512], in0=p_[:],
                                                              in1=h_[:, nn * 512:(nn + 1) * 512], op=ALU.add),
                             reads=[pb_, hb_], writes=[hb_])
                    S.dma("pool", h[hrows, :], h_[:], reads=[hb_])
        S.barrier()


def phase_final(S, W, ntok, h, out):
    with ExitStack() as st:
        C = Ctx(S, st)
        gain, gb = C.sb([128, D], F32, "fgain")
        S.dma("sp", gain[:], bcast_row(W["final_norm"], 128), writes=[gb])
        ht = Rot([C.sb([128, D], F32, "fh") for _ in range(3)])
        jk = Rot([C.sb([128, D], BF16, "fjk") for _ in range(2)])
        ss = Rot([C.sb([128, 4], F32, "fss") for _ in range(3)])
        for n in range(ntok // 128):
            rows = slice(n * 128, (n + 1) * 128)
            h_, hb_ = ht.next()
            j_, jb_ = jk.next()
            s_, sb_ = ss.next()
            S.dma("sp", h_[:], h[rows, :], writes=[hb_])
            S.op("act", lambda e: e.activation(out=j_[:], in_=h_[:], func=AF.Square, accum_out=s_[:, 0:1]),
                 reads=[hb_], writes=[jb_, sb_])
            S.op("dve", lambda e: e.tensor_scalar(out=s_[:, 1:2], in0=s_[:, 0:1], scalar1=1.0 / D, scalar2=RMS_EPS,
                                                  op0=ALU.mult, op1=ALU.add), reads=[sb_], writes=[sb_])
            S.op("act", lambda e: e.activation(out=s_[:, 2:3], in_=s_[:, 1:2], func=AF.Sqrt), reads=[sb_], writes=[sb_])
            S.op("dve", lambda e: e.reciprocal(out=s_[:, 3:4], in_=s_[:, 2:3]), reads=[sb_], writes=[sb_])
            S.op("dve", lambda e: e.scalar_tensor_tensor(out=h_[:], in0=h_[:], scalar=s_[:, 3:4], in1=gain[:],
                                                         op0=ALU.mult, op1=ALU.mult), reads=[hb_, sb_, gb], writes=[hb_])
            S.dma("pool", out[rows, :], h_[:], reads=[hb_])
        S.barrier()


WEIGHT_SHAPES = {
    "t5_bias": (32, 8), "ffn1_norm": (4, D), "ffn1_w_gate": (4, D, DFF), "ffn1_w_up": (4, D, DFF),
    "ffn1_w_down": (4, DFF, D), "mix_norm": (4, D), "w_in": (4, D, INW), "mlstm_conv_w": (4, 4, 512),
    "mlstm_conv_b": (4, 512), "mlstm_gate_b": (4, 8), "mlstm_norm": (4, 256), "pool_w": (4, 4, 64, 64),
    "pool_scale": (4, 256), "diff_lambda": (4, 4, 32), "diff_subln": (4, 64), "w_out": (4, D, D),
    "xattn_norm": (4, D), "mem_norm": (4, D), "xattn_wq": (4, D, D), "xattn_wkv": (4, D, 2 * D),
    "xattn_wo": (4, D, D), "ffn2_norm": (4, D), "ffn2_w_gate": (4, D, DFF), "ffn2_w_up": (4, D, DFF),
    "ffn2_w_down": (4, DFF, D), "final_norm": (D,),
}

ALL_PHASES = ("ffn1", "mixproj", "mlstm", "dil", "diff", "outproj", "xattn", "ffn2", "final")


def build(T=4096, NSEQ=2, DEPTH=4, MEM=256, phases=ALL_PHASES, debug=()):
    nc = bass.Bass("TRN2", target_bir_lowering=False)
    ntok = T * NSEQ
    x = nc.dram_tensor("x", [ntok, D], F32, kind="ExternalInput").ap()
    mem = nc.dram_tensor("mem", [NSEQ * MEM, D], F32, kind="ExternalInput").ap()
    W = {k: nc.dram_tensor(k, list(s), F32, kind="ExternalInput").ap() for k, s in WEIGHT_SHAPES.items()}
    CS = {k: nc.dram_tensor(k, list(s), F32, kind="ExternalInput").ap() for k, s in CONST_SHAPES.items()}
    out = nc.dram_tensor("out", [ntok, D], F32, kind="ExternalOutput").ap()

    def scratch(name, shape, dt):
        kind = "ExternalOutput" if name in debug else "Internal"
        return nc.dram_tensor("s_" + name, list(shape), dt, kind=kind).ap()

    SC = {
        "qkT": scratch("qkT", [NSEQ, 12 * 128, T], BF16),
        "kg": scratch("kg", [NSEQ, T, 256], BF16),
        "mv1": scratch("mv1", [NSEQ, T, 260], BF16),
        "og": scratch("og", [NSEQ, T, 256], BF16),
        "ag": scratch("ag", [NSEQ, 128, T // 128, 8], F32),
        "eb": scratch("eb", [NSEQ, 128, T // 128, 4], F32),
        "dv1": scratch("dv1", [NSEQ, T, 260], BF16),
        "fv1": scratch("fv1", [NSEQ, T, 260], BF16),
        "dacc": scratch("dacc", [NSEQ, 3, T, 260], F32),
        "y": scratch("y", [NSEQ, T, D], BF16),
        "gvec": scratch("gvec", [8, NG], BF16),
        "dbg": scratch("dbg", [128, 64], F32),
    }
    h = out
    with ExitStack() as st:
        S = Sched(nc, st)
        C0 = Ctx(S, st)
        ident, ibuf, jrev, jb = setup_consts(S, C0, W, CS, SC)
        for l in range(DEPTH):
            if "ffn1" in phases:
                phase_ffn(S, W, "ffn1", l, x if l == 0 else h, h, ntok, ident, ibuf)
            if "mixproj" in phases:
                phase_mixproj(S, W, CS, SC, l, NSEQ, T, h, ident, ibuf)
            for seq in range(NSEQ):
                if "mlstm" in phases:
                    phase_mlstm(S, W, CS, SC, l, seq, T)
                if "dil" in phases:
                    phase_dil(S, W, CS, SC, l, seq, T, jrev, jb)
                if "diff" in phases:
                    phase_diff(S, W, CS, SC, l, seq, T, jrev, jb)
            with ExitStack() as xst:
                Cx = Ctx(S, xst)
                xw = None
                if "xattn" in phases:
                    xw = (Cx.sb([128, 8, D], BF16, "wq"), Cx.sb([128, 8, 2 * D], BF16, "wkv"), Cx.sb([128, 8, D], BF16, "wo"))
                    load_w_bf16(S, xw[0][0], xw[0][1], W["xattn_wq"][l], 8, D)
                    load_w_bf16(S, xw[1][0], xw[1][1], W["xattn_wkv"][l], 8, 2 * D)
                    load_w_bf16(S, xw[2][0], xw[2][1], W["xattn_wo"][l], 8, D)
                if "outproj" in phases:
                    phase_outproj(S, W, SC, l, NSEQ, T, h, ident, ibuf)
                if "xattn" in phases:
                    phase_xattn(S, W, l, NSEQ, T, MEM, h, mem, ident, ibuf, xw)
            if "ffn2" in phases:
                phase_ffn(S, W, "ffn2", l, h, h, ntok, ident, ibuf)
        if "final" in phases:
            phase_final(S, W, ntok, h, out)
        S.barrier()
        print("instructions:", S.ninst, {e: c for e, c in S.cnt.items()})
    return nc


_CONSTS = None


def kernel(**inputs):
    global _CONSTS
    if _CONSTS is None:
        _CONSTS = host_consts()
    x = np.ascontiguousarray(inputs["x"], dtype=np.float32)
    mem = np.ascontiguousarray(inputs["mem"], dtype=np.float32)
    B, T, _ = x.shape
    MEM = mem.shape[1]
    ncores = 8
    nseq = B // ncores
    nc = build(T=T, NSEQ=nseq, DEPTH=4, MEM=MEM)
    base = {k: np.ascontiguousarray(inputs[k], dtype=np.float32) for k in WEIGHT_SHAPES}
    base.update(_CONSTS)
    in_maps = []
    for c in range(ncores):
        m = dict(base)
        m["x"] = x[c * nseq:(c + 1) * nseq].reshape(nseq * T, D)
        m["mem"] = mem[c * nseq:(c + 1) * nseq].reshape(nseq * MEM, D)
        in_maps.append(m)
    res = run_bass_kernel_spmd(nc, in_maps, core_ids=list(range(ncores)))
    outs = [np.asarray(r["out"], dtype=np.float32).reshape(nseq, T, D) for r in res.results]
    return np.concatenate(outs, axis=0)
```

```python
import math
import os
MIXSTAGE = int(os.environ.get('MIXSTAGE', '9'))
MLSTAGE = int(os.environ.get('MLSTAGE', '9'))
from contextlib import ExitStack
import numpy as np
import concourse.bass as bass
import concourse.mybir as mybir
from concourse.bass_utils import run_bass_kernel_spmd

F32 = mybir.dt.float32
BF16 = mybir.dt.bfloat16
AF = mybir.ActivationFunctionType
ALU = mybir.AluOpType
AX = mybir.AxisListType

D = 1024
DFF = 2816
NFF = DFF // 128
INW = 2824
RMS_EPS = 1e-6
N_DMA_SEMS = 24


class Buf:
    __slots__ = ("w", "r")

    def __init__(self):
        self.w = None
        self.r = {}


class Sched:
    def __init__(self, nc, st):
        self.nc = nc
        self.eng = {"pe": nc.tensor, "act": nc.scalar, "dve": nc.vector, "pool": nc.gpsimd, "sp": nc.sync}
        self.sem = {}
        for e in self.eng:
            self.sem[e] = st.enter_context(nc.semaphore("s_" + e))
        for i in range(N_DMA_SEMS):
            self.sem[("d", i)] = st.enter_context(nc.semaphore("s_d%d" % i))
        self.cnt = {e: 0 for e in self.eng}
        self.dval = [0] * N_DMA_SEMS
        self.rr = 0
        self.known = {e: {} for e in self.eng}
        self.uid = 0
        self.ninst = 0

    def name(self, p):
        self.uid += 1
        return "%s_%d" % (p, self.uid)

    def _wait(self, e, deps):
        kn = self.known[e]
        for k, v in deps.items():
            if k == e and e == "pe":
                continue
            if kn.get(k, 0) >= v:
                continue
            self.eng[e].wait_ge(self.sem[k], v)
            self.ninst += 1
            kn[k] = v

    @staticmethod
    def _deps(reads, writes, deps):
        def add(ev):
            if ev is not None and deps.get(ev[0], 0) < ev[1]:
                deps[ev[0]] = ev[1]
        for b in reads:
            add(b.w)
        for b in writes:
            add(b.w)
            for k, v in b.r.items():
                add((k, v))

    @staticmethod
    def _mark(ev, reads, writes):
        for b in reads:
            if b.r.get(ev[0], 0) < ev[1]:
                b.r[ev[0]] = ev[1]
        for b in writes:
            b.w = ev
            b.r = {}

    def op(self, e, fn, reads=(), writes=()):
        deps = {}
        self._deps(reads, writes, deps)
        self._wait(e, deps)
        inst = fn(self.eng[e])
        self.cnt[e] += 1
        self.ninst += 1
        inst.then_inc(self.sem[e], 1)
        self._mark((e, self.cnt[e]), reads, writes)

    def dma(self, e, out, in_, reads=(), writes=(), **kw):
        i = self.rr
        self.rr = (i + 1) % N_DMA_SEMS
        k = ("d", i)
        deps = {}
        if self.dval[i] > 0:
            deps[k] = self.dval[i]
        self._deps(reads, writes, deps)
        self._wait(e, deps)
        inst = self.eng[e].dma_start(out=out, in_=in_, **kw)
        self.ninst += 1
        self.dval[i] += 16
        inst.then_inc(self.sem[k], 16)
        self._mark((k, self.dval[i]), reads, writes)

    def barrier(self):
        deps = {e: c for e, c in self.cnt.items() if c > 0}
        for i in range(N_DMA_SEMS):
            if self.dval[i] > 0:
                deps[("d", i)] = self.dval[i]
        for e in self.eng:
            self._wait(e, dict(deps))


class Ctx:
    def __init__(self, S, st):
        self.S = S
        self.st = st
        self.nc = S.nc

    def sb(self, shape, dt, name="t"):
        t = self.st.enter_context(self.nc.sbuf_tensor(self.S.name(name), list(shape), dt))
        return t, Buf()

    def ps(self, shape, dt, name="p"):
        t = self.st.enter_context(self.nc.psum_tensor(self.S.name(name), list(shape), dt))
        return t, Buf()


def bcast_row(handle_ap, nparts):
    a = handle_ap
    return bass.AP(a.tensor, a.offset, [[0, nparts]] + [list(x) for x in a.ap])


def load_w_bf16(S, dst, dbuf, src, kchunks, ncols):
    for c in range(kchunks):
        S.dma("pool", dst[:, c, :], src[c * 128:(c + 1) * 128, :], writes=[dbuf], max_dma_last_dim=4096)


def norm_pre(S, C, h_src, r0, ntile, gain, gbuf, res, lq="sp"):
    hn, u, ss = res["hn"], res["u"], res["ss"]
    res["i"] = res.get("i", 0) + 1
    sst, ssb = ss[res["i"] % len(ss)]
    tiles = []
    us = []
    for i in range(ntile):
        res["j"] = res.get("j", 0) + 1
        ht, hb = hn[res["j"] % len(hn)]
        res["k"] = res.get("k", 0) + 1
        ut, ub = u[res["k"] % len(u)]
        rows = slice(r0 + i * 128, r0 + (i + 1) * 128)
        S.dma(lq, ht[:], h_src[rows, :], writes=[hb])
        S.op("act", lambda e: e.activation(out=ut[:], in_=ht[:], func=AF.Square, accum_out=sst[:, i:i + 1]),
             reads=[hb], writes=[ub, ssb])
        tiles.append((ht, hb))
        us.append((ut, ub))
    S.op("dve", lambda e: e.tensor_scalar(out=sst[:, 4:4 + ntile], in0=sst[:, 0:ntile], scalar1=1.0 / D, scalar2=RMS_EPS,
                                          op0=ALU.mult, op1=ALU.add), reads=[ssb], writes=[ssb])
    S.op("act", lambda e: e.activation(out=sst[:, 8:8 + ntile], in_=sst[:, 4:4 + ntile], func=AF.Sqrt), reads=[ssb], writes=[ssb])
    S.op("dve", lambda e: e.reciprocal(out=sst[:, 12:12 + ntile], in_=sst[:, 8:8 + ntile]), reads=[ssb], writes=[ssb])
    for i in range(ntile):
        ht, hb = tiles[i]
        ut, ub = us[i]
        S.op("dve", lambda e: e.scalar_tensor_tensor(out=ut[:], in0=ht[:], scalar=sst[:, 12 + i:13 + i], in1=gain[:],
                                                     op0=ALU.mult, op1=ALU.mult),
             reads=[hb, ssb, gbuf], writes=[ub])
    return us


def norm_post(S, us, ident, ibuf, uT, uTbuf, res):
    ptp = res["ptp"]
    for i, (ut, ub) in enumerate(us):
        res["m"] = res.get("m", 0) + 1
        pt, pb = ptp[res["m"] % len(ptp)]
        for c in range(8):
            S.op("pe", lambda e, c=c: e.transpose(out=pt[:, c * 128:(c + 1) * 128], in_=ut[:, c * 128:(c + 1) * 128],
                                                  identity=ident[:]), reads=[ub, ibuf], writes=[pb])
        S.op("act", lambda e: e.copy(out=uT[:, 0:4, i * 128:(i + 1) * 128],
                                     in_=pt[:, 0:512].rearrange("p (c t) -> p c t", c=4)),
             reads=[pb], writes=[uTbuf])
        S.op("dve", lambda e: e.tensor_copy(out=uT[:, 4:8, i * 128:(i + 1) * 128],
                                            in_=pt[:, 512:1024].rearrange("p (c t) -> p c t", c=4)),
             reads=[pb], writes=[uTbuf])


def norm_block(S, C, h_src, r0, ntile, gain, gbuf, ident, ibuf, uT, uTbuf, res, lq="sp"):
    us = norm_pre(S, C, h_src, r0, ntile, gain, gbuf, res, lq=lq)
    norm_post(S, us, ident, ibuf, uT, uTbuf, res)


def phase_ffn(S, W, pre, l, h_in, h_out, ntok, ident, ibuf):
    nc = S.nc
    NB = ntok // 512
    with ExitStack() as st:
        C = Ctx(S, st)
        wg, wgb = C.sb([128, 8, DFF], BF16, "wg")
        wu, wub = C.sb([128, 8, DFF], BF16, "wu")
        wd, wdb = C.sb([128, NFF, D], BF16, "wd")
        gain, gb = C.sb([128, D], F32, "gain")
        uT, uTb = C.sb([128, 8, 512], BF16, "uT")
        aT, aTb = C.sb([128, NFF, 512], BF16, "aT")
        res = norm_res(C)
        hr = [C.sb([128, D], F32, "hr") for _ in range(2)]
        sg = [C.sb([128, 512], F32, "sg") for _ in range(2)]
        psg = [C.ps([128, 512], F32, "psg") for _ in range(2)]
        psu = [C.ps([128, 512], F32, "psu") for _ in range(2)]
        psd = [C.ps([128, 512], F32, "psd") for _ in range(2)]

        S.dma("sp", gain[:], bcast_row(W[pre + "_norm"][l], 128), writes=[gb])
        load_w_bf16(S, wg, wgb, W[pre + "_w_gate"][l], 8, DFF)
        load_w_bf16(S, wu, wub, W[pre + "_w_up"][l], 8, DFF)
        load_w_bf16(S, wd, wdb, W[pre + "_w_down"][l], NFF, D)

        def npre(b):
            return norm_pre(S, C, h_in, b * 512, 4, gain, gb, res)

        def npost(us):
            norm_post(S, us, ident, ibuf, uT, uTb, res)

        def gateup(b):
            for f in range(NFF):
                pg, pgb = psg[f % 2]
                pu, pub = psu[f % 2]
                sgt, sgb = sg[f % 2]
                for k in range(8):
                    S.op("pe", lambda e, k=k: e.matmul(pg[:], lhsT=wg[:, k, f * 128:(f + 1) * 128], rhs=uT[:, k, :],
                                                       start=(k == 0), stop=(k == 7)),
                         reads=[wgb, uTb], writes=[pgb])
                for k in range(8):
                    S.op("pe", lambda e, k=k: e.matmul(pu[:], lhsT=wu[:, k, f * 128:(f + 1) * 128], rhs=uT[:, k, :],
                                                       start=(k == 0), stop=(k == 7)),
                         reads=[wub, uTb], writes=[pub])
                S.op("act", lambda e: e.activation(out=sgt[:], in_=pg[:], func=AF.Silu), reads=[pgb], writes=[sgb])
                S.op("dve", lambda e: e.tensor_tensor(out=aT[:, f, :], in0=sgt[:], in1=pu[:], op=ALU.mult),
                     reads=[sgb, pub], writes=[aTb])

        def down(b):
            for i in range(4):
                ht, hb = hr[i % 2]
                rows = slice(b * 512 + i * 128, b * 512 + (i + 1) * 128)
                S.dma("sp", ht[:], h_in[rows, :], writes=[hb])
                for n in range(2):
                    pd, pdb = psd[n]
                    for f in range(NFF):
                        S.op("pe", lambda e, f=f: e.matmul(pd[:], lhsT=aT[:, f, i * 128:(i + 1) * 128],
                                                           rhs=wd[:, f, n * 512:(n + 1) * 512],
                                                           start=(f == 0), stop=(f == NFF - 1)),
                             reads=[aTb, wdb], writes=[pdb])
                    S.op("dve", lambda e: e.scalar_tensor_tensor(out=ht[:, n * 512:(n + 1) * 512], in0=pd[:],
                                                                 scalar=0.5, in1=ht[:, n * 512:(n + 1) * 512],
                                                                 op0=ALU.mult, op1=ALU.add),
                         reads=[pdb, hb], writes=[hb])
                S.dma("pool", h_out[rows, :], ht[:], reads=[hb])

        npost(npre(0))
        for b in range(NB):
            us_next = npre(b + 1) if b + 1 < NB else None
            gateup(b)
            if us_next is not None:
                npost(us_next)
            down(b)
        S.barrier()


NEG = -30000.0
NG = 4608
DIL = ((128, 1), (512, 4), (2048, 16))


def t5_bucket_np(dist):
    dist = np.asarray(dist, dtype=np.int64)
    d = np.maximum(dist, 1).astype(np.float32)
    large = 16 + (np.log(d / np.float32(16)) / np.float32(math.log(2048 / 16)) * np.float32(16)).astype(np.int32)
    large = np.minimum(large, 31)
    return np.where(dist < 16, dist, large)


def host_consts():
    c = {}
    s = np.arange(128)
    c["c_tri"] = (s[:, None] <= s[None, :]).astype(np.float32)
    sel = np.zeros((128, 128), np.float32)
    sel[127, :] = 1
    c["c_sel"] = sel
    c["c_maskT"] = c["c_tri"] * np.float32(0.125)
    c["c_jrev"] = np.ascontiguousarray(np.eye(128, dtype=np.float32)[::-1])
    oh = np.zeros((33, NG), np.float32)
    for p, (w, d) in enumerate(DIL):
        jx = np.arange(384)
        dl = jx - 127
        valid = (dl >= 0) & (dl <= 128)
        bk = t5_bucket_np(np.clip(dl, 0, 128) * d)
        cols = p * 384 + jx
        oh[bk[valid], cols[valid]] = 1
        oh[32, cols[~valid]] = NEG
    jx = np.arange(NG - 1152)
    dl = jx - 511
    valid = dl >= 0
    bk = t5_bucket_np(np.clip(dl, 0, None))
    cols = 1152 + jx
    oh[bk[valid], cols[valid]] = 1
    oh[32, cols[~valid]] = NEG
    c["c_oh"] = oh
    wins = [2, 4, 8, 16]
    invc = np.zeros((128, 2, 2, 512), np.float32)
    t = np.arange(512)
    for cc in range(2):
        for half in range(2):
            w = wins[cc * 2 + half]
            rows = slice(half * 64, half * 64 + 64)
            invc[rows, 1, cc, :] = 1.0 / w
            invc[rows, 0, cc, :] = 1.0 / np.minimum(t + 1, w)
    c["c_invc"] = invc
    return c


CONST_SHAPES = {"c_tri": (128, 128), "c_sel": (128, 128), "c_maskT": (128, 128), "c_jrev": (128, 128),
                "c_oh": (33, NG), "c_invc": (128, 2, 2, 512)}


class Rot:
    def __init__(self, items):
        self.items = items
        self.i = 0

    def next(self):
        x = self.items[self.i % len(self.items)]
        self.i += 1
        return x


def col1(ap1d):
    return ap1d.rearrange("(p o) -> p o", o=1)


def load_tok(S, dst, dbuf, src, nch):
    for n0 in range(0, nch, 8):
        n1 = min(nch, n0 + 8)
        S.dma("sp", dst[:, n0:n1, :], src[n0 * 128:n1 * 128, :].rearrange("(n p) c -> p n c", p=128), writes=[dbuf])


def setup_consts(S, C, W, CS, SC):
    nc = S.nc
    idf, idfb = C.sb([128, 128], F32, "identf")
    ident, ibuf = C.sb([128, 128], BF16, "ident")
    jrev, jb = C.sb([128, 128], BF16, "jrev")
    S.op("pool", lambda e: e.memset(idf[:], 1.0), writes=[idfb])
    S.op("pool", lambda e: e.affine_select(out=idf[:], in_=idf[:], pattern=[[-1, 128]], compare_op=ALU.is_equal,
                                           fill=0.0, base=0, channel_multiplier=1), reads=[idfb], writes=[idfb])
    S.op("dve", lambda e: e.tensor_copy(out=ident[:], in_=idf[:]), reads=[idfb], writes=[ibuf])
    S.dma("pool", jrev[:], CS["c_jrev"], writes=[jb])
    with ExitStack() as st:
        C2 = Ctx(S, st)
        t5x, t5b = C2.sb([33, 8], F32, "t5x")
        oh, ohb = C2.sb([33, NG], F32, "oh")
        gsb, gsbb = C2.sb([8, NG], BF16, "gsb")
        pg = [C2.ps([128, 512], F32, "pgv") for _ in range(2)]
        S.op("dve", lambda e: e.memset(t5x[:], 1.0), writes=[t5b])
        S.dma("sp", t5x[0:32, :], W["t5_bias"], writes=[t5b])
        S.dma("sp", oh[:], CS["c_oh"], writes=[ohb])
        for n in range(NG // 512):
            p, pb = pg[n % 2]
            S.op("pe", lambda e: e.matmul(p[0:8, :], lhsT=t5x[:], rhs=oh[:, n * 512:(n + 1) * 512], start=True, stop=True),
                 reads=[t5b, ohb], writes=[pb])
            S.op("act", lambda e: e.copy(out=gsb[:, n * 512:(n + 1) * 512], in_=p[0:8, :]), reads=[pb], writes=[gsbb])
        S.dma("sp", SC["gvec"], gsb[:], reads=[gsbb])
        S.barrier()
    return ident, ibuf, jrev, jb


def norm_res(C):
    return {
        "hn": [C.sb([128, D], F32, "hn") for _ in range(4)],
        "u": [C.sb([128, D], BF16, "u") for _ in range(4)],
        "ss": [C.sb([128, 16], F32, "ss") for _ in range(2)],
        "ptp": [C.ps([128, 1024], BF16, "ptp") for _ in range(2)],
    }


def phase_mixproj(S, W, CS, SC, l, NSEQ, T, h, ident, ibuf):
    NB = T // 512
    with ExitStack() as st:
        C = Ctx(S, st)
        win, winb = C.sb([128, 8, INW], BF16, "win")
        gain, gb = C.sb([128, D], F32, "gain")
        uTs = [C.sb([128, 8, 512], BF16, "uT") for _ in range(2)]
        res = norm_res(C)
        cw, cwb = C.sb([128, 4, 4], F32, "cw")
        cbias, cbb = C.sb([128, 4], F32, "cb")
        gateb, gtb = C.sb([128, 8], F32, "gateb")
        pscale, pscb = C.sb([128, 256], F32, "pscale")
        wblkf, wfb = C.sb([128, 2, 128], F32, "wblkf")
        wblk, wkb = C.sb([128, 2, 128], BF16, "wblk")
        invc, invb = C.sb([128, 2, 2, 512], F32, "invc")
        tri, trib = C.sb([128, 128], BF16, "tri")
        sel, selb = C.sb([128, 128], BF16, "sel")
        gbr = Rot([C.sb([128, 4, 16], BF16, "gbb") for _ in range(2)])
        Xm = [C.sb([128, 515], F32, "Xm") for _ in range(4)]
        Xp = [C.sb([128, 528], F32, "Xp") for _ in range(2)]
        acc = Rot([C.sb([128, 512], F32, "acc") for _ in range(2)])
        stg = Rot([C.sb([128, 512], BF16, "stg") for _ in range(4)])
        ksg = [C.sb([128, 512], BF16, "ksg") for _ in range(2)]
        ssum = [C.sb([128, 528], F32, "ssum") for _ in range(4)]
        ptmp = Rot([C.sb([128, 512], F32, "ptmp") for _ in range(2)])
        dmT = [C.sb([128, 512], BF16, "dmT") for _ in range(2)]
        mvst = Rot([C.sb([128, 4, 65], BF16, "mvst") for _ in range(2)])
        dvst = Rot([C.sb([128, 4, 65], BF16, "dvst") for _ in range(2)])
        fvst = Rot([C.sb([128, 4, 65], BF16, "fvst") for _ in range(2)])
        ogst = Rot([C.sb([128, 256], BF16, "ogst") for _ in range(2)])
        kgst = Rot([C.sb([128, 256], BF16, "kgst") for _ in range(2)])
        ybst = Rot([C.sb([128, 256], BF16, "ybst") for _ in range(2)])
        agst = Rot([C.sb([128, 4, 8], F32, "agst") for _ in range(2)])
        ebst = Rot([C.sb([128, 4, 4], F32, "ebst") for _ in range(2)])
        gsr = Rot([C.sb([128, 4, 32], F32, "gs") for _ in range(2)])
        pgen = Rot([C.ps([128, 512], F32, "pgen") for _ in range(4)])
        pgp = C.ps([128, 512], F32, "pgp")
        ptk = C.ps([128, 1024], BF16, "ptk")

        load_w_bf16(S, win, winb, W["w_in"][l], 8, INW)
        S.dma("sp", gain[:], bcast_row(W["mix_norm"][l], 128), writes=[gb])
        S.dma("sp", gateb[:], bcast_row(W["mlstm_gate_b"][l], 128), writes=[gtb])
        S.dma("sp", pscale[:], bcast_row(W["pool_scale"][l], 128), writes=[pscb])
        for c in range(4):
            for j in range(4):
                S.dma("sp", cw[:, c, j:j + 1], col1(W["mlstm_conv_w"][l, j, c * 128:(c + 1) * 128]), writes=[cwb])
            S.dma("sp", cbias[:, c:c + 1], col1(W["mlstm_conv_b"][l, c * 128:(c + 1) * 128]), writes=[cbb])
        S.op("dve", lambda e: e.memset(wblkf[:], 0.0), writes=[wfb])
        for g in range(4):
            r0 = (g % 2) * 64
            S.dma("sp", wblkf[r0:r0 + 64, g // 2, r0:r0 + 64], W["pool_w"][l, g], writes=[wfb])
        S.op("dve", lambda e: e.tensor_copy(out=wblk[:], in_=wblkf[:]), reads=[wfb], writes=[wkb])
        S.dma("sp", invc[:], CS["c_invc"], writes=[invb])
        S.dma("pool", tri[:], CS["c_tri"], writes=[trib])
        S.dma("pool", sel[:], CS["c_sel"], writes=[selb])
        for r in (mvst, dvst, fvst):
            for t_, b_ in r.items:
                S.op("dve", lambda e: e.memset(t_[:], 1.0), writes=[b_])

        if MIXSTAGE <= 0:
            S.barrier()
            return
        fm_specs = [("m", 0, 0), ("m", 1, 128), ("m", 2, 256), ("m", 3, 384), ("p", 0, 1032), ("p", 1, 1160),
                    ("d", 4, 1288), ("d", 5, 1416), ("d", 6, 1544), ("d", 7, 1672),
                    ("d", 8, 2056), ("d", 9, 2184), ("d", 10, 2312), ("d", 11, 2440)]
        dscale = {4: 0.125, 5: 0.125, 6: 1.0, 7: 1.0, 8: 32 ** -0.5, 9: 32 ** -0.5, 10: 1.0, 11: 1.0}

        blocks = [(sq, bb) for sq in range(NSEQ) for bb in range(NB)]
        norm_block(S, C, h, 0, 4, gain, gb, ident, ibuf, uTs[0][0], uTs[0][1], res, lq="pool")
        for bi, (seq, b) in enumerate(blocks):
            if True:
                uT, uTb = uTs[bi % 2]
                us_next = None
                if bi + 1 < len(blocks):
                    nsq, nbb = blocks[bi + 1]
                    us_next = norm_pre(S, C, h, nsq * T + nbb * 512, 4, gain, gb, res, lq="pool")
                tok0 = seq * T + b * 512
                tsl = slice(b * 512, (b + 1) * 512)
                if b == 0:
                    for c in range(4):
                        S.op("dve", lambda e: e.memset(Xm[c][0][:, 0:3], 0.0), writes=[Xm[c][1]])
                    for c in range(2):
                        S.op("dve", lambda e: e.memset(Xp[c][0][:, 0:16], 0.0), writes=[Xp[c][1]])
                gs, _ = gsr.next()
                gbt, _ = gbr.next()
                pg, _ = pgp
                gsbs = [Buf() for _ in range(4)]
                gbbs = [Buf() for _ in range(4)]
                pgbs = [pgp[1]] * 4
                agts = [agst.next()]

                gsb_, gbtb, pgb = gsbs[0], gbbs[0], pgbs[0]
                ag_t, ag_b = agts[0]
                PV4 = pg[:, 0:64].rearrange("p (i c) -> p i c", c=16)

                def gateA():
                    for i in range(4):
                        tl = slice(i * 128, (i + 1) * 128)
                        for k in range(8):
                            S.op("pe", lambda e: e.matmul(pg[:, i * 16:i * 16 + 8], lhsT=uT[:, k, tl], rhs=win[:, k, 1024:1032],
                                                          start=(k == 0), stop=(k == 7)), reads=[uTb, winb], writes=[pgb])
                    S.op("dve", lambda e: e.tensor_tensor(out=gs[:, :, 0:8], in0=PV4[:, :, 0:8],
                                                          in1=gateb[:].unsqueeze(1).broadcast_to([128, 4, 8]), op=ALU.add),
                         reads=[pgb, gtb], writes=[gsb_])
                    S.op("act", lambda e: e.activation(out=gs[:, :, 8:12], in_=gs[:, :, 4:8], func=AF.Sigmoid),
                         reads=[gsb_], writes=[gsb_])
                    S.op("act", lambda e: e.activation(out=gs[:, :, 12:16], in_=gs[:, :, 8:12], func=AF.Ln),
                         reads=[gsb_], writes=[gsb_])
                    S.op("dve", lambda e: e.tensor_copy(out=gbt[:, :, 0:4], in_=gs[:, :, 12:16]), reads=[gsb_], writes=[gbtb])
                    S.op("dve", lambda e: e.tensor_copy(out=gs[:, :, 28:32], in_=gbt[:, :, 0:4]), reads=[gbtb], writes=[gsb_])
                    S.op("dve", lambda e: e.tensor_tensor(out=gbt[:, :, 4:8], in0=gs[:, :, 12:16], in1=gs[:, :, 28:32],
                                                          op=ALU.subtract), reads=[gsb_, gbtb], writes=[gbtb])

                def gateB():
                    for i in range(4):
                        S.op("pe", lambda e: e.matmul(pg[:, i * 16 + 8:i * 16 + 12], lhsT=tri[:], rhs=gbt[:, i, 0:4],
                                                      start=True, stop=False), reads=[trib, gbtb], writes=[pgb])
                        S.op("pe", lambda e: e.matmul(pg[:, i * 16 + 8:i * 16 + 12], lhsT=tri[:], rhs=gbt[:, i, 4:8],
                                                      start=False, stop=True), reads=[trib, gbtb], writes=[pgb])
                    S.op("act", lambda e: e.copy(out=gs[:, :, 16:20], in_=PV4[:, :, 8:12]), reads=[pgb], writes=[gsb_])
                    S.op("act", lambda e: e.activation(out=ag_t[:, :, 0:4], in_=gs[:, :, 16:20], func=AF.Exp),
                         reads=[gsb_], writes=[ag_b])
                    S.op("dve", lambda e: e.tensor_tensor(out=gs[:, :, 20:24], in0=gs[:, :, 0:4], in1=gs[:, :, 16:20],
                                                          op=ALU.subtract), reads=[gsb_], writes=[gsb_])
                    S.op("act", lambda e: e.activation(out=ag_t[:, :, 4:8], in_=gs[:, :, 20:24], func=AF.Exp),
                         reads=[gsb_], writes=[ag_b])
                    S.op("dve", lambda e: e.tensor_scalar(out=gs[:, :, 24:28], in0=ag_t[:, :, 4:8], scalar1=0.125, scalar2=None,
                                                          op0=ALU.mult), reads=[ag_b], writes=[gsb_])
                    S.op("dve", lambda e: e.tensor_copy(out=gbt[:, :, 8:12], in_=gs[:, :, 16:20]), reads=[gsb_], writes=[gbtb])
                    S.op("dve", lambda e: e.tensor_copy(out=gs[:, :, 28:32], in_=gbt[:, :, 8:12]), reads=[gbtb], writes=[gsb_])
                    S.op("dve", lambda e: e.tensor_tensor(out=gbt[:, :, 12:16], in0=gs[:, :, 16:20], in1=gs[:, :, 28:32],
                                                          op=ALU.subtract), reads=[gsb_, gbtb], writes=[gbtb])

                def gateC():
                    for i in range(4):
                        S.op("pe", lambda e: e.matmul(pg[:, i * 16 + 12:i * 16 + 16], lhsT=sel[:], rhs=gbt[:, i, 8:12],
                                                      start=True, stop=False), reads=[selb, gbtb], writes=[pgb])
                        S.op("pe", lambda e: e.matmul(pg[:, i * 16 + 12:i * 16 + 16], lhsT=sel[:], rhs=gbt[:, i, 12:16],
                                                      start=False, stop=True), reads=[selb, gbtb], writes=[pgb])
                    eb_t, eb_b = ebst.next()
                    S.op("act", lambda e: e.activation(out=eb_t[:], in_=PV4[:, :, 12:16], func=AF.Exp),
                         reads=[pgb], writes=[eb_b])
                    S.dma("sp", SC["ag"][seq, :, b * 4:(b + 1) * 4, :], ag_t[:], reads=[ag_b])
                    S.dma("sp", SC["eb"][seq, :, b * 4:(b + 1) * 4, :], eb_t[:], reads=[eb_b])

                for fi, (kind, ci, col) in enumerate(fm_specs):
                    if fi == 0:
                        gateA()
                    elif fi == 4:
                        gateB()
                    elif fi == 8:
                        gateC()
                    p, pb = pgen.next()
                    for k in range(8):
                        S.op("pe", lambda e: e.matmul(p[:], lhsT=win[:, k, col:col + 128], rhs=uT[:, k, :],
                                                      start=(k == 0), stop=(k == 7)), reads=[winb, uTb], writes=[pb])
                    if kind == "m":
                        X, Xb = Xm[ci]
                        S.op("act", lambda e: e.copy(out=X[:, 3:515], in_=p[:]), reads=[pb], writes=[Xb])
                        a_, ab_ = acc.next()
                        S.op("dve", lambda e: e.tensor_scalar(out=a_[:], in0=X[:, 3:515], scalar1=cw[:, ci, 3:4],
                                                              scalar2=cbias[:, ci:ci + 1], op0=ALU.mult, op1=ALU.add),
                             reads=[Xb, cwb, cbb], writes=[ab_])
                        for j in range(3):
                            S.op("dve", lambda e: e.scalar_tensor_tensor(out=a_[:], in0=X[:, j:j + 512],
                                                                         scalar=cw[:, ci, j:j + 1], in1=a_[:],
                                                                         op0=ALU.mult, op1=ALU.add),
                                 reads=[Xb, cwb, ab_], writes=[ab_])
                        S.op("dve", lambda e: e.tensor_copy(out=X[:, 0:3], in_=X[:, 512:515]), reads=[Xb], writes=[Xb])
                        if ci < 2:
                            s_, sb_ = stg.next()
                        else:
                            s_, sb_ = ksg[ci - 2]
                        S.op("act", lambda e: e.activation(out=s_[:], in_=a_[:], func=AF.Silu), reads=[ab_], writes=[sb_])
                        S.dma("sp", SC["qkT"][seq, ci * 128:(ci + 1) * 128, tsl], s_[:], reads=[sb_])
                        if ci >= 2:
                            pk, pkb = ptk
                            for i in range(4):
                                o0 = i * 256 + (ci - 2) * 128
                                S.op("pe", lambda e: e.transpose(out=pk[:, o0:o0 + 128], in_=s_[:, i * 128:(i + 1) * 128],
                                                                 identity=ident[:]), reads=[sb_, ibuf], writes=[pkb])
                    elif kind == "p":
                        X, Xb = Xp[ci]
                        S.op("act", lambda e: e.copy(out=X[:, 16:528], in_=p[:]), reads=[pb], writes=[Xb])
                        prev, prevb = X, Xb
                        sh = 1
                        nlev = 2 if ci == 0 else 4
                        levels = []
                        for lev in range(nlev):
                            s_, sb_ = ssum[lev]
                            lo = 2 * sh - 1
                            S.op("dve", lambda e: e.tensor_tensor(out=s_[:, lo:528], in0=prev[:, lo:528],
                                                                  in1=prev[:, lo - sh:528 - sh], op=ALU.add),
                                 reads=[prevb], writes=[sb_])
                            levels.append((s_, sb_))
                            prev, prevb = s_, sb_
                            sh *= 2
                        d_, db_ = dmT[ci]
                        for half in range(2):
                            s_, sb_ = levels[(0 if ci == 0 else 2) + half]
                            rs = slice(half * 64, half * 64 + 64)
                            if b == 0:
                                t_, tb_ = ptmp.next()
                                S.op("dve", lambda e: e.tensor_tensor(out=t_[rs, :], in0=s_[rs, 16:528],
                                                                      in1=invc[rs, 0, ci, :], op=ALU.mult),
                                     reads=[sb_, invb], writes=[tb_])
                                S.op("dve", lambda e: e.tensor_tensor(out=d_[rs, :], in0=t_[rs, :], in1=X[rs, 16:528],
                                                                      op=ALU.subtract), reads=[tb_, Xb], writes=[db_])
                            else:
                                S.op("dve", lambda e: e.scalar_tensor_tensor(out=d_[rs, :], in0=s_[rs, 16:528],
                                                                             scalar=invc[rs, 1, ci, 0:1], in1=X[rs, 16:528],
                                                                             op0=ALU.mult, op1=ALU.subtract),
                                     reads=[sb_, invb, Xb], writes=[db_])
                        S.op("dve", lambda e: e.tensor_copy(out=X[:, 0:16], in_=X[:, 512:528]), reads=[Xb], writes=[Xb])
                    else:
                        s_, sb_ = stg.next()
                        S.op("act", lambda e: e.activation(out=s_[:], in_=p[:], func=AF.Copy, scale=float(dscale[ci])),
                             reads=[pb], writes=[sb_])
                        S.dma("sp", SC["qkT"][seq, ci * 128:(ci + 1) * 128, tsl], s_[:], reads=[sb_])
                for i in range(4):
                    if i == 2 and us_next is not None:
                        norm_post(S, us_next, ident, ibuf, uTs[(bi + 1) % 2][0], uTs[(bi + 1) % 2][1], res)
                    tl = slice(i * 128, (i + 1) * 128)
                    rows = slice(b * 512 + i * 128, b * 512 + (i + 1) * 128)
                    G = gs[:, i, :]
                    k_, kb_ = kgst.next()
                    pk, pkb = ptk
                    S.op("dve", lambda e: e.tensor_tensor(
                        out=k_[:].rearrange("p (h d) -> p h d", h=4),
                        in0=pk[:, i * 256:(i + 1) * 256].rearrange("p (h d) -> p h d", h=4),
                        in1=G[:, 24:28].unsqueeze(2).broadcast_to([128, 4, 64]), op=ALU.mult),
                        reads=[pkb, gsbs[0]], writes=[kb_])
                    S.dma("sp", SC["kg"][seq, rows, :], k_[:], reads=[kb_])
                    pp, ppb = pgen.next()
                    for cc in range(2):
                        S.op("pe", lambda e: e.matmul(pp[:, cc * 128:(cc + 1) * 128], lhsT=dmT[cc][0][:, tl],
                                                      rhs=wblk[:, cc, :], start=True, stop=True),
                             reads=[dmT[cc][1], wkb], writes=[ppb])
                    y_, yb_ = ybst.next()
                    S.op("dve", lambda e: e.tensor_tensor(out=y_[:], in0=pp[:, 0:256], in1=pscale[:], op=ALU.mult),
                         reads=[ppb, pscb], writes=[yb_])
                    S.dma("sp", SC["y"][seq, rows, 256:512], y_[:], reads=[yb_])
                    p1, p1b = pgen.next()
                    for k in range(8):
                        S.op("pe", lambda e: e.matmul(p1[:], lhsT=uT[:, k, tl], rhs=win[:, k, 512:1024],
                                                      start=(k == 0), stop=(k == 7)), reads=[uTb, winb], writes=[p1b])
                    v_, vb_ = mvst.next()
                    S.op("act", lambda e: e.copy(out=v_[:, :, 0:64], in_=p1[:, 0:256].rearrange("p (h d) -> p h d", h=4)),
                         reads=[p1b], writes=[vb_])
                    o_, ob_ = ogst.next()
                    S.op("act", lambda e: e.activation(out=o_[:], in_=p1[:, 256:512], func=AF.Sigmoid),
                         reads=[p1b], writes=[ob_])
                    S.dma("sp", SC["mv1"][seq, rows, :], v_[:].rearrange("p h d -> p (h d)"), reads=[vb_])
                    S.dma("sp", SC["og"][seq, rows, :], o_[:], reads=[ob_])
                    p2, p2b = pgen.next()
                    for gi, col in ((0, 1800), (1, 2568)):
                        for k in range(8):
                            S.op("pe", lambda e: e.matmul(p2[:, gi * 256:(gi + 1) * 256], lhsT=uT[:, k, tl],
                                                          rhs=win[:, k, col:col + 256], start=(k == 0), stop=(k == 7)),
                                 reads=[uTb, winb], writes=[p2b])
                    dv_, dvb_ = dvst.next()
                    fv_, fvb_ = fvst.next()
                    S.op("act", lambda e: e.copy(out=dv_[:, :, 0:64], in_=p2[:, 0:256].rearrange("p (h d) -> p h d", h=4)),
                         reads=[p2b], writes=[dvb_])
                    S.op("act", lambda e: e.copy(out=fv_[:, :, 0:64], in_=p2[:, 256:512].rearrange("p (h d) -> p h d", h=4)),
                         reads=[p2b], writes=[fvb_])
                    S.dma("sp", SC["dv1"][seq, rows, :], dv_[:].rearrange("p h d -> p (h d)"), reads=[dvb_])
                    S.dma("sp", SC["fv1"][seq, rows, :], fv_[:].rearrange("p h d -> p (h d)"), reads=[fvb_])
        S.barrier()


def phase_mlstm(S, W, CS, SC, l, seq, T):
    NCH = T // 128
    with ExitStack() as st:
        C = Ctx(S, st)
        qk, qkb = C.sb([128, 4, T], BF16, "mqk")
        v1, v1b = C.sb([128, NCH, 260], BF16, "mv1")
        kg, kgb = C.sb([128, NCH, 256], BF16, "mkg")
        og, ogb = C.sb([128, NCH, 256], BF16, "mog")
        ag, agb = C.sb([128, NCH, 8], F32, "mag")
        eb, ebb = C.sb([128, NCH, 4], F32, "meb")
        ebp, ebpb = C.sb([128, NCH, 2], F32, "mebp")
        maskT, mkb = C.sb([128, 128], F32, "maskT")
        ng, ngb = C.sb([128, 256], F32, "ng")
        Cf, Cfb = C.sb([128, 2, 130], F32, "Cf")
        Cb, Cbb = C.sb([128, 2, 130], BF16, "Cb")
        tmpU = Rot([C.sb([128, 2, 130], F32, "tmpU") for _ in range(2)])
        PT = Rot([C.sb([128, 128], BF16, "PT") for _ in range(4)])
        nd = Rot([C.sb([128, 2, 4, 65], F32, "nd") for _ in range(2)])
        hh = Rot([C.sb([128, 2, 4, 64], F32, "hh") for _ in range(2)])
        sq = Rot([C.sb([128, 2, 4, 64], F32, "sq") for _ in range(2)])
        stt = Rot([C.sb([128, 2, 32], F32, "stt") for _ in range(2)])
        yst = Rot([C.sb([128, 2, 256], BF16, "yst") for _ in range(2)])
        psc = Rot([C.ps([128, 512], F32, "psc") for _ in range(2)])
        pU = Rot([C.ps([128, 512], F32, "pU") for _ in range(2)])
        po = Rot([C.ps([128, 2, 512], F32, "po") for _ in range(2)])

        for c in range(4):
            S.dma("sp", qk[:, c, :], SC["qkT"][seq, c * 128:(c + 1) * 128, :], writes=[qkb])
        load_tok(S, v1, v1b, SC["mv1"][seq], NCH)
        load_tok(S, kg, kgb, SC["kg"][seq], NCH)
        load_tok(S, og, ogb, SC["og"][seq], NCH)
        S.dma("sp", ag[:], SC["ag"][seq], writes=[agb])
        S.dma("sp", eb[:], SC["eb"][seq], writes=[ebb])
        S.dma("sp", maskT[:], CS["c_maskT"], writes=[mkb])
        S.dma("sp", ng[:], bcast_row(W["mlstm_norm"][l], 128), writes=[ngb])
        for pr in range(2):
            S.op("dve", lambda e: e.tensor_copy(out=ebp[0:64, :, pr], in_=eb[0:64, :, 2 * pr]), reads=[ebb], writes=[ebpb])
            S.op("dve", lambda e: e.tensor_copy(out=ebp[64:128, :, pr], in_=eb[64:128, :, 2 * pr + 1]),
                 reads=[ebb], writes=[ebpb])
        S.op("dve", lambda e: e.memset(Cf[:], 0.0), writes=[Cfb])
        S.op("dve", lambda e: e.memset(Cb[:], 0.0), writes=[Cbb])

        for c in range(NCH):
            if MLSTAGE <= 0:
                break
            cols = slice(c * 128, (c + 1) * 128)
            psA, psB = psc.items
            pts = []
            for hd in range(4):
                pr, hh_ = hd // 2, hd % 2
                rs = slice(hh_ * 64, hh_ * 64 + 64)
                ps_, psb_ = (psA, psB)[hh_]
                S.op("pe", lambda e: e.matmul(ps_[:, pr * 128:(pr + 1) * 128], lhsT=qk[rs, 2 + pr, cols], rhs=qk[rs, pr, cols],
                                              start=True, stop=True), reads=[qkb], writes=[psb_])
            if MLSTAGE <= 1:
                continue
            pu_, pub_ = pU.next()
            for pr in range(2):
                S.op("pe", lambda e: e.matmul(pu_[:, pr * 130:(pr + 1) * 130], lhsT=kg[:, c, pr * 128:(pr + 1) * 128],
                                              rhs=v1[:, c, pr * 130:(pr + 1) * 130], start=True, stop=True),
                     reads=[kgb, v1b], writes=[pub_])
            for hd in range(4):
                p_, pb_ = PT.next()
                ps_, psb_ = (psA, psB)[hd % 2]
                S.op("dve", lambda e: e.scalar_tensor_tensor(out=p_[:], in0=ps_[:, (hd // 2) * 128:(hd // 2 + 1) * 128],
                                                             scalar=ag[:, c, 4 + hd:5 + hd], in1=maskT[:],
                                                             op0=ALU.mult, op1=ALU.mult),
                     reads=[psb_, agb, mkb], writes=[pb_])
                pts.append((p_, pb_))
            if MLSTAGE <= 2:
                continue
            if c % 2 == 0:
                po2_, pob_ = po.next()
            po_ = po2_[:, c % 2, :]
            for pr in range(2):
                S.op("pe", lambda e: e.matmul(po_[:, pr * 130:(pr + 1) * 130], lhsT=qk[:, pr, cols], rhs=Cb[:, pr, :],
                                              start=True, stop=False), reads=[qkb, Cbb], writes=[pob_])
                for hh_ in range(2):
                    hd = 2 * pr + hh_
                    p_, pb_ = pts[hd]
                    S.op("pe", lambda e: e.matmul(po_[:, hd * 65:(hd + 1) * 65], lhsT=p_[:], rhs=v1[:, c, hd * 65:(hd + 1) * 65],
                                                  start=False, stop=True), reads=[pb_, v1b], writes=[pob_])
            if MLSTAGE <= 3:
                continue
            tu, tub = tmpU.next()
            for pr in range(2):
                S.op("act", lambda e: e.activation(out=tu[:, pr, :], in_=pu_[:, pr * 130:(pr + 1) * 130], func=AF.Identity,
                                                   scale=ebp[:, c, pr:pr + 1]), reads=[pub_, ebpb], writes=[tub])
                S.op("dve", lambda e: e.scalar_tensor_tensor(out=Cf[:, pr, :], in0=Cf[:, pr, :], scalar=ebp[:, c, pr:pr + 1],
                                                             in1=tu[:, pr, :], op0=ALU.mult, op1=ALU.add),
                     reads=[Cfb, ebpb, tub], writes=[Cfb])
                S.op("act", lambda e: e.copy(out=Cb[0:64, pr, 0:65], in_=Cf[0:64, pr, 0:65]), reads=[Cfb], writes=[Cbb])
                S.op("act", lambda e: e.copy(out=Cb[64:128, pr, 65:130], in_=Cf[64:128, pr, 65:130]), reads=[Cfb], writes=[Cbb])
            if MLSTAGE <= 4:
                continue
            if c % 2 == 0:
                continue
            c0 = c - 1
            n_, nb_ = nd.next()
            S.op("dve", lambda e: e.tensor_tensor(out=n_[:], in0=po2_[:, :, 0:260].rearrange("p n (h d) -> p n h d", h=4),
                                                  in1=ag[:, c0:c0 + 2, 0:4].unsqueeze(3).broadcast_to([128, 2, 4, 65]), op=ALU.mult),
                 reads=[pob_, agb], writes=[nb_])
            s_, sb_ = stt.next()
            S.op("act", lambda e: e.activation(out=s_[:, :, 0:4].unsqueeze(3), in_=n_[:, :, :, 64:65], func=AF.Abs),
                 reads=[nb_], writes=[sb_])
            S.op("dve", lambda e: e.tensor_scalar(out=s_[:, :, 0:4], in0=s_[:, :, 0:4], scalar1=1.0, scalar2=None,
                                                  op0=ALU.max), reads=[sb_], writes=[sb_])
            S.op("dve", lambda e: e.reciprocal(out=s_[:, :, 4:8], in_=s_[:, :, 0:4]), reads=[sb_], writes=[sb_])
            h_, hb_ = hh.next()
            S.op("dve", lambda e: e.tensor_tensor(out=h_[:], in0=n_[:, :, :, 0:64],
                                                  in1=s_[:, :, 4:8].unsqueeze(3).broadcast_to([128, 2, 4, 64]), op=ALU.mult),
                 reads=[nb_, sb_], writes=[hb_])
            q_, qb_ = sq.next()
            S.op("act", lambda e: e.activation(out=q_[:], in_=h_[:], func=AF.Square), reads=[hb_], writes=[qb_])
            S.op("dve", lambda e: e.tensor_reduce(out=s_[:, :, 8:12], in_=h_[:], axis=AX.X, op=ALU.add), reads=[hb_], writes=[sb_])
            S.op("dve", lambda e: e.tensor_reduce(out=s_[:, :, 12:16], in_=q_[:], axis=AX.X, op=ALU.add), reads=[qb_], writes=[sb_])
            S.op("dve", lambda e: e.tensor_scalar(out=s_[:, :, 16:20], in0=s_[:, :, 8:12], scalar1=1.0 / 64, scalar2=None,
                                                  op0=ALU.mult), reads=[sb_], writes=[sb_])
            S.op("dve", lambda e: e.tensor_tensor(out=s_[:, :, 20:24], in0=s_[:, :, 16:20], in1=s_[:, :, 16:20], op=ALU.mult),
                 reads=[sb_], writes=[sb_])
            S.op("dve", lambda e: e.scalar_tensor_tensor(out=s_[:, :, 24:28], in0=s_[:, :, 12:16], scalar=1.0 / 64,
                                                         in1=s_[:, :, 20:24], op0=ALU.mult, op1=ALU.subtract),
                 reads=[sb_], writes=[sb_])
            S.op("dve", lambda e: e.tensor_scalar(out=s_[:, :, 24:28], in0=s_[:, :, 24:28], scalar1=0.0, scalar2=RMS_EPS,
                                                  op0=ALU.max, op1=ALU.add), reads=[sb_], writes=[sb_])
            S.op("act", lambda e: e.activation(out=s_[:, :, 28:32], in_=s_[:, :, 24:28], func=AF.Sqrt), reads=[sb_], writes=[sb_])
            S.op("dve", lambda e: e.reciprocal(out=s_[:, :, 28:32], in_=s_[:, :, 28:32]), reads=[sb_], writes=[sb_])
            S.op("dve", lambda e: e.tensor_tensor(out=h_[:], in0=h_[:],
                                                  in1=s_[:, :, 16:20].unsqueeze(3).broadcast_to([128, 2, 4, 64]),
                                                  op=ALU.subtract), reads=[hb_, sb_], writes=[hb_])
            S.op("dve", lambda e: e.tensor_tensor(out=h_[:], in0=h_[:],
                                                  in1=s_[:, :, 28:32].unsqueeze(3).broadcast_to([128, 2, 4, 64]),
                                                  op=ALU.mult), reads=[hb_, sb_], writes=[hb_])
            hf = h_[:].rearrange("p n h d -> p n (h d)")
            S.op("dve", lambda e: e.tensor_tensor(out=hf, in0=hf, in1=ng[:].unsqueeze(1).broadcast_to([128, 2, 256]), op=ALU.mult),
                 reads=[hb_, ngb], writes=[hb_])
            y_, yb_ = yst.next()
            S.op("dve", lambda e: e.tensor_tensor(out=y_[:], in0=hf, in1=og[:, c0:c0 + 2, :], op=ALU.mult),
                 reads=[hb_, ogb], writes=[yb_])
            S.dma("pool", SC["y"][seq, c0 * 128:(c0 + 2) * 128, 0:256].rearrange("(n p) c -> p n c", p=128), y_[:], reads=[yb_])
        S.barrier()


def phase_dil(S, W, CS, SC, l, seq, T, jrev, jb):
    with ExitStack() as st:
        C = Ctx(S, st)
        qk, qkb = C.sb([128, 4, T], BF16, "dqk")
        Rd, Rdb = C.sb([128, 3, 4, 256], BF16, "Rd")
        vt = [C.sb([128, 260], BF16, "dvt") for _ in range(6)]
        PT = Rot([C.sb([128, 256], BF16, "dPT") for _ in range(6)])
        PTr = Rot([C.sb([128, 256], BF16, "dPTr") for _ in range(4)])
        Ed, Edb = C.sb([128, 3, 4, 256], BF16, "Ed")
        ost = Rot([C.sb([128, 260], F32, "dost") for _ in range(3)])
        pscs = [Rot([C.ps([128, 512], F32, "dpsc") for _ in range(2)]) for _ in range(2)]
        po = Rot([C.ps([128, 512], F32, "dpo") for _ in range(2)])
        for c in range(4):
            S.dma("sp", qk[:, c, :], SC["qkT"][seq, (4 + c) * 128:(5 + c) * 128, :], writes=[qkb])
        gv = SC["gvec"]
        for p in range(3):
            for hd in range(4):
                src = bass.AP(gv.tensor, gv.offset + hd * NG + p * 384, [[1, 128], [1, 256]])
                S.dma("sp", Rd[:, p, hd, :], src, writes=[Rdb])
        for p in range(3):
            for hd in range(4):
                ps_, psb_ = pscs[hd % 2].next()
                S.op("pe", lambda e: e.matmul(ps_[:, 0:256], lhsT=jrev[:], rhs=Rd[:, p, hd, :], start=True, stop=True),
                     reads=[jb, Rdb], writes=[psb_])
                S.op("act", lambda e: e.activation(out=Ed[:, p, hd, :], in_=ps_[:, 0:256], func=AF.Exp),
                     reads=[psb_], writes=[Edb])
        for p, (w, d) in enumerate(DIL):
            L = T // d
            ntl = L // 128
            for r in range(d):
                grp = {}

                def dscores(i, hd):
                    t0 = r + d * 128 * i
                    ks = [i, i - 1] if i > 0 else [i]
                    nk = len(ks)
                    if hd == 0:
                        for ii in ([0, 1, 2] if i == 0 else [i + 2]):
                            if ii < ntl:
                                v_, vb_ = vt[ii % 6]
                                tt = r + d * 128 * ii
                                S.dma("sp", v_[:], SC["dv1"][seq, tt:tt + d * 127 + 1:d, :], writes=[vb_])
                    ch, rs = hd // 2, slice((hd % 2) * 64, (hd % 2) * 64 + 64)
                    ps_, psb_ = pscs[hd % 2].next()
                    qsl = qk[rs, ch, t0:t0 + d * 127 + 1:d]
                    for jj, j in enumerate(ks):
                        k0 = r + d * 128 * j
                        S.op("pe", lambda e: e.matmul(ps_[:, jj * 128:(jj + 1) * 128], lhsT=qk[rs, 2 + ch, k0:k0 + d * 127 + 1:d],
                                                      rhs=qsl, start=True, stop=True), reads=[qkb], writes=[psb_])
                    pr_, prb_ = PTr.next()
                    S.op("act", lambda e: e.activation(out=pr_[:, 0:nk * 128], in_=ps_[:, 0:nk * 128], func=AF.Exp),
                         reads=[psb_], writes=[prb_])
                    p_, pb_ = PT.next()
                    S.op("dve", lambda e: e.tensor_tensor(out=p_[:, 0:nk * 128], in0=pr_[:, 0:nk * 128],
                                                          in1=Ed[:, p, hd, 0:nk * 128], op=ALU.mult),
                         reads=[prb_, Edb], writes=[pb_])
                    return p_, pb_

                def dpv(i, hd, p_, pb_):
                    t0 = r + d * 128 * i
                    ks = [i, i - 1] if i > 0 else [i]
                    nk = len(ks)
                    if hd == 0:
                        grp[i] = po.next()
                    po_, pob_ = grp[i]
                    for jj, j in enumerate(ks):
                        vj, vjb = vt[j % 6]
                        S.op("pe", lambda e: e.matmul(po_[:, hd * 65:(hd + 1) * 65], lhsT=p_[:, jj * 128:(jj + 1) * 128],
                                                      rhs=vj[:, hd * 65:(hd + 1) * 65], start=(jj == 0), stop=(jj == nk - 1)),
                             reads=[pb_, vjb], writes=[pob_])
                    if hd == 3:
                        o_, ob_ = ost.next()
                        S.op("dve", lambda e: e.tensor_copy(out=o_[:], in_=po_[:, 0:260]), reads=[pob_], writes=[ob_])
                        S.dma("pool", SC["dacc"][seq, p, t0:t0 + d * 127 + 1:d, :], o_[:], reads=[ob_])
                        del grp[i]

                units = [(i, hd) for i in range(ntl) for hd in range(4)]
                LA = 3
                pend = {}
                nxt = 0
                for u in range(len(units)):
                    while nxt < len(units) and nxt <= u + LA:
                        pend[nxt] = dscores(*units[nxt])
                        nxt += 1
                    dpv(*units[u], *pend.pop(u))
        S.barrier()
        ld = Rot([C.sb([128, 3, 260], F32, "dld") for _ in range(2)])
        yst = Rot([C.sb([128, 256], BF16, "dyst") for _ in range(2)])
        rd = Rot([C.sb([128, 4], F32, "drd") for _ in range(2)])
        for n in range(T // 128):
            rows = slice(n * 128, (n + 1) * 128)
            a_, ab_ = ld.next()
            S.dma("sp", a_[:], SC["dacc"][seq, :, rows, :].rearrange("t p c -> p t c"), writes=[ab_])
            S.op("dve", lambda e: e.tensor_tensor(out=a_[:, 0, :], in0=a_[:, 0, :], in1=a_[:, 1, :], op=ALU.add),
                 reads=[ab_], writes=[ab_])
            S.op("dve", lambda e: e.tensor_tensor(out=a_[:, 0, :], in0=a_[:, 0, :], in1=a_[:, 2, :], op=ALU.add),
                 reads=[ab_], writes=[ab_])
            r_, rb_ = rd.next()
            av = a_[:, 0, :].rearrange("p (h d) -> p h d", h=4)
            S.op("dve", lambda e: e.reciprocal(out=r_[:].unsqueeze(2), in_=av[:, :, 64:65]), reads=[ab_], writes=[rb_])
            y_, yb_ = yst.next()
            S.op("dve", lambda e: e.tensor_tensor(out=y_[:].rearrange("p (h d) -> p h d", h=4), in0=av[:, :, 0:64],
                                                  in1=r_[:].unsqueeze(2).broadcast_to([128, 4, 64]), op=ALU.mult),
                 reads=[ab_, rb_], writes=[yb_])
            S.dma("pool", SC["y"][seq, rows, 512:768], y_[:], reads=[yb_])
        S.barrier()


def phase_diff(S, W, CS, SC, l, seq, T, jrev, jb):
    NCH = T // 128
    lam_init = 0.8 - 0.6 * math.exp(-0.3 * l)
    with ExitStack() as st:
        C = Ctx(S, st)
        qk, qkb = C.sb([64, 8, T], BF16, "fqk")
        v1, v1b = C.sb([128, NCH, 260], BF16, "fv1")
        Rf, Rfb = C.sb([128, 4, 3072], BF16, "Rf")
        Ef, Efb = C.sb([128, 4, 3072], BF16, "Ef")
        PTr = Rot([C.sb([128, 2, 512], BF16, "fPTr") for _ in range(3)])
        lv, lvb = C.sb([1, 160], F32, "lv")
        ones1, o1b = C.sb([1, 128], F32, "ones1")
        nlam, nlb = C.sb([128, 1], F32, "nlam")
        subg, sgb = C.sb([128, 64], F32, "subg")
        zl, zlb = C.sb([1, 128], BF16, "zl")
        zr, zrb = C.sb([1, 260], BF16, "zr")
        PT = Rot([C.sb([128, 2, 512], BF16, "fPT") for _ in range(5)])
        om = [C.sb([128, 4, 65], F32, "om") for _ in range(2)]
        om2 = [C.sb([128, 4, 64], F32, "om2") for _ in range(2)]
        rdn = Rot([C.sb([128, 8], F32, "rdn") for _ in range(4)])
        od = Rot([C.sb([128, 4, 64], F32, "od") for _ in range(2)])
        sq = Rot([C.sb([128, 4, 64], F32, "fsq") for _ in range(2)])
        yst = Rot([C.sb([128, 4, 64], BF16, "fyst") for _ in range(2)])
        psc2 = Rot([C.ps([128, 2, 512], F32, "fpsc2") for _ in range(2)])
        pscs = [Rot([(t_[:, mm, :], b_) for (t_, b_) in psc2.items]) for mm in range(2)]
        po = Rot([C.ps([128, 512], F32, "fpo") for _ in range(4)])
        plam = po.items[0]

        for c in range(8):
            S.dma("sp", qk[:, c, :], SC["qkT"][seq, 1024 + c * 64:1024 + (c + 1) * 64, :], writes=[qkb])
        load_tok(S, v1, v1b, SC["fv1"][seq], NCH)
        gv = SC["gvec"]
        for hd in range(4):
            for part in range(2):
                src = bass.AP(gv.tensor, gv.offset + (4 + hd) * NG + 1152 + part * 1536, [[1, 128], [1, 1536]])
                S.dma("sp", Rf[:, hd, part * 1536:(part + 1) * 1536], src, writes=[Rfb])
        for hd in range(4):
            for n in range(6):
                pt2_, psb_ = psc2.next()
                S.op("pe", lambda e: e.matmul(pt2_[:, 0, :], lhsT=jrev[:], rhs=Rf[:, hd, n * 512:(n + 1) * 512], start=True, stop=True),
                     reads=[jb, Rfb], writes=[psb_])
                S.op("act", lambda e: e.activation(out=Ef[:, hd, n * 512:(n + 1) * 512], in_=pt2_[:, 0, :], func=AF.Exp),
                     reads=[psb_], writes=[Efb])
        S.dma("sp", lv[:, 0:128], W["diff_lambda"][l].rearrange("(o a) b -> o (a b)", o=1), writes=[lvb])
        S.op("dve", lambda e: e.tensor_tensor(out=lv[:, 128:160], in0=lv[:, 0:32], in1=lv[:, 32:64], op=ALU.mult),
             reads=[lvb], writes=[lvb])
        S.op("dve", lambda e: e.tensor_reduce(out=lv[:, 0:1], in_=lv[:, 128:160], axis=AX.X, op=ALU.add), reads=[lvb], writes=[lvb])
        S.op("dve", lambda e: e.tensor_tensor(out=lv[:, 128:160], in0=lv[:, 64:96], in1=lv[:, 96:128], op=ALU.mult),
             reads=[lvb], writes=[lvb])
        S.op("dve", lambda e: e.tensor_reduce(out=lv[:, 1:2], in_=lv[:, 128:160], axis=AX.X, op=ALU.add), reads=[lvb], writes=[lvb])
        S.op("act", lambda e: e.activation(out=lv[:, 2:4], in_=lv[:, 0:2], func=AF.Exp), reads=[lvb], writes=[lvb])
        S.op("dve", lambda e: e.scalar_tensor_tensor(out=lv[:, 4:5], in0=lv[:, 3:4], scalar=-float(lam_init), in1=lv[:, 2:3],
                                                     op0=ALU.add, op1=ALU.subtract), reads=[lvb], writes=[lvb])
        S.op("dve", lambda e: e.memset(ones1[:], 1.0), writes=[o1b])
        pl, plb = plam
        S.op("pe", lambda e: e.matmul(pl[:, 0:1], lhsT=ones1[:], rhs=lv[:, 4:5], start=True, stop=True),
             reads=[o1b, lvb], writes=[plb])
        S.op("act", lambda e: e.copy(out=nlam[:], in_=pl[:, 0:1]), reads=[plb], writes=[nlb])
        S.dma("sp", subg[:], bcast_row(W["diff_subln"][l], 128), writes=[sgb])
        S.op("dve", lambda e: e.tensor_scalar(out=subg[:], in0=subg[:], scalar1=float(1.0 - lam_init), scalar2=None,
                                              op0=ALU.mult), reads=[sgb], writes=[sgb])
        S.op("dve", lambda e: e.memset(zl[:], 0.0), writes=[zlb])
        S.op("dve", lambda e: e.memset(zr[:], 0.0), writes=[zrb])

        for i in range(T // 512):
            for hd in range(4):
                pos = [po.next(), po.next()]
                for m in range(2):
                    S.op("pe", lambda e: e.matmul(pos[m][0][:, 0:260], lhsT=zl[:], rhs=zr[:], start=True, stop=False),
                         reads=[zlb, zrb], writes=[pos[m][1]])
                nk = 4 * i + 4

                def scores(j):
                    jj = j - 4 * i
                    q0 = max(jj, 0) * 128
                    o = 512 * i - 128 * j
                    x0 = min(o, 2176) + 384 + q0
                    pt2_, psb_ = psc2.next()
                    for m in range(2):
                        rs = slice(32 * m, 32 * m + 32)
                        S.op("pe", lambda e: e.matmul(pt2_[:, m, q0:512], lhsT=qk[rs, 4 + hd, j * 128:(j + 1) * 128],
                                                      rhs=qk[rs, hd, i * 512 + q0:(i + 1) * 512], start=True, stop=True),
                             reads=[qkb], writes=[psb_])
                    pr_, prb_ = PTr.next()
                    S.op("act", lambda e: e.activation(out=pr_[:, :, q0:512], in_=pt2_[:, :, q0:512], func=AF.Exp),
                         reads=[psb_], writes=[prb_])
                    p_, pb_ = PT.next()
                    S.op("dve", lambda e: e.tensor_tensor(out=p_[:, :, q0:512], in0=pr_[:, :, q0:512],
                                                          in1=Ef[:, hd, x0:x0 + 512 - q0].unsqueeze(1).broadcast_to([128, 2, 512 - q0]),
                                                          op=ALU.mult),
                         reads=[prb_, Efb], writes=[pb_])
                    outs = [(p_[:, 0, :], pb_), (p_[:, 1, :], pb_)]
                    return outs, q0

                def pv(j, outs, q0):
                    for m in range(2):
                        p_, pb_ = outs[m]
                        po_, pob_ = pos[m]
                        for s in range(q0 // 128, 4):
                            S.op("pe", lambda e: e.matmul(po_[:, s * 65:(s + 1) * 65], lhsT=p_[:, s * 128:(s + 1) * 128],
                                                          rhs=v1[:, j, hd * 65:(hd + 1) * 65], start=False,
                                                          stop=(j == 4 * i + s)), reads=[pb_, v1b], writes=[pob_])

                LA = 2
                pend = {}
                nxt = 0
                for j in range(nk):
                    while nxt < nk and nxt <= j + LA:
                        pend[nxt] = scores(nxt)
                        nxt += 1
                    pv(j, *pend.pop(j))
                for m in range(2):
                    po_, pob_ = pos[m]
                    o_, ob_ = om[m]
                    S.op("act", lambda e: e.copy(out=o_[:], in_=po_[:, 0:260].rearrange("p (s d) -> p s d", s=4)),
                         reads=[pob_], writes=[ob_])
                    r_, rb_ = rdn.next()
                    S.op("dve", lambda e: e.reciprocal(out=r_[:, 0:4].unsqueeze(2), in_=o_[:, :, 64:65]), reads=[ob_], writes=[rb_])
                    o2_, o2b_ = om2[m]
                    S.op("dve", lambda e: e.tensor_tensor(out=o2_[:], in0=o_[:, :, 0:64],
                                                          in1=r_[:, 0:4].unsqueeze(2).broadcast_to([128, 4, 64]), op=ALU.mult),
                         reads=[ob_, rb_], writes=[o2b_])
                d_, db_ = od.next()
                S.op("dve", lambda e: e.scalar_tensor_tensor(out=d_[:], in0=om2[1][0][:], scalar=nlam[:, 0:1], in1=om2[0][0][:],
                                                             op0=ALU.mult, op1=ALU.add),
                     reads=[om2[0][1], om2[1][1], nlb], writes=[db_])
                q_, qb_ = sq.next()
                S.op("act", lambda e: e.activation(out=q_[:], in_=d_[:], func=AF.Square), reads=[db_], writes=[qb_])
                r_, rb_ = rdn.next()
                S.op("dve", lambda e: e.tensor_reduce(out=r_[:, 0:4], in_=q_[:], axis=AX.X, op=ALU.add), reads=[qb_], writes=[rb_])
                S.op("dve", lambda e: e.tensor_scalar(out=r_[:, 0:4], in0=r_[:, 0:4], scalar1=1.0 / 64, scalar2=1e-5,
                                                      op0=ALU.mult, op1=ALU.add), reads=[rb_], writes=[rb_])
                S.op("act", lambda e: e.activation(out=r_[:, 4:8], in_=r_[:, 0:4], func=AF.Sqrt), reads=[rb_], writes=[rb_])
                S.op("dve", lambda e: e.reciprocal(out=r_[:, 4:8], in_=r_[:, 4:8]), reads=[rb_], writes=[rb_])
                S.op("dve", lambda e: e.tensor_tensor(out=d_[:], in0=d_[:], in1=r_[:, 4:8].unsqueeze(2).broadcast_to([128, 4, 64]),
                                                      op=ALU.mult), reads=[db_, rb_], writes=[db_])
                y_, yb_ = yst.next()
                S.op("dve", lambda e: e.tensor_tensor(out=y_[:], in0=d_[:], in1=subg[:].unsqueeze(1).broadcast_to([128, 4, 64]),
                                                      op=ALU.mult), reads=[db_, sgb], writes=[yb_])
                S.dma("pool", SC["y"][seq, i * 512:(i + 1) * 512, 768 + hd * 64:832 + hd * 64].rearrange("(s p) c -> p s c", p=128),
                      y_[:], reads=[yb_])
        S.barrier()


def phase_outproj(S, W, SC, l, NSEQ, T, h, ident, ibuf):
    with ExitStack() as st:
        C = Ctx(S, st)
        wo, wob = C.sb([128, 8, D], BF16, "wout")
        load_w_bf16(S, wo, wob, W["w_out"][l], 8, D)
        yt = Rot([C.sb([128, D], BF16, "yt") for _ in range(2)])
        yT = Rot([C.sb([128, 8, 128], BF16, "yT") for _ in range(2)])
        ht = Rot([C.sb([128, D], F32, "oht") for _ in range(3)])
        ptp = Rot([C.ps([128, 1024], BF16, "optp") for _ in range(2)])
        pso = Rot([C.ps([128, 512], F32, "opso") for _ in range(4)])
        for seq in range(NSEQ):
            for n in range(T // 128):
                rows = slice(n * 128, (n + 1) * 128)
                hrows = slice(seq * T + n * 128, seq * T + (n + 1) * 128)
                y_, yb_ = yt.next()
                S.dma("sp", y_[:], SC["y"][seq, rows, :], writes=[yb_])
                h_, hb_ = ht.next()
                S.dma("sp", h_[:], h[hrows, :], writes=[hb_])
                pt, ptb = ptp.next()
                for c in range(8):
                    S.op("pe", lambda e: e.transpose(out=pt[:, c * 128:(c + 1) * 128], in_=y_[:, c * 128:(c + 1) * 128],
                                                     identity=ident[:]), reads=[yb_, ibuf], writes=[ptb])
                yT_, yTb_ = yT.next()
                S.op("act", lambda e: e.copy(out=yT_[:, 0:4, :], in_=pt[:, 0:512].rearrange("p (c t) -> p c t", c=4)),
                     reads=[ptb], writes=[yTb_])
                S.op("dve", lambda e: e.tensor_copy(out=yT_[:, 4:8, :], in_=pt[:, 512:1024].rearrange("p (c t) -> p c t", c=4)),
                     reads=[ptb], writes=[yTb_])
                for nn in range(2):
                    p_, pb_ = pso.next()
                    for k in range(8):
                        S.op("pe", lambda e: e.matmul(p_[:], lhsT=yT_[:, k, :], rhs=wo[:, k, nn * 512:(nn + 1) * 512],
                                                      start=(k == 0), stop=(k == 7)), reads=[yTb_, wob], writes=[pb_])
                    S.op("dve", lambda e: e.tensor_tensor(out=h_[:, nn * 512:(nn + 1) * 512], in0=p_[:],
                                                          in1=h_[:, nn * 512:(nn + 1) * 512], op=ALU.add),
                         reads=[pb_, hb_], writes=[hb_])
                S.dma("pool", h[hrows, :], h_[:], reads=[hb_])
        S.barrier()


def phase_xattn(S, W, l, NSEQ, T, MEM, h, mem, ident, ibuf, xw):
    NB = T // 512
    NMT = MEM // 128
    with ExitStack() as st:
        C = Ctx(S, st)
        (wq, wqb), (wkv, wkvb), (wo, wob) = xw
        xg, xgb = C.sb([128, D], F32, "xg")
        mg, mgb = C.sb([128, D], F32, "mg")
        res = norm_res(C)
        uTs = [C.sb([128, 8, 512], BF16, "xuT") for _ in range(2)]
        mT, mTb = C.sb([128, 8, MEM], BF16, "mT")
        KT, KTb = C.sb([128, 8, MEM], BF16, "KT")
        V1, V1b = C.sb([128, NMT, 4, 257], BF16, "V1")
        qT, qTb = C.sb([128, 8, 512], BF16, "qT")
        PT = Rot([C.sb([128, 512], BF16, "xPT") for _ in range(6)])
        osb, osbb = C.sb([128, 4, D], BF16, "osb")
        oT, oTb = C.sb([128, 8, 512], BF16, "oT")
        rd = Rot([C.sb([128, 1], F32, "xrd") for _ in range(4)])
        ht = Rot([C.sb([128, D], F32, "xht") for _ in range(8)])
        oTbs = [Buf() for _ in range(4)]
        psq = Rot([C.ps([128, 512], F32, "psq") for _ in range(2)])
        pov = Rot([C.ps([128, 512], F32, "pov") for _ in range(4)])
        pwo = pov
        S.dma("sp", xg[:], bcast_row(W["xattn_norm"][l], 128), writes=[xgb])
        S.dma("sp", mg[:], bcast_row(W["mem_norm"][l], 128), writes=[mgb])
        S.op("dve", lambda e: e.memset(V1[:], 1.0), writes=[V1b])
        for seq in range(NSEQ):
            norm_block(S, C, mem, seq * MEM, NMT, mg, mgb, ident, ibuf, mT, mTb, res)
            for c in range(8):
                p_, pb_ = psq.next()
                for k in range(8):
                    S.op("pe", lambda e: e.matmul(p_[:, 0:MEM], lhsT=wkv[:, k, c * 128:(c + 1) * 128], rhs=mT[:, k, :],
                                                  start=(k == 0), stop=(k == 7)), reads=[wkvb, mTb], writes=[pb_])
                S.op("act", lambda e: e.copy(out=KT[:, c, :], in_=p_[:, 0:MEM]), reads=[pb_], writes=[KTb])
            for j in range(NMT):
                for nn in range(2):
                    p_, pb_ = psq.next()
                    for k in range(8):
                        S.op("pe", lambda e: e.matmul(p_[:], lhsT=mT[:, k, j * 128:(j + 1) * 128],
                                                      rhs=wkv[:, k, D + nn * 512:D + (nn + 1) * 512],
                                                      start=(k == 0), stop=(k == 7)), reads=[wkvb, mTb], writes=[pb_])
                    S.op("act", lambda e: e.copy(out=V1[:, j, 2 * nn:2 * nn + 2, 0:256],
                                                 in_=p_[:].rearrange("p (h d) -> p h d", h=2)),
                         reads=[pb_], writes=[V1b])
            norm_block(S, C, h, seq * T, 4, xg, xgb, ident, ibuf, uTs[0][0], uTs[0][1], res)
            for b in range(NB):
                uT, uTb = uTs[b % 2]
                tok0 = seq * T + b * 512
                hts = [ht.next() for _ in range(4)]
                for s4 in range(4):
                    S.dma("sp", hts[s4][0][:], h[tok0 + s4 * 128:tok0 + (s4 + 1) * 128, :], writes=[hts[s4][1]])
                for c in range(8):
                    p_, pb_ = psq.next()
                    for k in range(8):
                        S.op("pe", lambda e: e.matmul(p_[:], lhsT=wq[:, k, c * 128:(c + 1) * 128], rhs=uT[:, k, :],
                                                      start=(k == 0), stop=(k == 7)), reads=[wqb, uTb], writes=[pb_])
                    S.op("act", lambda e: e.activation(out=qT[:, c, :], in_=p_[:], func=AF.Copy, scale=1.0 / 16),
                         reads=[pb_], writes=[qTb])
                def xscores(hd):
                    pts = []
                    for j in range(NMT):
                        p_, pb_ = psq.next()
                        for cc in range(2):
                            S.op("pe", lambda e: e.matmul(p_[:], lhsT=KT[:, 2 * hd + cc, j * 128:(j + 1) * 128],
                                                          rhs=qT[:, 2 * hd + cc, :], start=(cc == 0), stop=(cc == 1)),
                                 reads=[KTb, qTb], writes=[pb_])
                        t_, tb_ = PT.next()
                        S.op("act", lambda e: e.activation(out=t_[:], in_=p_[:], func=AF.Exp), reads=[pb_], writes=[tb_])
                        pts.append((t_, tb_))
                    return pts

                def xpv(hd, pts):
                    for s in range(4):
                        p_, pb_ = pov.next()
                        for j in range(NMT):
                            t_, tb_ = pts[j]
                            S.op("pe", lambda e: e.matmul(p_[:, 0:257], lhsT=t_[:, s * 128:(s + 1) * 128], rhs=V1[:, j, hd, :],
                                                          start=(j == 0), stop=(j == NMT - 1)), reads=[tb_, V1b], writes=[pb_])
                        r_, rb_ = rd.next()
                        S.op("dve", lambda e: e.reciprocal(out=r_[:], in_=p_[:, 256:257]), reads=[pb_], writes=[rb_])
                        S.op("act", lambda e: e.activation(out=osb[:, s, hd * 256:(hd + 1) * 256], in_=p_[:, 0:256],
                                                           func=AF.Identity, scale=r_[:, 0:1]),
                             reads=[pb_, rb_], writes=[osbb])

                prevp = xscores(0)
                for hd in range(1, 4):
                    curp = xscores(hd)
                    xpv(hd - 1, prevp)
                    prevp = curp
                xpv(3, prevp)
                if b + 1 < NB:
                    norm_block(S, C, h, tok0 + 512, 4, xg, xgb, ident, ibuf, uTs[(b + 1) % 2][0], uTs[(b + 1) % 2][1], res)
                for s in range(4):
                    pt, ptb = res["ptp"][s % 2]
                    for c in range(8):
                        S.op("pe", lambda e: e.transpose(out=pt[:, c * 128:(c + 1) * 128], in_=osb[:, s, c * 128:(c + 1) * 128],
                                                         identity=ident[:]), reads=[osbb, ibuf], writes=[ptb])
                    S.op("act", lambda e: e.copy(out=oT[:, 0:4, s * 128:(s + 1) * 128],
                                                 in_=pt[:, 0:512].rearrange("p (c t) -> p c t", c=4)), reads=[ptb], writes=[oTbs[s]])
                    S.op("dve", lambda e: e.tensor_copy(out=oT[:, 4:8, s * 128:(s + 1) * 128],
                                                        in_=pt[:, 512:1024].rearrange("p (c t) -> p c t", c=4)),
                         reads=[ptb], writes=[oTbs[s]])
                for s in range(4):
                    hrows = slice(tok0 + s * 128, tok0 + (s + 1) * 128)
                    h_, hb_ = hts[s]
                    for nn in range(2):
                        p_, pb_ = pwo.next()
                        for k in range(8):
                            S.op("pe", lambda e: e.matmul(p_[:], lhsT=oT[:, k, s * 128:(s + 1) * 128],
                                                          rhs=wo[:, k, nn * 512:(nn + 1) * 512], start=(k == 0), stop=(k == 7)),
                                 reads=[oTbs[s], wob], writes=[pb_])
                        S.op("dve", lambda e: e.tensor_tensor(out=h_[:, nn * 512:(nn + 1) * 512], in0=p_[:],
                                                              in1=h_[:, nn * 512:(nn + 1) * 512], op=ALU.add),
                             reads=[pb_, hb_], writes=[hb_])
                    S.dma("pool", h[hrows, :], h_[:], reads=[hb_])
        S.barrier()


def phase_final(S, W, ntok, h, out):
    with ExitStack() as st:
        C = Ctx(S, st)
        gain, gb = C.sb([128, D], F32, "fgain")
        S.dma("sp", gain[:], bcast_row(W["final_norm"], 128), writes=[gb])
        ht = Rot([C.sb([128, D], F32, "fh") for _ in range(3)])
        jk = Rot([C.sb([128, D], BF16, "fjk") for _ in range(2)])
        ss = Rot([C.sb([128, 4], F32, "fss") for _ in range(3)])
        for n in range(ntok // 128):
            rows = slice(n * 128, (n + 1) * 128)
            h_, hb_ = ht.next()
            j_, jb_ = jk.next()
            s_, sb_ = ss.next()
            S.dma("sp", h_[:], h[rows, :], writes=[hb_])
            S.op("act", lambda e: e.activation(out=j_[:], in_=h_[:], func=AF.Square, accum_out=s_[:, 0:1]),
                 reads=[hb_], writes=[jb_, sb_])
            S.op("dve", lambda e: e.tensor_scalar(out=s_[:, 1:2], in0=s_[:, 0:1], scalar1=1.0 / D, scalar2=RMS_EPS,
                                                  op0=ALU.mult, op1=ALU.add), reads=[sb_], writes=[sb_])
            S.op("act", lambda e: e.activation(out=s_[:, 2:3], in_=s_[:, 1:2], func=AF.Sqrt), reads=[sb_], writes=[sb_])
            S.op("dve", lambda e: e.reciprocal(out=s_[:, 3:4], in_=s_[:, 2:3]), reads=[sb_], writes=[sb_])
            S.op("dve", lambda e: e.scalar_tensor_tensor(out=h_[:], in0=h_[:], scalar=s_[:, 3:4], in1=gain[:],
                                                         op0=ALU.mult, op1=ALU.mult), reads=[hb_, sb_, gb], writes=[hb_])
            S.dma("pool", out[rows, :], h_[:], reads=[hb_])
        S.barrier()


WEIGHT_SHAPES = {
    "t5_bias": (32, 8), "ffn1_norm": (4, D), "ffn1_w_gate": (4, D, DFF), "ffn1_w_up": (4, D, DFF),
    "ffn1_w_down": (4, DFF, D), "mix_norm": (4, D), "w_in": (4, D, INW), "mlstm_conv_w": (4, 4, 512),
    "mlstm_conv_b": (4, 512), "mlstm_gate_b": (4, 8), "mlstm_norm": (4, 256), "pool_w": (4, 4, 64, 64),
    "pool_scale": (4, 256), "diff_lambda": (4, 4, 32), "diff_subln": (4, 64), "w_out": (4, D, D),
    "xattn_norm": (4, D), "mem_norm": (4, D), "xattn_wq": (4, D, D), "xattn_wkv": (4, D, 2 * D),
    "xattn_wo": (4, D, D), "ffn2_norm": (4, D), "ffn2_w_gate": (4, D, DFF), "ffn2_w_up": (4, D, DFF),
    "ffn2_w_down": (4, DFF, D), "final_norm": (D,),
}

ALL_PHASES = ("ffn1", "mixproj", "mlstm", "dil", "diff", "outproj", "xattn", "ffn2", "final")


def build(T=4096, NSEQ=2, DEPTH=4, MEM=256, phases=ALL_PHASES, debug=()):
    nc = bass.Bass("TRN2", target_bir_lowering=False)
    ntok = T * NSEQ
    x = nc.dram_tensor("x", [ntok, D], F32, kind="ExternalInput").ap()
    mem = nc.dram_tensor("mem", [NSEQ * MEM, D], F32, kind="ExternalInput").ap()
    W = {k: nc.dram_tensor(k, list(s), F32, kind="ExternalInput").ap() for k, s in WEIGHT_SHAPES.items()}
    CS = {k: nc.dram_tensor(k, list(s), F32, kind="ExternalInput").ap() for k, s in CONST_SHAPES.items()}
    out = nc.dram_tensor("out", [ntok, D], F32, kind="ExternalOutput").ap()

    def scratch(name, shape, dt):
        kind = "ExternalOutput" if name in debug else "Internal"
        return nc.dram_tensor("s_" + name, list(shape), dt, kind=kind).ap()

    SC = {
        "qkT": scratch("qkT", [NSEQ, 12 * 128, T], BF16),
        "kg": scratch("kg", [NSEQ, T, 256], BF16),
        "mv1": scratch("mv1", [NSEQ, T, 260], BF16),
        "og": scratch("og", [NSEQ, T, 256], BF16),
        "ag": scratch("ag", [NSEQ, 128, T // 128, 8], F32),
        "eb": scratch("eb", [NSEQ, 128, T // 128, 4], F32),
        "dv1": scratch("dv1", [NSEQ, T, 260], BF16),
        "fv1": scratch("fv1", [NSEQ, T, 260], BF16),
        "dacc": scratch("dacc", [NSEQ, 3, T, 260], F32),
        "y": scratch("y", [NSEQ, T, D], BF16),
        "gvec": scratch("gvec", [8, NG], BF16),
        "dbg": scratch("dbg", [128, 64], F32),
    }
    h = out
    with ExitStack() as st:
        S = Sched(nc, st)
        C0 = Ctx(S, st)
        ident, ibuf, jrev, jb = setup_consts(S, C0, W, CS, SC)
        for l in range(DEPTH):
            if "ffn1" in phases:
                phase_ffn(S, W, "ffn1", l, x if l == 0 else h, h, ntok, ident, ibuf)
            if "mixproj" in phases:
                phase_mixproj(S, W, CS, SC, l, NSEQ, T, h, ident, ibuf)
            for seq in range(NSEQ):
                if "mlstm" in phases:
                    phase_mlstm(S, W, CS, SC, l, seq, T)
                if "dil" in phases:
                    phase_dil(S, W, CS, SC, l, seq, T, jrev, jb)
                if "diff" in phases:
                    phase_diff(S, W, CS, SC, l, seq, T, jrev, jb)
            with ExitStack() as xst:
                Cx = Ctx(S, xst)
                xw = None
                if "xattn" in phases:
                    xw = (Cx.sb([128, 8, D], BF16, "wq"), Cx.sb([128, 8, 2 * D], BF16, "wkv"), Cx.sb([128, 8, D], BF16, "wo"))
                    load_w_bf16(S, xw[0][0], xw[0][1], W["xattn_wq"][l], 8, D)
                    load_w_bf16(S, xw[1][0], xw[1][1], W["xattn_wkv"][l], 8, 2 * D)
                    load_w_bf16(S, xw[2][0], xw[2][1], W["xattn_wo"][l], 8, D)
                if "outproj" in phases:
                    phase_outproj(S, W, SC, l, NSEQ, T, h, ident, ibuf)
                if "xattn" in phases:
                    phase_xattn(S, W, l, NSEQ, T, MEM, h, mem, ident, ibuf, xw)
            if "ffn2" in phases:
                phase_ffn(S, W, "ffn2", l, h, h, ntok, ident, ibuf)
        if "final" in phases:
            phase_final(S, W, ntok, h, out)
        S.barrier()
        print("instructions:", S.ninst, {e: c for e, c in S.cnt.items()})
    return nc


_CONSTS = None


def kernel(**inputs):
    global _CONSTS
    if _CONSTS is None:
        _CONSTS = host_consts()
    x = np.ascontiguousarray(inputs["x"], dtype=np.float32)
    mem = np.ascontiguousarray(inputs["mem"], dtype=np.float32)
    B, T, _ = x.shape
    MEM = mem.shape[1]
    ncores = 8
    nseq = B // ncores
    nc = build(T=T, NSEQ=nseq, DEPTH=4, MEM=MEM)
    base = {k: np.ascontiguousarray(inputs[k], dtype=np.float32) for k in WEIGHT_SHAPES}
    base.update(_CONSTS)
    in_maps = []
    for c in range(ncores):
        m = dict(base)
        m["x"] = x[c * nseq:(c + 1) * nseq].reshape(nseq * T, D)
        m["mem"] = mem[c * nseq:(c + 1) * nseq].reshape(nseq * MEM, D)
        in_maps.append(m)
    res = run_bass_kernel_spmd(nc, in_maps, core_ids=list(range(ncores)))
    outs = [np.asarray(r["out"], dtype=np.float32).reshape(nseq, T, D) for r in res.results]
    return np.concatenate(outs, axis=0)
```

```python
import math
import os
MIXSTAGE = int(os.environ.get('MIXSTAGE', '9'))
MLSTAGE = int(os.environ.get('MLSTAGE', '9'))
from contextlib import ExitStack
import numpy as np
import concourse.bass as bass
import concourse.mybir as mybir
from concourse.bass_utils import run_bass_kernel_spmd

F32 = mybir.dt.float32
BF16 = mybir.dt.bfloat16
AF = mybir.ActivationFunctionType
ALU = mybir.AluOpType
AX = mybir.AxisListType

D = 1024
DFF = 2816
NFF = DFF // 128
INW = 2824
RMS_EPS = 1e-6
N_DMA_SEMS = 24


class Buf:
    __slots__ = ("w", "r")

    def __init__(self):
        self.w = None
        self.r = {}


class Sched:
    def __init__(self, nc, st):
        self.nc = nc
        self.eng = {"pe": nc.tensor, "act": nc.scalar, "dve": nc.vector, "pool": nc.gpsimd, "sp": nc.sync}
        self.sem = {}
        for e in self.eng:
            self.sem[e] = st.enter_context(nc.semaphore("s_" + e))
        for i in range(N_DMA_SEMS):
            self.sem[("d", i)] = st.enter_context(nc.semaphore("s_d%d" % i))
        self.cnt = {e: 0 for e in self.eng}
        self.dval = [0] * N_DMA_SEMS
        self.rr = 0
        self.known = {e: {} for e in self.eng}
        self.uid = 0
        self.ninst = 0

    def name(self, p):
        self.uid += 1
        return "%s_%d" % (p, self.uid)

    def _wait(self, e, deps):
        kn = self.known[e]
        for k, v in deps.items():
            if k == e and e == "pe":
                continue
            if kn.get(k, 0) >= v:
                continue
            self.eng[e].wait_ge(self.sem[k], v)
            self.ninst += 1
            kn[k] = v

    @staticmethod
    def _deps(reads, writes, deps):
        def add(ev):
            if ev is not None and deps.get(ev[0], 0) < ev[1]:
                deps[ev[0]] = ev[1]
        for b in reads:
            add(b.w)
        for b in writes:
            add(b.w)
            for k, v in b.r.items():
                add((k, v))

    @staticmethod
    def _mark(ev, reads, writes):
        for b in reads:
            if b.r.get(ev[0], 0) < ev[1]:
                b.r[ev[0]] = ev[1]
        for b in writes:
            b.w = ev
            b.r = {}

    def op(self, e, fn, reads=(), writes=()):
        deps = {}
        self._deps(reads, writes, deps)
        self._wait(e, deps)
        inst = fn(self.eng[e])
        self.cnt[e] += 1
        self.ninst += 1
        inst.then_inc(self.sem[e], 1)
        self._mark((e, self.cnt[e]), reads, writes)

    def dma(self, e, out, in_, reads=(), writes=(), **kw):
        i = self.rr
        self.rr = (i + 1) % N_DMA_SEMS
        k = ("d", i)
        deps = {}
        if self.dval[i] > 0:
            deps[k] = self.dval[i]
        self._deps(reads, writes, deps)
        self._wait(e, deps)
        inst = self.eng[e].dma_start(out=out, in_=in_, **kw)
        self.ninst += 1
        self.dval[i] += 16
        inst.then_inc(self.sem[k], 16)
        self._mark((k, self.dval[i]), reads, writes)

    def barrier(self):
        deps = {e: c for e, c in self.cnt.items() if c > 0}
        for i in range(N_DMA_SEMS):
            if self.dval[i] > 0:
                deps[("d", i)] = self.dval[i]
        for e in self.eng:
            self._wait(e, dict(deps))


class Ctx:
    def __init__(self, S, st):
        self.S = S
        self.st = st
        self.nc = S.nc

    def sb(self, shape, dt, name="t"):
        t = self.st.enter_context(self.nc.sbuf_tensor(self.S.name(name), list(shape), dt))
        return t, Buf()

    def ps(self, shape, dt, name="p"):
        t = self.st.enter_context(self.nc.psum_tensor(self.S.name(name), list(shape), dt))
        return t, Buf()


def bcast_row(handle_ap, nparts):
    a = handle_ap
    return bass.AP(a.tensor, a.offset, [[0, nparts]] + [list(x) for x in a.ap])


def load_w_bf16(S, dst, dbuf, src, kchunks, ncols):
    for c in range(kchunks):
        S.dma("pool", dst[:, c, :], src[c * 128:(c + 1) * 128, :], writes=[dbuf], max_dma_last_dim=4096)


def norm_pre(S, C, h_src, r0, ntile, gain, gbuf, res, lq="sp"):
    hn, u, ss = res["hn"], res["u"], res["ss"]
    res["i"] = res.get("i", 0) + 1
    sst, ssb = ss[res["i"] % len(ss)]
    tiles = []
    us = []
    for i in range(ntile):
        res["j"] = res.get("j", 0) + 1
        ht, hb = hn[res["j"] % len(hn)]
        res["k"] = res.get("k", 0) + 1
        ut, ub = u[res["k"] % len(u)]
        rows = slice(r0 + i * 128, r0 + (i + 1) * 128)
        S.dma(lq, ht[:], h_src[rows, :], writes=[hb])
        S.op("act", lambda e: e.activation(out=ut[:], in_=ht[:], func=AF.Square, accum_out=sst[:, i:i + 1]),
             reads=[hb], writes=[ub, ssb])
        tiles.append((ht, hb))
        us.append((ut, ub))
    S.op("dve", lambda e: e.tensor_scalar(out=sst[:, 4:4 + ntile], in0=sst[:, 0:ntile], scalar1=1.0 / D, scalar2=RMS_EPS,
                                          op0=ALU.mult, op1=ALU.add), reads=[ssb], writes=[ssb])
    S.op("act", lambda e: e.activation(out=sst[:, 8:8 + ntile], in_=sst[:, 4:4 + ntile], func=AF.Sqrt), reads=[ssb], writes=[ssb])
    S.op("dve", lambda e: e.reciprocal(out=sst[:, 12:12 + ntile], in_=sst[:, 8:8 + ntile]), reads=[ssb], writes=[ssb])
    for i in range(ntile):
        ht, hb = tiles[i]
        ut, ub = us[i]
        S.op("dve", lambda e: e.scalar_tensor_tensor(out=ut[:], in0=ht[:], scalar=sst[:, 12 + i:13 + i], in1=gain[:],
                                                     op0=ALU.mult, op1=ALU.mult),
             reads=[hb, ssb, gbuf], writes=[ub])
    return us


def norm_post(S, us, ident, ibuf, uT, uTbuf, res):
    ptp = res["ptp"]
    for i, (ut, ub) in enumerate(us):
        res["m"] = res.get("m", 0) + 1
        pt, pb = ptp[res["m"] % len(ptp)]
        for c in range(8):
            S.op("pe", lambda e, c=c: e.transpose(out=pt[:, c * 128:(c + 1) * 128], in_=ut[:, c * 128:(c + 1) * 128],
                                                  identity=ident[:]), reads=[ub, ibuf], writes=[pb])
        S.op("act", lambda e: e.copy(out=uT[:, 0:4, i * 128:(i + 1) * 128],
                                     in_=pt[:, 0:512].rearrange("p (c t) -> p c t", c=4)),
             reads=[pb], writes=[uTbuf])
        S.op("dve", lambda e: e.tensor_copy(out=uT[:, 4:8, i * 128:(i + 1) * 128],
                                            in_=pt[:, 512:1024].rearrange("p (c t) -> p c t", c=4)),
             reads=[pb], writes=[uTbuf])


def norm_block(S, C, h_src, r0, ntile, gain, gbuf, ident, ibuf, uT, uTbuf, res, lq="sp"):
    us = norm_pre(S, C, h_src, r0, ntile, gain, gbuf, res, lq=lq)
    norm_post(S, us, ident, ibuf, uT, uTbuf, res)


def phase_ffn(S, W, pre, l, h_in, h_out, ntok, ident, ibuf):
    nc = S.nc
    NB = ntok // 512
    with ExitStack() as st:
        C = Ctx(S, st)
        wg, wgb = C.sb([128, 8, DFF], BF16, "wg")
        wu, wub = C.sb([128, 8, DFF], BF16, "wu")
        wd, wdb = C.sb([128, NFF, D], BF16, "wd")
        gain, gb = C.sb([128, D], F32, "gain")
        uT, uTb = C.sb([128, 8, 512], BF16, "uT")
        aT, aTb = C.sb([128, NFF, 512], BF16, "aT")
        res = norm_res(C)
        hr = [C.sb([128, D], F32, "hr") for _ in range(2)]
        sg = [C.sb([128, 512], F32, "sg") for _ in range(2)]
        psg = [C.ps([128, 512], F32, "psg") for _ in range(2)]
        psu = [C.ps([128, 512], F32, "psu") for _ in range(2)]
        psd = [C.ps([128, 512], F32, "psd") for _ in range(2)]

        S.dma("sp", gain[:], bcast_row(W[pre + "_norm"][l], 128), writes=[gb])
        load_w_bf16(S, wg, wgb, W[pre + "_w_gate"][l], 8, DFF)
        load_w_bf16(S, wu, wub, W[pre + "_w_up"][l], 8, DFF)
        load_w_bf16(S, wd, wdb, W[pre + "_w_down"][l], NFF, D)

        def npre(b):
            return norm_pre(S, C, h_in, b * 512, 4, gain, gb, res)

        def npost(us):
            norm_post(S, us, ident, ibuf, uT, uTb, res)

        def gateup(b):
            for f in range(NFF):
                pg, pgb = psg[f % 2]
                pu, pub = psu[f % 2]
                sgt, sgb = sg[f % 2]
                for k in range(8):
                    S.op("pe", lambda e, k=k: e.matmul(pg[:], lhsT=wg[:, k, f * 128:(f + 1) * 128], rhs=uT[:, k, :],
                                                       start=(k == 0), stop=(k == 7)),
                         reads=[wgb, uTb], writes=[pgb])
                for k in range(8):
                    S.op("pe", lambda e, k=k: e.matmul(pu[:], lhsT=wu[:, k, f * 128:(f + 1) * 128], rhs=uT[:, k, :],
                                                       start=(k == 0), stop=(k == 7)),
                         reads=[wub, uTb], writes=[pub])
                S.op("act", lambda e: e.activation(out=sgt[:], in_=pg[:], func=AF.Silu), reads=[pgb], writes=[sgb])
                S.op("dve", lambda e: e.tensor_tensor(out=aT[:, f, :], in0=sgt[:], in1=pu[:], op=ALU.mult),
                     reads=[sgb, pub], writes=[aTb])

        def down(b):
            for i in range(4):
                ht, hb = hr[i % 2]
                rows = slice(b * 512 + i * 128, b * 512 + (i + 1) * 128)
                S.dma("sp", ht[:], h_in[rows, :], writes=[hb])
                for n in range(2):
                    pd, pdb = psd[n]
                    for f in range(NFF):
                        S.op("pe", lambda e, f=f: e.matmul(pd[:], lhsT=aT[:, f, i * 128:(i + 1) * 128],
                                                           rhs=wd[:, f, n * 512:(n + 1) * 512],
                                                           start=(f == 0), stop=(f == NFF - 1)),
                             reads=[aTb, wdb], writes=[pdb])
                    S.op("dve", lambda e: e.scalar_tensor_tensor(out=ht[:, n * 512:(n + 1) * 512], in0=pd[:],
                                                                 scalar=0.5, in1=ht[:, n * 512:(n + 1) * 512],
                                                                 op0=ALU.mult, op1=ALU.add),
                         reads=[pdb, hb], writes=[hb])
                S.dma("pool", h_out[rows, :], ht[:], reads=[hb])

        npost(npre(0))
        for b in range(NB):
            us_next = npre(b + 1) if b + 1 < NB else None
            gateup(b)
            if us_next is not None:
                npost(us_next)
            down(b)
        S.barrier()


NEG = -30000.0
NG = 4608
DIL = ((128, 1), (512, 4), (2048, 16))


def t5_bucket_np(dist):
    dist = np.asarray(dist, dtype=np.int64)
    d = np.maximum(dist, 1).astype(np.float32)
    large = 16 + (np.log(d / np.float32(16)) / np.float32(math.log(2048 / 16)) * np.float32(16)).astype(np.int32)
    large = np.minimum(large, 31)
    return np.where(dist < 16, dist, large)


def host_consts():
    c = {}
    s = np.arange(128)
    c["c_tri"] = (s[:, None] <= s[None, :]).astype(np.float32)
    sel = np.zeros((128, 128), np.float32)
    sel[127, :] = 1
    c["c_sel"] = sel
    c["c_maskT"] = c["c_tri"] * np.float32(0.125)
    c["c_jrev"] = np.ascontiguousarray(np.eye(128, dtype=np.float32)[::-1])
    oh = np.zeros((33, NG), np.float32)
    for p, (w, d) in enumerate(DIL):
        jx = np.arange(384)
        dl = jx - 127
        valid = (dl >= 0) & (dl <= 128)
        bk = t5_bucket_np(np.clip(dl, 0, 128) * d)
        cols = p * 384 + jx
        oh[bk[valid], cols[valid]] = 1
        oh[32, cols[~valid]] = NEG
    jx = np.arange(NG - 1152)
    dl = jx - 511
    valid = dl >= 0
    bk = t5_bucket_np(np.clip(dl, 0, None))
    cols = 1152 + jx
    oh[bk[valid], cols[valid]] = 1
    oh[32, cols[~valid]] = NEG
    c["c_oh"] = oh
    wins = [2, 4, 8, 16]
    invc = np.zeros((128, 2, 2, 512), np.float32)
    t = np.arange(512)
    for cc in range(2):
        for half in range(2):
            w = wins[cc * 2 + half]
            rows = slice(half * 64, half * 64 + 64)
            invc[rows, 1, cc, :] = 1.0 / w
            invc[rows, 0, cc, :] = 1.0 / np.minimum(t + 1, w)
    c["c_invc"] = invc
    return c


CONST_SHAPES = {"c_tri": (128, 128), "c_sel": (128, 128), "c_maskT": (128, 128), "c_jrev": (128, 128),
                "c_oh": (33, NG), "c_invc": (128, 2, 2, 512)}


class Rot:
    def __init__(self, items):
        self.items = items
        self.i = 0

    def next(self):
        x = self.items[self.i % len(self.items)]
        self.i += 1
        return x


def col1(ap1d):
    return ap1d.rearrange("(p o) -> p o", o=1)


def load_tok(S, dst, dbuf, src, nch):
    for n0 in range(0, nch, 8):
        n1 = min(nch, n0 + 8)
        S.dma("sp", dst[:, n0:n1, :], src[n0 * 128:n1 * 128, :].rearrange("(n p) c -> p n c", p=128), writes=[dbuf])


def setup_consts(S, C, W, CS, SC):
    nc = S.nc
    idf, idfb = C.sb([128, 128], F32, "identf")
    ident, ibuf = C.sb([128, 128], BF16, "ident")
    jrev, jb = C.sb([128, 128], BF16, "jrev")
    S.op("pool", lambda e: e.memset(idf[:], 1.0), writes=[idfb])
    S.op("pool", lambda e: e.affine_select(out=idf[:], in_=idf[:], pattern=[[-1, 128]], compare_op=ALU.is_equal,
                                           fill=0.0, base=0, channel_multiplier=1), reads=[idfb], writes=[idfb])
    S.op("dve", lambda e: e.tensor_copy(out=ident[:], in_=idf[:]), reads=[idfb], writes=[ibuf])
    S.dma("pool", jrev[:], CS["c_jrev"], writes=[jb])
    with ExitStack() as st:
        C2 = Ctx(S, st)
        t5x, t5b = C2.sb([33, 8], F32, "t5x")
        oh, ohb = C2.sb([33, NG], F32, "oh")
        gsb, gsbb = C2.sb([8, NG], BF16, "gsb")
        pg = [C2.ps([128, 512], F32, "pgv") for _ in range(2)]
        S.op("dve", lambda e: e.memset(t5x[:], 1.0), writes=[t5b])
        S.dma("sp", t5x[0:32, :], W["t5_bias"], writes=[t5b])
        S.dma("sp", oh[:], CS["c_oh"], writes=[ohb])
        for n in range(NG // 512):
            p, pb = pg[n % 2]
            S.op("pe", lambda e: e.matmul(p[0:8, :], lhsT=t5x[:], rhs=oh[:, n * 512:(n + 1) * 512], start=True, stop=True),
                 reads=[t5b, ohb], writes=[pb])
            S.op("act", lambda e: e.copy(out=gsb[:, n * 512:(n + 1) * 512], in_=p[0:8, :]), reads=[pb], writes=[gsbb])
        S.dma("sp", SC["gvec"], gsb[:], reads=[gsbb])
        S.barrier()
    return ident, ibuf, jrev, jb


def norm_res(C):
    return {
        "hn": [C.sb([128, D], F32, "hn") for _ in range(4)],
        "u": [C.sb([128, D], BF16, "u") for _ in range(4)],
        "ss": [C.sb([128, 16], F32, "ss") for _ in range(2)],
        "ptp": [C.ps([128, 1024], BF16, "ptp") for _ in range(2)],
    }


def phase_mixproj(S, W, CS, SC, l, NSEQ, T, h, ident, ibuf):
    NB = T // 512
    with ExitStack() as st:
        C = Ctx(S, st)
        win, winb = C.sb([128, 8, INW], BF16, "win")
        gain, gb = C.sb([128, D], F32, "gain")
        uTs = [C.sb([128, 8, 512], BF16, "uT") for _ in range(2)]
        res = norm_res(C)
        cw, cwb = C.sb([128, 4, 4], F32, "cw")
        cbias, cbb = C.sb([128, 4], F32, "cb")
        gateb, gtb = C.sb([128, 8], F32, "gateb")
        pscale, pscb = C.sb([128, 256], F32, "pscale")
        wblkf, wfb = C.sb([128, 2, 128], F32, "wblkf")
        wblk, wkb = C.sb([128, 2, 128], BF16, "wblk")
        invc, invb = C.sb([128, 2, 2, 512], F32, "invc")
        tri, trib = C.sb([128, 128], BF16, "tri")
        sel, selb = C.sb([128, 128], BF16, "sel")
        gbr = Rot([C.sb([128, 4, 16], BF16, "gbb") for _ in range(2)])
        Xm = [C.sb([128, 515], F32, "Xm") for _ in range(4)]
        Xp = [C.sb([128, 528], F32, "Xp") for _ in range(2)]
        acc = Rot([C.sb([128, 512], F32, "acc") for _ in range(2)])
        stg = Rot([C.sb([128, 512], BF16, "stg") for _ in range(4)])
        ksg = [C.sb([128, 512], BF16, "ksg") for _ in range(2)]
        ssum = [C.sb([128, 528], F32, "ssum") for _ in range(4)]
        ptmp = Rot([C.sb([128, 512], F32, "ptmp") for _ in range(2)])
        dmT = [C.sb([128, 512], BF16, "dmT") for _ in range(2)]
        mvst = Rot([C.sb([128, 4, 65], BF16, "mvst") for _ in range(2)])
        dvst = Rot([C.sb([128, 4, 65], BF16, "dvst") for _ in range(2)])
        fvst = Rot([C.sb([128, 4, 65], BF16, "fvst") for _ in range(2)])
        ogst = Rot([C.sb([128, 256], BF16, "ogst") for _ in range(2)])
        kgst = Rot([C.sb([128, 256], BF16, "kgst") for _ in range(2)])
        ybst = Rot([C.sb([128, 256], BF16, "ybst") for _ in range(2)])
        agst = Rot([C.sb([128, 4, 8], F32, "agst") for _ in range(2)])
        ebst = Rot([C.sb([128, 4, 4], F32, "ebst") for _ in range(2)])
        gsr = Rot([C.sb([128, 4, 32], F32, "gs") for _ in range(2)])
        pgen = Rot([C.ps([128, 512], F32, "pgen") for _ in range(4)])
        pgp = C.ps([128, 512], F32, "pgp")
        ptk = C.ps([128, 1024], BF16, "ptk")

        load_w_bf16(S, win, winb, W["w_in"][l], 8, INW)
        S.dma("sp", gain[:], bcast_row(W["mix_norm"][l], 128), writes=[gb])
        S.dma("sp", gateb[:], bcast_row(W["mlstm_gate_b"][l], 128), writes=[gtb])
        S.dma("sp", pscale[:], bcast_row(W["pool_scale"][l], 128), writes=[pscb])
        for c in range(4):
            for j in range(4):
                S.dma("sp", cw[:, c, j:j + 1], col1(W["mlstm_conv_w"][l, j, c * 128:(c + 1) * 128]), writes=[cwb])
            S.dma("sp", cbias[:, c:c + 1], col1(W["mlstm_conv_b"][l, c * 128:(c + 1) * 128]), writes=[cbb])
        S.op("dve", lambda e: e.memset(wblkf[:], 0.0), writes=[wfb])
        for g in range(4):
            r0 = (g % 2) * 64
            S.dma("sp", wblkf[r0:r0 + 64, g // 2, r0:r0 + 64], W["pool_w"][l, g], writes=[wfb])
        S.op("dve", lambda e: e.tensor_copy(out=wblk[:], in_=wblkf[:]), reads=[wfb], writes=[wkb])
        S.dma("sp", invc[:], CS["c_invc"], writes=[invb])
        S.dma("pool", tri[:], CS["c_tri"], writes=[trib])
        S.dma("pool", sel[:], CS["c_sel"], writes=[selb])
        for r in (mvst, dvst, fvst):
            for t_, b_ in r.items:
                S.op("dve", lambda e: e.memset(t_[:], 1.0), writes=[b_])

        if MIXSTAGE <= 0:
            S.barrier()
            return
        fm_specs = [("m", 0, 0), ("m", 1, 128), ("m", 2, 256), ("m", 3, 384), ("p", 0, 1032), ("p", 1, 1160),
                    ("d", 4, 1288), ("d", 5, 1416), ("d", 6, 1544), ("d", 7, 1672),
                    ("d", 8, 2056), ("d", 9, 2184), ("d", 10, 2312), ("d", 11, 2440)]
        dscale = {4: 0.125, 5: 0.125, 6: 1.0, 7: 1.0, 8: 32 ** -0.5, 9: 32 ** -0.5, 10: 1.0, 11: 1.0}

        blocks = [(sq, bb) for sq in range(NSEQ) for bb in range(NB)]
        norm_block(S, C, h, 0, 4, gain, gb, ident, ibuf, uTs[0][0], uTs[0][1], res, lq="pool")
        for bi, (seq, b) in enumerate(blocks):
            if True:
                uT, uTb = uTs[bi % 2]
                us_next = None
                if bi + 1 < len(blocks):
                    nsq, nbb = blocks[bi + 1]
                    us_next = norm_pre(S, C, h, nsq * T + nbb * 512, 4, gain, gb, res, lq="pool")
                tok0 = seq * T + b * 512
                tsl = slice(b * 512, (b + 1) * 512)
                if b == 0:
                    for c in range(4):
                        S.op("dve", lambda e: e.memset(Xm[c][0][:, 0:3], 0.0), writes=[Xm[c][1]])
                    for c in range(2):
                        S.op("dve", lambda e: e.memset(Xp[c][0][:, 0:16], 0.0), writes=[Xp[c][1]])
                gs, _ = gsr.next()
                gbt, _ = gbr.next()
                pg, _ = pgp
                gsbs = [Buf() for _ in range(4)]
                gbbs = [Buf() for _ in range(4)]
                pgbs = [pgp[1]] * 4
                agts = [agst.next()]

                gsb_, gbtb, pgb = gsbs[0], gbbs[0], pgbs[0]
                ag_t, ag_b = agts[0]
                PV4 = pg[:, 0:64].rearrange("p (i c) -> p i c", c=16)

                def gateA():
                    for i in range(4):
                        tl = slice(i * 128, (i + 1) * 128)
                        for k in range(8):
                            S.op("pe", lambda e: e.matmul(pg[:, i * 16:i * 16 + 8], lhsT=uT[:, k, tl], rhs=win[:, k, 1024:1032],
                                                          start=(k == 0), stop=(k == 7)), reads=[uTb, winb], writes=[pgb])
                    S.op("dve", lambda e: e.tensor_tensor(out=gs[:, :, 0:8], in0=PV4[:, :, 0:8],
                                                          in1=gateb[:].unsqueeze(1).broadcast_to([128, 4, 8]), op=ALU.add),
                         reads=[pgb, gtb], writes=[gsb_])
                    S.op("act", lambda e: e.activation(out=gs[:, :, 8:12], in_=gs[:, :, 4:8], func=AF.Sigmoid),
                         reads=[gsb_], writes=[gsb_])
                    S.op("act", lambda e: e.activation(out=gs[:, :, 12:16], in_=gs[:, :, 8:12], func=AF.Ln),
                         reads=[gsb_], writes=[gsb_])
                    S.op("dve", lambda e: e.tensor_copy(out=gbt[:, :, 0:4], in_=gs[:, :, 12:16]), reads=[gsb_], writes=[gbtb])
                    S.op("dve", lambda e: e.tensor_copy(out=gs[:, :, 28:32], in_=gbt[:, :, 0:4]), reads=[gbtb], writes=[gsb_])
                    S.op("dve", lambda e: e.tensor_tensor(out=gbt[:, :, 4:8], in0=gs[:, :, 12:16], in1=gs[:, :, 28:32],
                                                          op=ALU.subtract), reads=[gsb_, gbtb], writes=[gbtb])

                def gateB():
                    for i in range(4):
                        S.op("pe", lambda e: e.matmul(pg[:, i * 16 + 8:i * 16 + 12], lhsT=tri[:], rhs=gbt[:, i, 0:4],
                                                      start=True, stop=False), reads=[trib, gbtb], writes=[pgb])
                        S.op("pe", lambda e: e.matmul(pg[:, i * 16 + 8:i * 16 + 12], lhsT=tri[:], rhs=gbt[:, i, 4:8],
                                                      start=False, stop=True), reads=[trib, gbtb], writes=[pgb])
                    S.op("act", lambda e: e.copy(out=gs[:, :, 16:20], in_=PV4[:, :, 8:12]), reads=[pgb], writes=[gsb_])
                    S.op("act", lambda e: e.activation(out=ag_t[:, :, 0:4], in_=gs[:, :, 16:20], func=AF.Exp),
                         reads=[gsb_], writes=[ag_b])
                    S.op("dve", lambda e: e.tensor_tensor(out=gs[:, :, 20:24], in0=gs[:, :, 0:4], in1=gs[:, :, 16:20],
                                                          op=ALU.subtract), reads=[gsb_], writes=[gsb_])
                    S.op("act", lambda e: e.activation(out=ag_t[:, :, 4:8], in_=gs[:, :, 20:24], func=AF.Exp),
                         reads=[gsb_], writes=[ag_b])
                    S.op("dve", lambda e: e.tensor_scalar(out=gs[:, :, 24:28], in0=ag_t[:, :, 4:8], scalar1=0.125, scalar2=None,
                                                          op0=ALU.mult), reads=[ag_b], writes=[gsb_])
                    S.op("dve", lambda e: e.tensor_copy(out=gbt[:, :, 8:12], in_=gs[:, :, 16:20]), reads=[gsb_], writes=[gbtb])
                    S.op("dve", lambda e: e.tensor_copy(out=gs[:, :, 28:32], in_=gbt[:, :, 8:12]), reads=[gbtb], writes=[gsb_])
                    S.op("dve", lambda e: e.tensor_tensor(out=gbt[:, :, 12:16], in0=gs[:, :, 16:20], in1=gs[:, :, 28:32],
                                                          op=ALU.subtract), reads=[gsb_, gbtb], writes=[gbtb])

                def gateC():
                    for i in range(4):
                        S.op("pe", lambda e: e.matmul(pg[:, i * 16 + 12:i * 16 + 16], lhsT=sel[:], rhs=gbt[:, i, 8:12],
                                                      start=True, stop=False), reads=[selb, gbtb], writes=[pgb])
                        S.op("pe", lambda e: e.matmul(pg[:, i * 16 + 12:i * 16 + 16], lhsT=sel[:], rhs=gbt[:, i, 12:16],
                                                      start=False, stop=True), reads=[selb, gbtb], writes=[pgb])
                    eb_t, eb_b = ebst.next()
                    S.op("act", lambda e: e.activation(out=eb_t[:], in_=PV4[:, :, 12:16], func=AF.Exp),
                         reads=[pgb], writes=[eb_b])
                    S.dma("sp", SC["ag"][seq, :, b * 4:(b + 1) * 4, :], ag_t[:], reads=[ag_b])
                    S.dma("sp", SC["eb"][seq, :, b * 4:(b + 1) * 4, :], eb_t[:], reads=[eb_b])

                for fi, (kind, ci, col) in enumerate(fm_specs):
                    if fi == 0:
                        gateA()
                    elif fi == 4:
                        gateB()
                    elif fi == 8:
                        gateC()
                    p, pb = pgen.next()
                    for k in range(8):
                        S.op("pe", lambda e: e.matmul(p[:], lhsT=win[:, k, col:col + 128], rhs=uT[:, k, :],
                                                      start=(k == 0), stop=(k == 7)), reads=[winb, uTb], writes=[pb])
                    if kind == "m":
                        X, Xb = Xm[ci]
                        S.op("act", lambda e: e.copy(out=X[:, 3:515], in_=p[:]), reads=[pb], writes=[Xb])
                        a_, ab_ = acc.next()
                        S.op("dve", lambda e: e.tensor_scalar(out=a_[:], in0=X[:, 3:515], scalar1=cw[:, ci, 3:4],
                                                              scalar2=cbias[:, ci:ci + 1], op0=ALU.mult, op1=ALU.add),
                             reads=[Xb, cwb, cbb], writes=[ab_])
                        for j in range(3):
                            S.op("dve", lambda e: e.scalar_tensor_tensor(out=a_[:], in0=X[:, j:j + 512],
                                                                         scalar=cw[:, ci, j:j + 1], in1=a_[:],
                                                                         op0=ALU.mult, op1=ALU.add),
                                 reads=[Xb, cwb, ab_], writes=[ab_])
                        S.op("dve", lambda e: e.tensor_copy(out=X[:, 0:3], in_=X[:, 512:515]), reads=[Xb], writes=[Xb])
                        if ci < 2:
                            s_, sb_ = stg.next()
                        else:
                            s_, sb_ = ksg[ci - 2]
                        S.op("act", lambda e: e.activation(out=s_[:], in_=a_[:], func=AF.Silu), reads=[ab_], writes=[sb_])
                        S.dma("sp", SC["qkT"][seq, ci * 128:(ci + 1) * 128, tsl], s_[:], reads=[sb_])
                        if ci >= 2:
                            pk, pkb = ptk
                            for i in range(4):
                                o0 = i * 256 + (ci - 2) * 128
                                S.op("pe", lambda e: e.transpose(out=pk[:, o0:o0 + 128], in_=s_[:, i * 128:(i + 1) * 128],
                                                                 identity=ident[:]), reads=[sb_, ibuf], writes=[pkb])
                    elif kind == "p":
                        X, Xb = Xp[ci]
                        S.op("act", lambda e: e.copy(out=X[:, 16:528], in_=p[:]), reads=[pb], writes=[Xb])
                        prev, prevb = X, Xb
                        sh = 1
                        nlev = 2 if ci == 0 else 4
                        levels = []
                        for lev in range(nlev):
                            s_, sb_ = ssum[lev]
                            lo = 2 * sh - 1
                            S.op("dve", lambda e: e.tensor_tensor(out=s_[:, lo:528], in0=prev[:, lo:528],
                                                                  in1=prev[:, lo - sh:528 - sh], op=ALU.add),
                                 reads=[prevb], writes=[sb_])
                            levels.append((s_, sb_))
                            prev, prevb = s_, sb_
                            sh *= 2
                        d_, db_ = dmT[ci]
                        for half in range(2):
                            s_, sb_ = levels[(0 if ci == 0 else 2) + half]
                            rs = slice(half * 64, half * 64 + 64)
                            if b == 0:
                                t_, tb_ = ptmp.next()
                                S.op("dve", lambda e: e.tensor_tensor(out=t_[rs, :], in0=s_[rs, 16:528],
                                                                      in1=invc[rs, 0, ci, :], op=ALU.mult),
                                     reads=[sb_, invb], writes=[tb_])
                                S.op("dve", lambda e: e.tensor_tensor(out=d_[rs, :], in0=t_[rs, :], in1=X[rs, 16:528],
                                                                      op=ALU.subtract), reads=[tb_, Xb], writes=[db_])
                            else:
                                S.op("dve", lambda e: e.scalar_tensor_tensor(out=d_[rs, :], in0=s_[rs, 16:528],
                                                                             scalar=invc[rs, 1, ci, 0:1], in1=X[rs, 16:528],
                                                                             op0=ALU.mult, op1=ALU.subtract),
                                     reads=[sb_, invb, Xb], writes=[db_])
                        S.op("dve", lambda e: e.tensor_copy(out=X[:, 0:16], in_=X[:, 512:528]), reads=[Xb], writes=[Xb])
                    else:
                        s_, sb_ = stg.next()
                        S.op("act", lambda e: e.activation(out=s_[:], in_=p[:], func=AF.Copy, scale=float(dscale[ci])),
                             reads=[pb], writes=[sb_])
                        S.dma("sp", SC["qkT"][seq, ci * 128:(ci + 1) * 128, tsl], s_[:], reads=[sb_])
                for i in range(4):
                    if i == 2 and us_next is not None:
                        norm_post(S, us_next, ident, ibuf, uTs[(bi + 1) % 2][0], uTs[(bi + 1) % 2][1], res)
                    tl = slice(i * 128, (i + 1) * 128)
                    rows = slice(b * 512 + i * 128, b * 512 + (i + 1) * 128)
                    G = gs[:, i, :]
                    k_, kb_ = kgst.next()
                    pk, pkb = ptk
                    S.op("dve", lambda e: e.tensor_tensor(
                        out=k_[:].rearrange("p (h d) -> p h d", h=4),
                        in0=pk[:, i * 256:(i + 1) * 256].rearrange("p (h d) -> p h d", h=4),
                        in1=G[:, 24:28].unsqueeze(2).broadcast_to([128, 4, 64]), op=ALU.mult),
                        reads=[pkb, gsbs[0]], writes=[kb_])
                    S.dma("sp", SC["kg"][seq, rows, :], k_[:], reads=[kb_])
                    pp, ppb = pgen.next()
                    for cc in range(2):
                        S.op("pe", lambda e: e.matmul(pp[:, cc * 128:(cc + 1) * 128], lhsT=dmT[cc][0][:, tl],
                                                      rhs=wblk[:, cc, :], start=True, stop=True),
                             reads=[dmT[cc][1], wkb], writes=[ppb])
                    y_, yb_ = ybst.next()
                    S.op("dve", lambda e: e.tensor_tensor(out=y_[:], in0=pp[:, 0:256], in1=pscale[:], op=ALU.mult),
                         reads=[ppb, pscb], writes=[yb_])
                    S.dma("sp", SC["y"][seq, rows, 256:512], y_[:], reads=[yb_])
                    p1, p1b = pgen.next()
                    for k in range(8):
                        S.op("pe", lambda e: e.matmul(p1[:], lhsT=uT[:, k, tl], rhs=win[:, k, 512:1024],
                                                      start=(k == 0), stop=(k == 7)), reads=[uTb, winb], writes=[p1b])
                    v_, vb_ = mvst.next()
                    S.op("act", lambda e: e.copy(out=v_[:, :, 0:64], in_=p1[:, 0:256].rearrange("p (h d) -> p h d", h=4)),
                         reads=[p1b], writes=[vb_])
                    o_, ob_ = ogst.next()
                    S.op("act", lambda e: e.activation(out=o_[:], in_=p1[:, 256:512], func=AF.Sigmoid),
                         reads=[p1b], writes=[ob_])
                    S.dma("sp", SC["mv1"][seq, rows, :], v_[:].rearrange("p h d -> p (h d)"), reads=[vb_])
                    S.dma("sp", SC["og"][seq, rows, :], o_[:], reads=[ob_])
                    p2, p2b = pgen.next()
                    for gi, col in ((0, 1800), (1, 2568)):
                        for k in range(8):
                            S.op("pe", lambda e: e.matmul(p2[:, gi * 256:(gi + 1) * 256], lhsT=uT[:, k, tl],
                                                          rhs=win[:, k, col:col + 256], start=(k == 0), stop=(k == 7)),
                                 reads=[uTb, winb], writes=[p2b])
                    dv_, dvb_ = dvst.next()
                    fv_, fvb_ = fvst.next()
                    S.op("act", lambda e: e.copy(out=dv_[:, :, 0:64], in_=p2[:, 0:256].rearrange("p (h d) -> p h d", h=4)),
                         reads=[p2b], writes=[dvb_])
                    S.op("act", lambda e: e.copy(out=fv_[:, :, 0:64], in_=p2[:, 256:512].rearrange("p (h d) -> p h d", h=4)),
                         reads=[p2b], writes=[fvb_])
                    S.dma("sp", SC["dv1"][seq, rows, :], dv_[:].rearrange("p h d -> p (h d)"), reads=[dvb_])
                    S.dma("sp", SC["fv1"][seq, rows, :], fv_[:].rearrange("p h d -> p (h d)"), reads=[fvb_])
        S.barrier()


def phase_mlstm(S, W, CS, SC, l, seq, T):
    NCH = T // 128
    with ExitStack() as st:
        C = Ctx(S, st)
        qk, qkb = C.sb([128, 4, T], BF16, "mqk")
        v1, v1b = C.sb([128, NCH, 260], BF16, "mv1")
        kg, kgb = C.sb([128, NCH, 256], BF16, "mkg")
        og, ogb = C.sb([128, NCH, 256], BF16, "mog")
        ag, agb = C.sb([128, NCH, 8], F32, "mag")
        eb, ebb = C.sb([128, NCH, 4], F32, "meb")
        ebp, ebpb = C.sb([128, NCH, 2], F32, "mebp")
        maskT, mkb = C.sb([128, 128], F32, "maskT")
        ng, ngb = C.sb([128, 256], F32, "ng")
        Cf, Cfb = C.sb([128, 2, 130], F32, "Cf")
        Cb, Cbb = C.sb([128, 2, 130], BF16, "Cb")
        tmpU = Rot([C.sb([128, 2, 130], F32, "tmpU") for _ in range(2)])
        PT = Rot([C.sb([128, 128], BF16, "PT") for _ in range(4)])
        nd = Rot([C.sb([128, 2, 4, 65], F32, "nd") for _ in range(2)])
        hh = Rot([C.sb([128, 2, 4, 64], F32, "hh") for _ in range(2)])
        sq = Rot([C.sb([128, 2, 4, 64], F32, "sq") for _ in range(2)])
        stt = Rot([C.sb([128, 2, 32], F32, "stt") for _ in range(2)])
        yst = Rot([C.sb([128, 2, 256], BF16, "yst") for _ in range(2)])
        psc = Rot([C.ps([128, 512], F32, "psc") for _ in range(2)])
        pU = Rot([C.ps([128, 512], F32, "pU") for _ in range(2)])
        po = Rot([C.ps([128, 2, 512], F32, "po") for _ in range(2)])

        for c in range(4):
            S.dma("sp", qk[:, c, :], SC["qkT"][seq, c * 128:(c + 1) * 128, :], writes=[qkb])
        load_tok(S, v1, v1b, SC["mv1"][seq], NCH)
        load_tok(S, kg, kgb, SC["kg"][seq], NCH)
        load_tok(S, og, ogb, SC["og"][seq], NCH)
        S.dma("sp", ag[:], SC["ag"][seq], writes=[agb])
        S.dma("sp", eb[:], SC["eb"][seq], writes=[ebb])
        S.dma("sp", maskT[:], CS["c_maskT"], writes=[mkb])
        S.dma("sp", ng[:], bcast_row(W["mlstm_norm"][l], 128), writes=[ngb])
        for pr in range(2):
            S.op("dve", lambda e: e.tensor_copy(out=ebp[0:64, :, pr], in_=eb[0:64, :, 2 * pr]), reads=[ebb], writes=[ebpb])
            S.op("dve", lambda e: e.tensor_copy(out=ebp[64:128, :, pr], in_=eb[64:128, :, 2 * pr + 1]),
                 reads=[ebb], writes=[ebpb])
        S.op("dve", lambda e: e.memset(Cf[:], 0.0), writes=[Cfb])
        S.op("dve", lambda e: e.memset(Cb[:], 0.0), writes=[Cbb])

        for c in range(NCH):
            if MLSTAGE <= 0:
                break
            cols = slice(c * 128, (c + 1) * 128)
            psA, psB = psc.items
            pts = []
            for hd in range(4):
                pr, hh_ = hd // 2, hd % 2
                rs = slice(hh_ * 64, hh_ * 64 + 64)
                ps_, psb_ = (psA, psB)[hh_]
                S.op("pe", lambda e: e.matmul(ps_[:, pr * 128:(pr + 1) * 128], lhsT=qk[rs, 2 + pr, cols], rhs=qk[rs, pr, cols],
                                              start=True, stop=True), reads=[qkb], writes=[psb_])
            if MLSTAGE <= 1:
                continue
            pu_, pub_ = pU.next()
            for pr in range(2):
                S.op("pe", lambda e: e.matmul(pu_[:, pr * 130:(pr + 1) * 130], lhsT=kg[:, c, pr * 128:(pr + 1) * 128],
                                              rhs=v1[:, c, pr * 130:(pr + 1) * 130], start=True, stop=True),
                     reads=[kgb, v1b], writes=[pub_])
            for hd in range(4):
                p_, pb_ = PT.next()
                ps_, psb_ = (psA, psB)[hd % 2]
                S.op("dve", lambda e: e.scalar_tensor_tensor(out=p_[:], in0=ps_[:, (hd // 2) * 128:(hd // 2 + 1) * 128],
                                                             scalar=ag[:, c, 4 + hd:5 + hd], in1=maskT[:],
                                                             op0=ALU.mult, op1=ALU.mult),
                     reads=[psb_, agb, mkb], writes=[pb_])
                pts.append((p_, pb_))
            if MLSTAGE <= 2:
                continue
            if c % 2 == 0:
                po2_, pob_ = po.next()
            po_ = po2_[:, c % 2, :]
            for pr in range(2):
                S.op("pe", lambda e: e.matmul(po_[:, pr * 130:(pr + 1) * 130], lhsT=qk[:, pr, cols], rhs=Cb[:, pr, :],
                                              start=True, stop=False), reads=[qkb, Cbb], writes=[pob_])
                for hh_ in range(2):
                    hd = 2 * pr + hh_
                    p_, pb_ = pts[hd]
                    S.op("pe", lambda e: e.matmul(po_[:, hd * 65:(hd + 1) * 65], lhsT=p_[:], rhs=v1[:, c, hd * 65:(hd + 1) * 65],
                                                  start=False, stop=True), reads=[pb_, v1b], writes=[pob_])
            if MLSTAGE <= 3:
                continue
            tu, tub = tmpU.next()
            for pr in range(2):
                S.op("act", lambda e: e.activation(out=tu[:, pr, :], in_=pu_[:, pr * 130:(pr + 1) * 130], func=AF.Identity,
                                                   scale=ebp[:, c, pr:pr + 1]), reads=[pub_, ebpb], writes=[tub])
                S.op("dve", lambda e: e.scalar_tensor_tensor(out=Cf[:, pr, :], in0=Cf[:, pr, :], scalar=ebp[:, c, pr:pr + 1],
                                                             in1=tu[:, pr, :], op0=ALU.mult, op1=ALU.add),
                     reads=[Cfb, ebpb, tub], writes=[Cfb])
                S.op("act", lambda e: e.copy(out=Cb[0:64, pr, 0:65], in_=Cf[0:64, pr, 0:65]), reads=[Cfb], writes=[Cbb])
                S.op("act", lambda e: e.copy(out=Cb[64:128, pr, 65:130], in_=Cf[64:128, pr, 65:130]), reads=[Cfb], writes=[Cbb])
            if MLSTAGE <= 4:
                continue
            if c % 2 == 0:
                continue
            c0 = c - 1
            n_, nb_ = nd.next()
            S.op("dve", lambda e: e.tensor_tensor(out=n_[:], in0=po2_[:, :, 0:260].rearrange("p n (h d) -> p n h d", h=4),
                                                  in1=ag[:, c0:c0 + 2, 0:4].unsqueeze(3).broadcast_to([128, 2, 4, 65]), op=ALU.mult),
                 reads=[pob_, agb], writes=[nb_])
            s_, sb_ = stt.next()
            S.op("act", lambda e: e.activation(out=s_[:, :, 0:4].unsqueeze(3), in_=n_[:, :, :, 64:65], func=AF.Abs),
                 reads=[nb_], writes=[sb_])
            S.op("dve", lambda e: e.tensor_scalar(out=s_[:, :, 0:4], in0=s_[:, :, 0:4], scalar1=1.0, scalar2=None,
                                                  op0=ALU.max), reads=[sb_], writes=[sb_])
            S.op("dve", lambda e: e.reciprocal(out=s_[:, :, 4:8], in_=s_[:, :, 0:4]), reads=[sb_], writes=[sb_])
            h_, hb_ = hh.next()
            S.op("dve", lambda e: e.tensor_tensor(out=h_[:], in0=n_[:, :, :, 0:64],
                                                  in1=s_[:, :, 4:8].unsqueeze(3).broadcast_to([128, 2, 4, 64]), op=ALU.mult),
                 reads=[nb_, sb_], writes=[hb_])
            q_, qb_ = sq.next()
            S.op("act", lambda e: e.activation(out=q_[:], in_=h_[:], func=AF.Square), reads=[hb_], writes=[qb_])
            S.op("dve", lambda e: e.tensor_reduce(out=s_[:, :, 8:12], in_=h_[:], axis=AX.X, op=ALU.add), reads=[hb_], writes=[sb_])
            S.op("dve", lambda e: e.tensor_reduce(out=s_[:, :, 12:16], in_=q_[:], axis=AX.X, op=ALU.add), reads=[qb_], writes=[sb_])
            S.op("dve", lambda e: e.tensor_scalar(out=s_[:, :, 16:20], in0=s_[:, :, 8:12], scalar1=1.0 / 64, scalar2=None,
                                                  op0=ALU.mult), reads=[sb_], writes=[sb_])
            S.op("dve", lambda e: e.tensor_tensor(out=s_[:, :, 20:24], in0=s_[:, :, 16:20], in1=s_[:, :, 16:20], op=ALU.mult),
                 reads=[sb_], writes=[sb_])
            S.op("dve", lambda e: e.scalar_tensor_tensor(out=s_[:, :, 24:28], in0=s_[:, :, 12:16], scalar=1.0 / 64,
                                                         in1=s_[:, :, 20:24], op0=ALU.mult, op1=ALU.subtract),
                 reads=[sb_], writes=[sb_])
            S.op("dve", lambda e: e.tensor_scalar(out=s_[:, :, 24:28], in0=s_[:, :, 24:28], scalar1=0.0, scalar2=RMS_EPS,
                                                  op0=ALU.max, op1=ALU.add), reads=[sb_], writes=[sb_])
            S.op("act", lambda e: e.activation(out=s_[:, :, 28:32], in_=s_[:, :, 24:28], func=AF.Sqrt), reads=[sb_], writes=[sb_])
            S.op("dve", lambda e: e.reciprocal(out=s_[:, :, 28:32], in_=s_[:, :, 28:32]), reads=[sb_], writes=[sb_])
            S.op("dve", lambda e: e.tensor_tensor(out=h_[:], in0=h_[:],
                                                  in1=s_[:, :, 16:20].unsqueeze(3).broadcast_to([128, 2, 4, 64]),
                                                  op=ALU.subtract), reads=[hb_, sb_], writes=[hb_])
            S.op("dve", lambda e: e.tensor_tensor(out=h_[:], in0=h_[:],
                                                  in1=s_[:, :, 28:32].unsqueeze(3).broadcast_to([128, 2, 4, 64]),
                                                  op=ALU.mult), reads=[hb_, sb_], writes=[hb_])
            hf = h_[:].rearrange("p n h d -> p n (h d)")
            S.op("dve", lambda e: e.tensor_tensor(out=hf, in0=hf, in1=ng[:].unsqueeze(1).broadcast_to([128, 2, 256]), op=ALU.mult),
                 reads=[hb_, ngb], writes=[hb_])
            y_, yb_ = yst.next()
            S.op("dve", lambda e: e.tensor_tensor(out=y_[:], in0=hf, in1=og[:, c0:c0 + 2, :], op=ALU.mult),
                 reads=[hb_, ogb], writes=[yb_])
            S.dma("pool", SC["y"][seq, c0 * 128:(c0 + 2) * 128, 0:256].rearrange("(n p) c -> p n c", p=128), y_[:], reads=[yb_])
        S.barrier()


def phase_dil(S, W, CS, SC, l, seq, T, jrev, jb):
    with ExitStack() as st:
        C = Ctx(S, st)
        qk, qkb = C.sb([128, 4, T], BF16, "dqk")
        Rd, Rdb = C.sb([128, 3, 4, 256], BF16, "Rd")
        vt = [C.sb([128, 260], BF16, "dvt") for _ in range(6)]
        PT = Rot([C.sb([128, 256], BF16, "dPT") for _ in range(6)])
        PTr = Rot([C.sb([128, 256], BF16, "dPTr") for _ in range(4)])
        Ed, Edb = C.sb([128, 3, 4, 256], BF16, "Ed")
        ost = Rot([C.sb([128, 260], F32, "dost") for _ in range(3)])
        pscs = [Rot([C.ps([128, 512], F32, "dpsc") for _ in range(2)]) for _ in range(2)]
        po = Rot([C.ps([128, 512], F32, "dpo") for _ in range(2)])
        for c in range(4):
            S.dma("sp", qk[:, c, :], SC["qkT"][seq, (4 + c) * 128:(5 + c) * 128, :], writes=[qkb])
        gv = SC["gvec"]
        for p in range(3):
            for hd in range(4):
                src = bass.AP(gv.tensor, gv.offset + hd * NG + p * 384, [[1, 128], [1, 256]])
                S.dma("sp", Rd[:, p, hd, :], src, writes=[Rdb])
        for p in range(3):
            for hd in range(4):
                ps_, psb_ = pscs[hd % 2].next()
                S.op("pe", lambda e: e.matmul(ps_[:, 0:256], lhsT=jrev[:], rhs=Rd[:, p, hd, :], start=True, stop=True),
                     reads=[jb, Rdb], writes=[psb_])
                S.op("act", lambda e: e.activation(out=Ed[:, p, hd, :], in_=ps_[:, 0:256], func=AF.Exp),
                     reads=[psb_], writes=[Edb])
        for p, (w, d) in enumerate(DIL):
            L = T // d
            ntl = L // 128
            for r in range(d):
                grp = {}

                def dscores(i, hd):
                    t0 = r + d * 128 * i
                    ks = [i, i - 1] if i > 0 else [i]
                    nk = len(ks)
                    if hd == 0:
                        for ii in ([0, 1, 2] if i == 0 else [i + 2]):
                            if ii < ntl:
                                v_, vb_ = vt[ii % 6]
                                tt = r + d * 128 * ii
                                S.dma("sp", v_[:], SC["dv1"][seq, tt:tt + d * 127 + 1:d, :], writes=[vb_])
                    ch, rs = hd // 2, slice((hd % 2) * 64, (hd % 2) * 64 + 64)
                    ps_, psb_ = pscs[hd % 2].next()
                    qsl = qk[rs, ch, t0:t0 + d * 127 + 1:d]
                    for jj, j in enumerate(ks):
                        k0 = r + d * 128 * j
                        S.op("pe", lambda e: e.matmul(ps_[:, jj * 128:(jj + 1) * 128], lhsT=qk[rs, 2 + ch, k0:k0 + d * 127 + 1:d],
                                                      rhs=qsl, start=True, stop=True), reads=[qkb], writes=[psb_])
                    pr_, prb_ = PTr.next()
                    S.op("act", lambda e: e.activation(out=pr_[:, 0:nk * 128], in_=ps_[:, 0:nk * 128], func=AF.Exp),
                         reads=[psb_], writes=[prb_])
                    p_, pb_ = PT.next()
                    S.op("dve", lambda e: e.tensor_tensor(out=p_[:, 0:nk * 128], in0=pr_[:, 0:nk * 128],
                                                          in1=Ed[:, p, hd, 0:nk * 128], op=ALU.mult),
                         reads=[prb_, Edb], writes=[pb_])
                    return p_, pb_

                def dpv(i, hd, p_, pb_):
                    t0 = r + d * 128 * i
                    ks = [i, i - 1] if i > 0 else [i]
                    nk = len(ks)
                    if hd == 0:
                        grp[i] = po.next()
                    po_, pob_ = grp[i]
                    for jj, j in enumerate(ks):
                        vj, vjb = vt[j % 6]
                        S.op("pe", lambda e: e.matmul(po_[:, hd * 65:(hd + 1) * 65], lhsT=p_[:, jj * 128:(jj + 1) * 128],
                                                      rhs=vj[:, hd * 65:(hd + 1) * 65], start=(jj == 0), stop=(jj == nk - 1)),
                             reads=[pb_, vjb], writes=[pob_])
                    if hd == 3:
                        o_, ob_ = ost.next()
                        S.op("dve", lambda e: e.tensor_copy(out=o_[:], in_=po_[:, 0:260]), reads=[pob_], writes=[ob_])
                        S.dma("pool", SC["dacc"][seq, p, t0:t0 + d * 127 + 1:d, :], o_[:], reads=[ob_])
                        del grp[i]

                units = [(i, hd) for i in range(ntl) for hd in range(4)]
                LA = 3
                pend = {}
                nxt = 0
                for u in range(len(units)):
                    while nxt < len(units) and nxt <= u + LA:
                        pend[nxt] = dscores(*units[nxt])
                        nxt += 1
                    dpv(*units[u], *pend.pop(u))
        S.barrier()
        ld = Rot([C.sb([128, 3, 260], F32, "dld") for _ in range(2)])
        yst = Rot([C.sb([128, 256], BF16, "dyst") for _ in range(2)])
        rd = Rot([C.sb([128, 4], F32, "drd") for _ in range(2)])
        for n in range(T // 128):
            rows = slice(n * 128, (n + 1) * 128)
            a_, ab_ = ld.next()
            S.dma("sp", a_[:], SC["dacc"][seq, :, rows, :].rearrange("t p c -> p t c"), writes=[ab_])
            S.op("dve", lambda e: e.tensor_tensor(out=a_[:, 0, :], in0=a_[:, 0, :], in1=a_[:, 1, :], op=ALU.add),
                 reads=[ab_], writes=[ab_])
            S.op("dve", lambda e: e.tensor_tensor(out=a_[:, 0, :], in0=a_[:, 0, :], in1=a_[:, 2, :], op=ALU.add),
                 reads=[ab_], writes=[ab_])
            r_, rb_ = rd.next()
            av = a_[:, 0, :].rearrange("p (h d) -> p h d", h=4)
            S.op("dve", lambda e: e.reciprocal(out=r_[:].unsqueeze(2), in_=av[:, :, 64:65]), reads=[ab_], writes=[rb_])
            y_, yb_ = yst.next()
            S.op("dve", lambda e: e.tensor_tensor(out=y_[:].rearrange("p (h d) -> p h d", h=4), in0=av[:, :, 0:64],
                                                  in1=r_[:].unsqueeze(2).broadcast_to([128, 4, 64]), op=ALU.mult),
                 reads=[ab_, rb_], writes=[yb_])
            S.dma("pool", SC["y"][seq, rows, 512:768], y_[:], reads=[yb_])
        S.barrier()


def phase_diff(S, W, CS, SC, l, seq, T, jrev, jb):
    NCH = T // 128
    lam_init = 0.8 - 0.6 * math.exp(-0.3 * l)
    with ExitStack() as st:
        C = Ctx(S, st)
        qk, qkb = C.sb([64, 8, T], BF16, "fqk")
        v1, v1b = C.sb([128, NCH, 260], BF16, "fv1")
        Rf, Rfb = C.sb([128, 4, 3072], BF16, "Rf")
        Ef, Efb = C.sb([128, 4, 3072], BF16, "Ef")
        PTr = Rot([C.sb([128, 512], BF16, "fPTr") for _ in range(6)])
        lv, lvb = C.sb([1, 160], F32, "lv")
        ones1, o1b = C.sb([1, 128], F32, "ones1")
        nlam, nlb = C.sb([128, 1], F32, "nlam")
        subg, sgb = C.sb([128, 64], F32, "subg")
        zl, zlb = C.sb([1, 128], BF16, "zl")
        zr, zrb = C.sb([1, 260], BF16, "zr")
        PT = Rot([C.sb([128, 512], BF16, "fPT") for _ in range(10)])
        om = [C.sb([128, 4, 65], F32, "om") for _ in range(2)]
        om2 = [C.sb([128, 4, 64], F32, "om2") for _ in range(2)]
        rdn = Rot([C.sb([128, 8], F32, "rdn") for _ in range(4)])
        od = Rot([C.sb([128, 4, 64], F32, "od") for _ in range(2)])
        sq = Rot([C.sb([128, 4, 64], F32, "fsq") for _ in range(2)])
        yst = Rot([C.sb([128, 4, 64], BF16, "fyst") for _ in range(2)])
        pscs = [Rot([C.ps([128, 512], F32, "fpsc") for _ in range(2)]) for _ in range(2)]
        po = Rot([C.ps([128, 512], F32, "fpo") for _ in range(4)])
        plam = po.items[0]

        for c in range(8):
            S.dma("sp", qk[:, c, :], SC["qkT"][seq, 1024 + c * 64:1024 + (c + 1) * 64, :], writes=[qkb])
        load_tok(S, v1, v1b, SC["fv1"][seq], NCH)
        gv = SC["gvec"]
        for hd in range(4):
            for part in range(2):
                src = bass.AP(gv.tensor, gv.offset + (4 + hd) * NG + 1152 + part * 1536, [[1, 128], [1, 1536]])
                S.dma("sp", Rf[:, hd, part * 1536:(part + 1) * 1536], src, writes=[Rfb])
        for hd in range(4):
            for n in range(6):
                ps_, psb_ = pscs[n % 2].next()
                S.op("pe", lambda e: e.matmul(ps_[:], lhsT=jrev[:], rhs=Rf[:, hd, n * 512:(n + 1) * 512], start=True, stop=True),
                     reads=[jb, Rfb], writes=[psb_])
                S.op("act", lambda e: e.activation(out=Ef[:, hd, n * 512:(n + 1) * 512], in_=ps_[:], func=AF.Exp),
                     reads=[psb_], writes=[Efb])
        S.dma("sp", lv[:, 0:128], W["diff_lambda"][l].rearrange("(o a) b -> o (a b)", o=1), writes=[lvb])
        S.op("dve", lambda e: e.tensor_tensor(out=lv[:, 128:160], in0=lv[:, 0:32], in1=lv[:, 32:64], op=ALU.mult),
             reads=[lvb], writes=[lvb])
        S.op("dve", lambda e: e.tensor_reduce(out=lv[:, 0:1], in_=lv[:, 128:160], axis=AX.X, op=ALU.add), reads=[lvb], writes=[lvb])
        S.op("dve", lambda e: e.tensor_tensor(out=lv[:, 128:160], in0=lv[:, 64:96], in1=lv[:, 96:128], op=ALU.mult),
             reads=[lvb], writes=[lvb])
        S.op("dve", lambda e: e.tensor_reduce(out=lv[:, 1:2], in_=lv[:, 128:160], axis=AX.X, op=ALU.add), reads=[lvb], writes=[lvb])
        S.op("act", lambda e: e.activation(out=lv[:, 2:4], in_=lv[:, 0:2], func=AF.Exp), reads=[lvb], writes=[lvb])
        S.op("dve", lambda e: e.scalar_tensor_tensor(out=lv[:, 4:5], in0=lv[:, 3:4], scalar=-float(lam_init), in1=lv[:, 2:3],
                                                     op0=ALU.add, op1=ALU.subtract), reads=[lvb], writes=[lvb])
        S.op("dve", lambda e: e.memset(ones1[:], 1.0), writes=[o1b])
        pl, plb = plam
        S.op("pe", lambda e: e.matmul(pl[:, 0:1], lhsT=ones1[:], rhs=lv[:, 4:5], start=True, stop=True),
             reads=[o1b, lvb], writes=[plb])
        S.op("act", lambda e: e.copy(out=nlam[:], in_=pl[:, 0:1]), reads=[plb], writes=[nlb])
        S.dma("sp", subg[:], bcast_row(W["diff_subln"][l], 128), writes=[sgb])
        S.op("dve", lambda e: e.tensor_scalar(out=subg[:], in0=subg[:], scalar1=float(1.0 - lam_init), scalar2=None,
                                              op0=ALU.mult), reads=[sgb], writes=[sgb])
        S.op("dve", lambda e: e.memset(zl[:], 0.0), writes=[zlb])
        S.op("dve", lambda e: e.memset(zr[:], 0.0), writes=[zrb])

        for i in range(T // 512):
            for hd in range(4):
                pos = [po.next(), po.next()]
                for m in range(2):
                    S.op("pe", lambda e: e.matmul(pos[m][0][:, 0:260], lhsT=zl[:], rhs=zr[:], start=True, stop=False),
                         reads=[zlb, zrb], writes=[pos[m][1]])
                nk = 4 * i + 4

                def scores(j):
                    jj = j - 4 * i
                    q0 = max(jj, 0) * 128
                    o = 512 * i - 128 * j
                    x0 = min(o, 2176) + 384 + q0
                    pss = []
                    for m in range(2):
                        rs = slice(32 * m, 32 * m + 32)
                        ps_, psb_ = pscs[m].next()
                        S.op("pe", lambda e: e.matmul(ps_[:, q0:512], lhsT=qk[rs, 4 + hd, j * 128:(j + 1) * 128],
                                                      rhs=qk[rs, hd, i * 512 + q0:(i + 1) * 512], start=True, stop=True),
                             reads=[qkb], writes=[psb_])
                        pss.append((ps_, psb_))
                    prs = []
                    for m in range(2):
                        ps_, psb_ = pss[m]
                        pr_, prb_ = PTr.next()
                        S.op("act", lambda e: e.activation(out=pr_[:, q0:512], in_=ps_[:, q0:512], func=AF.Exp),
                             reads=[psb_], writes=[prb_])
                        prs.append((pr_, prb_))
                    outs = []
                    for m in range(2):
                        pr_, prb_ = prs[m]
                        p_, pb_ = PT.next()
                        S.op("dve", lambda e: e.tensor_tensor(out=p_[:, q0:512], in0=pr_[:, q0:512],
                                                              in1=Ef[:, hd, x0:x0 + 512 - q0], op=ALU.mult),
                             reads=[prb_, Efb], writes=[pb_])
                        outs.append((p_, pb_))
                    return outs, q0

                def pv(j, outs, q0):
                    for m in range(2):
                        p_, pb_ = outs[m]
                        po_, pob_ = pos[m]
                        for s in range(q0 // 128, 4):
                            S.op("pe", lambda e: e.matmul(po_[:, s * 65:(s + 1) * 65], lhsT=p_[:, s * 128:(s + 1) * 128],
                                                          rhs=v1[:, j, hd * 65:(hd + 1) * 65], start=False,
                                                          stop=(j == 4 * i + s)), reads=[pb_, v1b], writes=[pob_])

                LA = 2
                pend = {}
                nxt = 0
                for j in range(nk):
                    while nxt < nk and nxt <= j + LA:
                        pend[nxt] = scores(nxt)
                        nxt += 1
                    pv(j, *pend.pop(j))
                for m in range(2):
                    po_, pob_ = pos[m]
                    o_, ob_ = om[m]
                    S.op("act", lambda e: e.copy(out=o_[:], in_=po_[:, 0:260].rearrange("p (s d) -> p s d", s=4)),
                         reads=[pob_], writes=[ob_])
                    r_, rb_ = rdn.next()
                    S.op("dve", lambda e: e.reciprocal(out=r_[:, 0:4].unsqueeze(2), in_=o_[:, :, 64:65]), reads=[ob_], writes=[rb_])
                    o2_, o2b_ = om2[m]
                    S.op("dve", lambda e: e.tensor_tensor(out=o2_[:], in0=o_[:, :, 0:64],
                                                          in1=r_[:, 0:4].unsqueeze(2).broadcast_to([128, 4, 64]), op=ALU.mult),
                         reads=[ob_, rb_], writes=[o2b_])
                d_, db_ = od.next()
                S.op("dve", lambda e: e.scalar_tensor_tensor(out=d_[:], in0=om2[1][0][:], scalar=nlam[:, 0:1], in1=om2[0][0][:],
                                                             op0=ALU.mult, op1=ALU.add),
                     reads=[om2[0][1], om2[1][1], nlb], writes=[db_])
                q_, qb_ = sq.next()
                S.op("act", lambda e: e.activation(out=q_[:], in_=d_[:], func=AF.Square), reads=[db_], writes=[qb_])
                r_, rb_ = rdn.next()
                S.op("dve", lambda e: e.tensor_reduce(out=r_[:, 0:4], in_=q_[:], axis=AX.X, op=ALU.add), reads=[qb_], writes=[rb_])
                S.op("dve", lambda e: e.tensor_scalar(out=r_[:, 0:4], in0=r_[:, 0:4], scalar1=1.0 / 64, scalar2=1e-5,
                                                      op0=ALU.mult, op1=ALU.add), reads=[rb_], writes=[rb_])
                S.op("act", lambda e: e.activation(out=r_[:, 4:8], in_=r_[:, 0:4], func=AF.Sqrt), reads=[rb_], writes=[rb_])
                S.op("dve", lambda e: e.reciprocal(out=r_[:, 4:8], in_=r_[:, 4:8]), reads=[rb_], writes=[rb_])
                S.op("dve", lambda e: e.tensor_tensor(out=d_[:], in0=d_[:], in1=r_[:, 4:8].unsqueeze(2).broadcast_to([128, 4, 64]),
                                                      op=ALU.mult), reads=[db_, rb_], writes=[db_])
                y_, yb_ = yst.next()
                S.op("dve", lambda e: e.tensor_tensor(out=y_[:], in0=d_[:], in1=subg[:].unsqueeze(1).broadcast_to([128, 4, 64]),
                                                      op=ALU.mult), reads=[db_, sgb], writes=[yb_])
                S.dma("pool", SC["y"][seq, i * 512:(i + 1) * 512, 768 + hd * 64:832 + hd * 64].rearrange("(s p) c -> p s c", p=128),
                      y_[:], reads=[yb_])
        S.barrier()


def phase_outproj(S, W, SC, l, NSEQ, T, h, ident, ibuf):
    with ExitStack() as st:
        C = Ctx(S, st)
        wo, wob = C.sb([128, 8, D], BF16, "wout")
        load_w_bf16(S, wo, wob, W["w_out"][l], 8, D)
        yt = Rot([C.sb([128, D], BF16, "yt") for _ in range(2)])
        yT = Rot([C.sb([128, 8, 128], BF16, "yT") for _ in range(2)])
        ht = Rot([C.sb([128, D], F32, "oht") for _ in range(3)])
        ptp = Rot([C.ps([128, 1024], BF16, "optp") for _ in range(2)])
        pso = Rot([C.ps([128, 512], F32, "opso") for _ in range(4)])
        for seq in range(NSEQ):
            for n in range(T // 128):
                rows = slice(n * 128, (n + 1) * 128)
                hrows = slice(seq * T + n * 128, seq * T + (n + 1) * 128)
                y_, yb_ = yt.next()
                S.dma("sp", y_[:], SC["y"][seq, rows, :], writes=[yb_])
                h_, hb_ = ht.next()
                S.dma("sp", h_[:], h[hrows, :], writes=[hb_])
                pt, ptb = ptp.next()
                for c in range(8):
                    S.op("pe", lambda e: e.transpose(out=pt[:, c * 128:(c + 1) * 128], in_=y_[:, c * 128:(c + 1) * 128],
                                                     identity=ident[:]), reads=[yb_, ibuf], writes=[ptb])
                yT_, yTb_ = yT.next()
                S.op("act", lambda e: e.copy(out=yT_[:, 0:4, :], in_=pt[:, 0:512].rearrange("p (c t) -> p c t", c=4)),
                     reads=[ptb], writes=[yTb_])
                S.op("dve", lambda e: e.tensor_copy(out=yT_[:, 4:8, :], in_=pt[:, 512:1024].rearrange("p (c t) -> p c t", c=4)),
                     reads=[ptb], writes=[yTb_])
                for nn in range(2):
                    p_, pb_ = pso.next()
                    for k in range(8):
                        S.op("pe", lambda e: e.matmul(p_[:], lhsT=yT_[:, k, :], rhs=wo[:, k, nn * 512:(nn + 1) * 512],
                                                      start=(k == 0), stop=(k == 7)), reads=[yTb_, wob], writes=[pb_])
                    S.op("dve", lambda e: e.tensor_tensor(out=h_[:, nn * 512:(nn + 1) * 512], in0=p_[:],
                                                          in1=h_[:, nn * 512:(nn + 1) * 512], op=ALU.add),
                         reads=[pb_, hb_], writes=[hb_])
                S.dma("pool", h[hrows, :], h_[:], reads=[hb_])
        S.barrier()


def phase_xattn(S, W, l, NSEQ, T, MEM, h, mem, ident, ibuf, xw):
    NB = T // 512
    NMT = MEM // 128
    with ExitStack() as st:
        C = Ctx(S, st)
        (wq, wqb), (wkv, wkvb), (wo, wob) = xw
        xg, xgb = C.sb([128, D], F32, "xg")
        mg, mgb = C.sb([128, D], F32, "mg")
        res = norm_res(C)
        uTs = [C.sb([128, 8, 512], BF16, "xuT") for _ in range(2)]
        mT, mTb = C.sb([128, 8, MEM], BF16, "mT")
        KT, KTb = C.sb([128, 8, MEM], BF16, "KT")
        V1, V1b = C.sb([128, NMT, 4, 257], BF16, "V1")
        qT, qTb = C.sb([128, 8, 512], BF16, "qT")
        PT = Rot([C.sb([128, 512], BF16, "xPT") for _ in range(6)])
        osb, osbb = C.sb([128, 4, D], BF16, "osb")
        oT, oTb = C.sb([128, 8, 512], BF16, "oT")
        rd = Rot([C.sb([128, 1], F32, "xrd") for _ in range(4)])
        ht = Rot([C.sb([128, D], F32, "xht") for _ in range(8)])
        oTbs = [Buf() for _ in range(4)]
        psq = Rot([C.ps([128, 512], F32, "psq") for _ in range(2)])
        pov = Rot([C.ps([128, 512], F32, "pov") for _ in range(4)])
        pwo = pov
        S.dma("sp", xg[:], bcast_row(W["xattn_norm"][l], 128), writes=[xgb])
        S.dma("sp", mg[:], bcast_row(W["mem_norm"][l], 128), writes=[mgb])
        S.op("dve", lambda e: e.memset(V1[:], 1.0), writes=[V1b])
        for seq in range(NSEQ):
            norm_block(S, C, mem, seq * MEM, NMT, mg, mgb, ident, ibuf, mT, mTb, res)
            for c in range(8):
                p_, pb_ = psq.next()
                for k in range(8):
                    S.op("pe", lambda e: e.matmul(p_[:, 0:MEM], lhsT=wkv[:, k, c * 128:(c + 1) * 128], rhs=mT[:, k, :],
                                                  start=(k == 0), stop=(k == 7)), reads=[wkvb, mTb], writes=[pb_])
                S.op("act", lambda e: e.copy(out=KT[:, c, :], in_=p_[:, 0:MEM]), reads=[pb_], writes=[KTb])
            for j in range(NMT):
                for nn in range(2):
                    p_, pb_ = psq.next()
                    for k in range(8):
                        S.op("pe", lambda e: e.matmul(p_[:], lhsT=mT[:, k, j * 128:(j + 1) * 128],
                                                      rhs=wkv[:, k, D + nn * 512:D + (nn + 1) * 512],
                                                      start=(k == 0), stop=(k == 7)), reads=[wkvb, mTb], writes=[pb_])
                    S.op("act", lambda e: e.copy(out=V1[:, j, 2 * nn:2 * nn + 2, 0:256],
                                                 in_=p_[:].rearrange("p (h d) -> p h d", h=2)),
                         reads=[pb_], writes=[V1b])
            norm_block(S, C, h, seq * T, 4, xg, xgb, ident, ibuf, uTs[0][0], uTs[0][1], res)
            for b in range(NB):
                uT, uTb = uTs[b % 2]
                tok0 = seq * T + b * 512
                hts = [ht.next() for _ in range(4)]
                for s4 in range(4):
                    S.dma("sp", hts[s4][0][:], h[tok0 + s4 * 128:tok0 + (s4 + 1) * 128, :], writes=[hts[s4][1]])
                for c in range(8):
                    p_, pb_ = psq.next()
                    for k in range(8):
                        S.op("pe", lambda e: e.matmul(p_[:], lhsT=wq[:, k, c * 128:(c + 1) * 128], rhs=uT[:, k, :],
                                                      start=(k == 0), stop=(k == 7)), reads=[wqb, uTb], writes=[pb_])
                    S.op("act", lambda e: e.activation(out=qT[:, c, :], in_=p_[:], func=AF.Copy, scale=1.0 / 16),
                         reads=[pb_], writes=[qTb])
                def xscores(hd):
                    pts = []
                    for j in range(NMT):
                        p_, pb_ = psq.next()
                        for cc in range(2):
                            S.op("pe", lambda e: e.matmul(p_[:], lhsT=KT[:, 2 * hd + cc, j * 128:(j + 1) * 128],
                                                          rhs=qT[:, 2 * hd + cc, :], start=(cc == 0), stop=(cc == 1)),
                                 reads=[KTb, qTb], writes=[pb_])
                        t_, tb_ = PT.next()
                        S.op("act", lambda e: e.activation(out=t_[:], in_=p_[:], func=AF.Exp), reads=[pb_], writes=[tb_])
                        pts.append((t_, tb_))
                    return pts

                def xpv(hd, pts):
                    for s in range(4):
                        p_, pb_ = pov.next()
                        for j in range(NMT):
                            t_, tb_ = pts[j]
                            S.op("pe", lambda e: e.matmul(p_[:, 0:257], lhsT=t_[:, s * 128:(s + 1) * 128], rhs=V1[:, j, hd, :],
                                                          start=(j == 0), stop=(j == NMT - 1)), reads=[tb_, V1b], writes=[pb_])
                        r_, rb_ = rd.next()
                        S.op("dve", lambda e: e.reciprocal(out=r_[:], in_=p_[:, 256:257]), reads=[pb_], writes=[rb_])
                        S.op("act", lambda e: e.activation(out=osb[:, s, hd * 256:(hd + 1) * 256], in_=p_[:, 0:256],
                                                           func=AF.Identity, scale=r_[:, 0:1]),
                             reads=[pb_, rb_], writes=[osbb])

                prevp = xscores(0)
                for hd in range(1, 4):
                    curp = xscores(hd)
                    xpv(hd - 1, prevp)
                    prevp = curp
                xpv(3, prevp)
                if b + 1 < NB:
                    norm_block(S, C, h, tok0 + 512, 4, xg, xgb, ident, ibuf, uTs[(b + 1) % 2][0], uTs[(b + 1) % 2][1], res)
                for s in range(4):
                    pt, ptb = res["ptp"][s % 2]
                    for c in range(8):
                        S.op("pe", lambda e: e.transpose(out=pt[:, c * 128:(c + 1) * 128], in_=osb[:, s, c * 128:(c + 1) * 128],
                                                         identity=ident[:]), reads=[osbb, ibuf], writes=[ptb])
                    S.op("act", lambda e: e.copy(out=oT[:, 0:4, s * 128:(s + 1) * 128],
                                                 in_=pt[:, 0:512].rearrange("p (c t) -> p c t", c=4)), reads=[ptb], writes=[oTbs[s]])
                    S.op("dve", lambda e: e.tensor_copy(out=oT[:, 4:8, s * 128:(s + 1) * 128],
                                                        in_=pt[:, 512:1024].rearrange("p (c t) -> p c t", c=4)),
                         reads=[ptb], writes=[oTbs[s]])
                for s in range(4):
                    hrows = slice(tok0 + s * 128, tok0 + (s + 1) * 128)
                    h_, hb_ = hts[s]
                    for nn in range(2):
                        p_, pb_ = pwo.next()
                        for k in range(8):
                            S.op("pe", lambda e: e.matmul(p_[:], lhsT=oT[:, k, s * 128:(s + 1) * 128],
                                                          rhs=wo[:, k, nn * 512:(nn + 1) * 512], start=(k == 0), stop=(k == 7)),
                                 reads=[oTbs[s], wob], writes=[pb_])
                        S.op("dve", lambda e: e.tensor_tensor(out=h_[:, nn * 512:(nn + 1) * 512], in0=p_[:],
                                                              in1=h_[:, nn * 512:(nn + 1) * 512], op=ALU.add),
                             reads=[pb_, hb_], writes=[hb_])
                    S.dma("pool", h[hrows, :], h_[:], reads=[hb_])
        S.barrier()


def phase_final(S, W, ntok, h, out):
    with ExitStack() as st:
        C = Ctx(S, st)
        gain, gb = C.sb([128, D], F32, "fgain")
        S.dma("sp", gain[:], bcast_row(W["final_norm"], 128), writes=[gb])
        ht = Rot([C.sb([128, D], F32, "fh") for _ in range(3)])
        jk = Rot([C.sb([128, D], BF16, "fjk") for _ in range(2)])
        ss = Rot([C.sb([128, 4], F32, "fss") for _ in range(3)])
        for n in range(ntok // 128):
            rows = slice(n * 128, (n + 1) * 128)
            h_, hb_ = ht.next()
            j_, jb_ = jk.next()
            s_, sb_ = ss.next()
            S.dma("sp", h_[:], h[rows, :], writes=[hb_])
            S.op("act", lambda e: e.activation(out=j_[:], in_=h_[:], func=AF.Square, accum_out=s_[:, 0:1]),
                 reads=[hb_], writes=[jb_, sb_])
            S.op("dve", lambda e: e.tensor_scalar(out=s_[:, 1:2], in0=s_[:, 0:1], scalar1=1.0 / D, scalar2=RMS_EPS,
                                                  op0=ALU.mult, op1=ALU.add), reads=[sb_], writes=[sb_])
            S.op("act", lambda e: e.activation(out=s_[:, 2:3], in_=s_[:, 1:2], func=AF.Sqrt), reads=[sb_], writes=[sb_])
            S.op("dve", lambda e: e.reciprocal(out=s_[:, 3:4], in_=s_[:, 2:3]), reads=[sb_], writes=[sb_])
            S.op("dve", lambda e: e.scalar_tensor_tensor(out=h_[:], in0=h_[:], scalar=s_[:, 3:4], in1=gain[:],
                                                         op0=ALU.mult, op1=ALU.mult), reads=[hb_, sb_, gb], writes=[hb_])
            S.dma("pool", out[rows, :], h_[:], reads=[hb_])
        S.barrier()


WEIGHT_SHAPES = {
    "t5_bias": (32, 8), "ffn1_norm": (4, D), "ffn1_w_gate": (4, D, DFF), "ffn1_w_up": (4, D, DFF),
    "ffn1_w_down": (4, DFF, D), "mix_norm": (4, D), "w_in": (4, D, INW), "mlstm_conv_w": (4, 4, 512),
    "mlstm_conv_b": (4, 512), "mlstm_gate_b": (4, 8), "mlstm_norm": (4, 256), "pool_w": (4, 4, 64, 64),
    "pool_scale": (4, 256), "diff_lambda": (4, 4, 32), "diff_subln": (4, 64), "w_out": (4, D, D),
    "xattn_norm": (4, D), "mem_norm": (4, D), "xattn_wq": (4, D, D), "xattn_wkv": (4, D, 2 * D),
    "xattn_wo": (4, D, D), "ffn2_norm": (4, D), "ffn2_w_gate": (4, D, DFF), "ffn2_w_up": (4, D, DFF),
    "ffn2_w_down": (4, DFF, D), "final_norm": (D,),
}

ALL_PHASES = ("ffn1", "mixproj", "mlstm", "dil", "diff", "outproj", "xattn", "ffn2", "final")


def build(T=4096, NSEQ=2, DEPTH=4, MEM=256, phases=ALL_PHASES, debug=()):
    nc = bass.Bass("TRN2", target_bir_lowering=False)
    ntok = T * NSEQ
    x = nc.dram_tensor("x", [ntok, D], F32, kind="ExternalInput").ap()
    mem = nc.dram_tensor("mem", [NSEQ * MEM, D], F32, kind="ExternalInput").ap()
    W = {k: nc.dram_tensor(k, list(s), F32, kind="ExternalInput").ap() for k, s in WEIGHT_SHAPES.items()}
    CS = {k: nc.dram_tensor(k, list(s), F32, kind="ExternalInput").ap() for k, s in CONST_SHAPES.items()}
    out = nc.dram_tensor("out", [ntok, D], F32, kind="ExternalOutput").ap()

    def scratch(name, shape, dt):
        kind = "ExternalOutput" if name in debug else "Internal"
        return nc.dram_tensor("s_" + name, list(shape), dt, kind=kind).ap()

    SC = {
        "qkT": scratch("qkT", [NSEQ, 12 * 128, T], BF16),
        "kg": scratch("kg", [NSEQ, T, 256], BF16),
        "mv1": scratch("mv1", [NSEQ, T, 260], BF16),
        "og": scratch("og", [NSEQ, T, 256], BF16),
        "ag": scratch("ag", [NSEQ, 128, T // 128, 8], F32),
        "eb": scratch("eb", [NSEQ, 128, T // 128, 4], F32),
        "dv1": scratch("dv1", [NSEQ, T, 260], BF16),
        "fv1": scratch("fv1", [NSEQ, T, 260], BF16),
        "dacc": scratch("dacc", [NSEQ, 3, T, 260], F32),
        "y": scratch("y", [NSEQ, T, D], BF16),
        "gvec": scratch("gvec", [8, NG], BF16),
        "dbg": scratch("dbg", [128, 64], F32),
    }
    h = out
    with ExitStack() as st:
        S = Sched(nc, st)
        C0 = Ctx(S, st)
        ident, ibuf, jrev, jb = setup_consts(S, C0, W, CS, SC)
        for l in range(DEPTH):
            if "ffn1" in phases:
                phase_ffn(S, W, "ffn1", l, x if l == 0 else h, h, ntok, ident, ibuf)
            if "mixproj" in phases:
                phase_mixproj(S, W, CS, SC, l, NSEQ, T, h, ident, ibuf)
            for seq in range(NSEQ):
                if "mlstm" in phases:
                    phase_mlstm(S, W, CS, SC, l, seq, T)
                if "dil" in phases:
                    phase_dil(S, W, CS, SC, l, seq, T, jrev, jb)
                if "diff" in phases:
                    phase_diff(S, W, CS, SC, l, seq, T, jrev, jb)
            with ExitStack() as xst:
                Cx = Ctx(S, xst)
                xw = None
                if "xattn" in phases:
                    xw = (Cx.sb([128, 8, D], BF16, "wq"), Cx.sb([128, 8, 2 * D], BF16, "wkv"), Cx.sb([128, 8, D], BF16, "wo"))
                    load_w_bf16(S, xw[0][0], xw[0][1], W["xattn_wq"][l], 8, D)
                    load_w_bf16(S, xw[1][0], xw[1][1], W["xattn_wkv"][l], 8, 2 * D)
                    load_w_bf16(S, xw[2][0], xw[2][1], W["xattn_wo"][l], 8, D)
                if "outproj" in phases:
                    phase_outproj(S, W, SC, l, NSEQ, T, h, ident, ibuf)
                if "xattn" in phases:
                    phase_xattn(S, W, l, NSEQ, T, MEM, h, mem, ident, ibuf, xw)
            if "ffn2" in phases:
                phase_ffn(S, W, "ffn2", l, h, h, ntok, ident, ibuf)
        if "final" in phases:
            phase_final(S, W, ntok, h, out)
        S.barrier()
        print("instructions:", S.ninst, {e: c for e, c in S.cnt.items()})
    return nc


_CONSTS = None


def kernel(**inputs):
    global _CONSTS
    if _CONSTS is None:
        _CONSTS = host_consts()
    x = np.ascontiguousarray(inputs["x"], dtype=np.float32)
    mem = np.ascontiguousarray(inputs["mem"], dtype=np.float32)
    B, T, _ = x.shape
    MEM = mem.shape[1]
    ncores = 8
    nseq = B // ncores
    nc = build(T=T, NSEQ=nseq, DEPTH=4, MEM=MEM)
    base = {k: np.ascontiguousarray(inputs[k], dtype=np.float32) for k in WEIGHT_SHAPES}
    base.update(_CONSTS)
    in_maps = []
    for c in range(ncores):
        m = dict(base)
        m["x"] = x[c * nseq:(c + 1) * nseq].reshape(nseq * T, D)
        m["mem"] = mem[c * nseq:(c + 1) * nseq].reshape(nseq * MEM, D)
        in_maps.append(m)
    res = run_bass_kernel_spmd(nc, in_maps, core_ids=list(range(ncores)))
    outs = [np.asarray(r["out"], dtype=np.float32).reshape(nseq, T, D) for r in res.results]
    return np.concatenate(outs, axis=0)
```

```python
import math
import os
MIXSTAGE = int(os.environ.get('MIXSTAGE', '9'))
MLSTAGE = int(os.environ.get('MLSTAGE', '9'))
from contextlib import ExitStack
import numpy as np
import concourse.bass as bass
import concourse.mybir as mybir
from concourse.bass_utils import run_bass_kernel_spmd

F32 = mybir.dt.float32
BF16 = mybir.dt.bfloat16
AF = mybir.ActivationFunctionType
ALU = mybir.AluOpType
AX = mybir.AxisListType

D = 1024
DFF = 2816
NFF = DFF // 128
INW = 2824
RMS_EPS = 1e-6
N_DMA_SEMS = 32


class Buf:
    __slots__ = ("w", "r")

    def __init__(self):
        self.w = None
        self.r = {}


class Sched:
    def __init__(self, nc, st):
        self.nc = nc
        self.eng = {"pe": nc.tensor, "act": nc.scalar, "dve": nc.vector, "pool": nc.gpsimd, "sp": nc.sync}
        self.sem = {}
        for e in self.eng:
            self.sem[e] = st.enter_context(nc.semaphore("s_" + e))
        for i in range(N_DMA_SEMS):
            self.sem[("d", i)] = st.enter_context(nc.semaphore("s_d%d" % i))
        self.cnt = {e: 0 for e in self.eng}
        self.dval = [0] * N_DMA_SEMS
        self.rr = 0
        self.rrq = [0, 0]
        self.known = {e: {} for e in self.eng}
        self.uid = 0
        self.ninst = 0

    def name(self, p):
        self.uid += 1
        return "%s_%d" % (p, self.uid)

    def _wait(self, e, deps):
        kn = self.known[e]
        for k, v in deps.items():
            if k == e and e == "pe":
                continue
            if kn.get(k, 0) >= v:
                continue
            self.eng[e].wait_ge(self.sem[k], v)
            self.ninst += 1
            kn[k] = v

    @staticmethod
    def _deps(reads, writes, deps):
        def add(ev):
            if ev is not None and deps.get(ev[0], 0) < ev[1]:
                deps[ev[0]] = ev[1]
        for b in reads:
            add(b.w)
        for b in writes:
            add(b.w)
            for k, v in b.r.items():
                add((k, v))

    @staticmethod
    def _mark(ev, reads, writes):
        for b in reads:
            if b.r.get(ev[0], 0) < ev[1]:
                b.r[ev[0]] = ev[1]
        for b in writes:
            b.w = ev
            b.r = {}

    def op(self, e, fn, reads=(), writes=()):
        deps = {}
        self._deps(reads, writes, deps)
        self._wait(e, deps)
        inst = fn(self.eng[e])
        self.cnt[e] += 1
        self.ninst += 1
        inst.then_inc(self.sem[e], 1)
        self._mark((e, self.cnt[e]), reads, writes)

    def dma(self, e, out, in_, reads=(), writes=(), **kw):
        half = N_DMA_SEMS // 2
        qi = 0 if e == "sp" else 1
        i = qi * half + self.rrq[qi]
        self.rrq[qi] = (self.rrq[qi] + 1) % half
        k = ("d", i)
        deps = {}
        if self.dval[i] > 0:
            deps[k] = self.dval[i]
        self._deps(reads, writes, deps)
        self._wait(e, deps)
        inst = self.eng[e].dma_start(out=out, in_=in_, **kw)
        self.ninst += 1
        self.dval[i] += 16
        inst.then_inc(self.sem[k], 16)
        self._mark((k, self.dval[i]), reads, writes)

    def barrier(self):
        deps = {e: c for e, c in self.cnt.items() if c > 0}
        for i in range(N_DMA_SEMS):
            if self.dval[i] > 0:
                deps[("d", i)] = self.dval[i]
        for e in self.eng:
            self._wait(e, dict(deps))


class Ctx:
    def __init__(self, S, st):
        self.S = S
        self.st = st
        self.nc = S.nc

    def sb(self, shape, dt, name="t"):
        t = self.st.enter_context(self.nc.sbuf_tensor(self.S.name(name), list(shape), dt))
        return t, Buf()

    def ps(self, shape, dt, name="p"):
        t = self.st.enter_context(self.nc.psum_tensor(self.S.name(name), list(shape), dt))
        return t, Buf()


def bcast_row(handle_ap, nparts):
    a = handle_ap
    return bass.AP(a.tensor, a.offset, [[0, nparts]] + [list(x) for x in a.ap])


def load_w_bf16(S, dst, dbuf, src, kchunks, ncols):
    for c in range(kchunks):
        S.dma("pool", dst[:, c, :], src[c * 128:(c + 1) * 128, :], writes=[dbuf], max_dma_last_dim=4096)


def norm_pre(S, C, h_src, r0, ntile, gain, gbuf, res, lq="sp"):
    hn, u, ss = res["hn"], res["u"], res["ss"]
    res["i"] = res.get("i", 0) + 1
    sst, ssb = ss[res["i"] % len(ss)]
    tiles = []
    us = []
    for i in range(ntile):
        res["j"] = res.get("j", 0) + 1
        ht, hb = hn[res["j"] % len(hn)]
        res["k"] = res.get("k", 0) + 1
        ut, ub = u[res["k"] % len(u)]
        rows = slice(r0 + i * 128, r0 + (i + 1) * 128)
        S.dma(lq, ht[:], h_src[rows, :], writes=[hb])
        S.op("act", lambda e: e.activation(out=ut[:], in_=ht[:], func=AF.Square, accum_out=sst[:, i:i + 1]),
             reads=[hb], writes=[ub, ssb])
        tiles.append((ht, hb))
        us.append((ut, ub))
    S.op("dve", lambda e: e.tensor_scalar(out=sst[:, 4:4 + ntile], in0=sst[:, 0:ntile], scalar1=1.0 / D, scalar2=RMS_EPS,
                                          op0=ALU.mult, op1=ALU.add), reads=[ssb], writes=[ssb])
    S.op("act", lambda e: e.activation(out=sst[:, 8:8 + ntile], in_=sst[:, 4:4 + ntile], func=AF.Sqrt), reads=[ssb], writes=[ssb])
    S.op("dve", lambda e: e.reciprocal(out=sst[:, 12:12 + ntile], in_=sst[:, 8:8 + ntile]), reads=[ssb], writes=[ssb])
    for i in range(ntile):
        ht, hb = tiles[i]
        ut, ub = us[i]
        S.op("dve", lambda e: e.scalar_tensor_tensor(out=ut[:], in0=ht[:], scalar=sst[:, 12 + i:13 + i], in1=gain[:],
                                                     op0=ALU.mult, op1=ALU.mult),
             reads=[hb, ssb, gbuf], writes=[ub])
    return us


def norm_post(S, us, ident, ibuf, uT, uTbuf, res):
    ptp = res["ptp"]
    for i, (ut, ub) in enumerate(us):
        res["m"] = res.get("m", 0) + 1
        pt, pb = ptp[res["m"] % len(ptp)]
        for c in range(8):
            S.op("pe", lambda e, c=c: e.transpose(out=pt[:, c * 128:(c + 1) * 128], in_=ut[:, c * 128:(c + 1) * 128],
                                                  identity=ident[:]), reads=[ub, ibuf], writes=[pb])
        S.op("act", lambda e: e.copy(out=uT[:, 0:4, i * 128:(i + 1) * 128],
                                     in_=pt[:, 0:512].rearrange("p (c t) -> p c t", c=4)),
             reads=[pb], writes=[uTbuf])
        S.op("dve", lambda e: e.tensor_copy(out=uT[:, 4:8, i * 128:(i + 1) * 128],
                                            in_=pt[:, 512:1024].rearrange("p (c t) -> p c t", c=4)),
             reads=[pb], writes=[uTbuf])


def norm_block(S, C, h_src, r0, ntile, gain, gbuf, ident, ibuf, uT, uTbuf, res, lq="sp"):
    us = norm_pre(S, C, h_src, r0, ntile, gain, gbuf, res, lq=lq)
    norm_post(S, us, ident, ibuf, uT, uTbuf, res)


def phase_ffn(S, W, pre, l, h_in, h_out, ntok, ident, ibuf):
    nc = S.nc
    NB = ntok // 512
    with ExitStack() as st:
        C = Ctx(S, st)
        wg, wgb = C.sb([128, 8, DFF], BF16, "wg")
        wu, wub = C.sb([128, 8, DFF], BF16, "wu")
        wd, wdb = C.sb([128, NFF, D], BF16, "wd")
        gain, gb = C.sb([128, D], F32, "gain")
        uT, uTb = C.sb([128, 8, 512], BF16, "uT")
        aT, aTb = C.sb([128, NFF, 512], BF16, "aT")
        res = norm_res(C)
        hr = [C.sb([128, D], F32, "hr") for _ in range(2)]
        sg = [C.sb([128, 512], F32, "sg") for _ in range(2)]
        psg = [C.ps([128, 512], F32, "psg") for _ in range(2)]
        psu = [C.ps([128, 512], F32, "psu") for _ in range(2)]
        psd = [C.ps([128, 512], F32, "psd") for _ in range(2)]

        S.dma("sp", gain[:], bcast_row(W[pre + "_norm"][l], 128), writes=[gb])
        load_w_bf16(S, wg, wgb, W[pre + "_w_gate"][l], 8, DFF)
        load_w_bf16(S, wu, wub, W[pre + "_w_up"][l], 8, DFF)
        load_w_bf16(S, wd, wdb, W[pre + "_w_down"][l], NFF, D)

        def npre(b):
            return norm_pre(S, C, h_in, b * 512, 4, gain, gb, res)

        def npost(us):
            norm_post(S, us, ident, ibuf, uT, uTb, res)

        def gateup(b):
            for f in range(NFF):
                pg, pgb = psg[f % 2]
                pu, pub = psu[f % 2]
                sgt, sgb = sg[f % 2]
                for k in range(8):
                    S.op("pe", lambda e, k=k: e.matmul(pg[:], lhsT=wg[:, k, f * 128:(f + 1) * 128], rhs=uT[:, k, :],
                                                       start=(k == 0), stop=(k == 7)),
                         reads=[wgb, uTb], writes=[pgb])
                for k in range(8):
                    S.op("pe", lambda e, k=k: e.matmul(pu[:], lhsT=wu[:, k, f * 128:(f + 1) * 128], rhs=uT[:, k, :],
                                                       start=(k == 0), stop=(k == 7)),
                         reads=[wub, uTb], writes=[pub])
                S.op("act", lambda e: e.activation(out=sgt[:], in_=pg[:], func=AF.Silu), reads=[pgb], writes=[sgb])
                S.op("dve", lambda e: e.tensor_tensor(out=aT[:, f, :], in0=sgt[:], in1=pu[:], op=ALU.mult),
                     reads=[sgb, pub], writes=[aTb])

        def down(b):
            for i in range(4):
                ht, hb = hr[i % 2]
                rows = slice(b * 512 + i * 128, b * 512 + (i + 1) * 128)
                S.dma("sp", ht[:], h_in[rows, :], writes=[hb])
                for n in range(2):
                    pd, pdb = psd[n]
                    for f in range(NFF):
                        S.op("pe", lambda e, f=f: e.matmul(pd[:], lhsT=aT[:, f, i * 128:(i + 1) * 128],
                                                           rhs=wd[:, f, n * 512:(n + 1) * 512],
                                                           start=(f == 0), stop=(f == NFF - 1)),
                             reads=[aTb, wdb], writes=[pdb])
                    S.op("dve", lambda e: e.scalar_tensor_tensor(out=ht[:, n * 512:(n + 1) * 512], in0=pd[:],
                                                                 scalar=0.5, in1=ht[:, n * 512:(n + 1) * 512],
                                                                 op0=ALU.mult, op1=ALU.add),
                         reads=[pdb, hb], writes=[hb])
                S.dma("pool", h_out[rows, :], ht[:], reads=[hb])

        npost(npre(0))
        for b in range(NB):
            us_next = npre(b + 1) if b + 1 < NB else None
            gateup(b)
            if us_next is not None:
                npost(us_next)
            down(b)
        S.barrier()


NEG = -30000.0
NG = 4608
DIL = ((128, 1), (512, 4), (2048, 16))


def t5_bucket_np(dist):
    dist = np.asarray(dist, dtype=np.int64)
    d = np.maximum(dist, 1).astype(np.float32)
    large = 16 + (np.log(d / np.float32(16)) / np.float32(math.log(2048 / 16)) * np.float32(16)).astype(np.int32)
    large = np.minimum(large, 31)
    return np.where(dist < 16, dist, large)


def host_consts():
    c = {}
    s = np.arange(128)
    c["c_tri"] = (s[:, None] <= s[None, :]).astype(np.float32)
    sel = np.zeros((128, 128), np.float32)
    sel[127, :] = 1
    c["c_sel"] = sel
    c["c_maskT"] = c["c_tri"] * np.float32(0.125)
    c["c_jrev"] = np.ascontiguousarray(np.eye(128, dtype=np.float32)[::-1])
    oh = np.zeros((33, NG), np.float32)
    for p, (w, d) in enumerate(DIL):
        jx = np.arange(384)
        dl = jx - 127
        valid = (dl >= 0) & (dl <= 128)
        bk = t5_bucket_np(np.clip(dl, 0, 128) * d)
        cols = p * 384 + jx
        oh[bk[valid], cols[valid]] = 1
        oh[32, cols[~valid]] = NEG
    jx = np.arange(NG - 1152)
    dl = jx - 511
    valid = dl >= 0
    bk = t5_bucket_np(np.clip(dl, 0, None))
    cols = 1152 + jx
    oh[bk[valid], cols[valid]] = 1
    oh[32, cols[~valid]] = NEG
    c["c_oh"] = oh
    wins = [2, 4, 8, 16]
    invc = np.zeros((128, 2, 2, 512), np.float32)
    t = np.arange(512)
    for cc in range(2):
        for half in range(2):
            w = wins[cc * 2 + half]
            rows = slice(half * 64, half * 64 + 64)
            invc[rows, 1, cc, :] = 1.0 / w
            invc[rows, 0, cc, :] = 1.0 / np.minimum(t + 1, w)
    c["c_invc"] = invc
    return c


CONST_SHAPES = {"c_tri": (128, 128), "c_sel": (128, 128), "c_maskT": (128, 128), "c_jrev": (128, 128),
                "c_oh": (33, NG), "c_invc": (128, 2, 2, 512)}


class Rot:
    def __init__(self, items):
        self.items = items
        self.i = 0

    def next(self):
        x = self.items[self.i % len(self.items)]
        self.i += 1
        return x


def col1(ap1d):
    return ap1d.rearrange("(p o) -> p o", o=1)


def load_tok(S, dst, dbuf, src, nch):
    for n0 in range(0, nch, 8):
        n1 = min(nch, n0 + 8)
        S.dma("sp", dst[:, n0:n1, :], src[n0 * 128:n1 * 128, :].rearrange("(n p) c -> p n c", p=128), writes=[dbuf])


def setup_consts(S, C, W, CS, SC):
    nc = S.nc
    idf, idfb = C.sb([128, 128], F32, "identf")
    ident, ibuf = C.sb([128, 128], BF16, "ident")
    jrev, jb = C.sb([128, 128], BF16, "jrev")
    S.op("pool", lambda e: e.memset(idf[:], 1.0), writes=[idfb])
    S.op("pool", lambda e: e.affine_select(out=idf[:], in_=idf[:], pattern=[[-1, 128]], compare_op=ALU.is_equal,
                                           fill=0.0, base=0, channel_multiplier=1), reads=[idfb], writes=[idfb])
    S.op("dve", lambda e: e.tensor_copy(out=ident[:], in_=idf[:]), reads=[idfb], writes=[ibuf])
    S.dma("pool", jrev[:], CS["c_jrev"], writes=[jb])
    with ExitStack() as st:
        C2 = Ctx(S, st)
        t5x, t5b = C2.sb([33, 8], F32, "t5x")
        oh, ohb = C2.sb([33, NG], F32, "oh")
        gsb, gsbb = C2.sb([8, NG], BF16, "gsb")
        pg = [C2.ps([128, 512], F32, "pgv") for _ in range(2)]
        S.op("dve", lambda e: e.memset(t5x[:], 1.0), writes=[t5b])
        S.dma("sp", t5x[0:32, :], W["t5_bias"], writes=[t5b])
        S.dma("sp", oh[:], CS["c_oh"], writes=[ohb])
        for n in range(NG // 512):
            p, pb = pg[n % 2]
            S.op("pe", lambda e: e.matmul(p[0:8, :], lhsT=t5x[:], rhs=oh[:, n * 512:(n + 1) * 512], start=True, stop=True),
                 reads=[t5b, ohb], writes=[pb])
            S.op("act", lambda e: e.copy(out=gsb[:, n * 512:(n + 1) * 512], in_=p[0:8, :]), reads=[pb], writes=[gsbb])
        S.dma("sp", SC["gvec"], gsb[:], reads=[gsbb])
        S.barrier()
    return ident, ibuf, jrev, jb


def norm_res(C):
    return {
        "hn": [C.sb([128, D], F32, "hn") for _ in range(4)],
        "u": [C.sb([128, D], BF16, "u") for _ in range(4)],
        "ss": [C.sb([128, 16], F32, "ss") for _ in range(2)],
        "ptp": [C.ps([128, 1024], BF16, "ptp") for _ in range(2)],
    }


def phase_mixproj(S, W, CS, SC, l, NSEQ, T, h, ident, ibuf):
    NB = T // 512
    with ExitStack() as st:
        C = Ctx(S, st)
        win, winb = C.sb([128, 8, INW], BF16, "win")
        gain, gb = C.sb([128, D], F32, "gain")
        uTs = [C.sb([128, 8, 512], BF16, "uT") for _ in range(2)]
        res = norm_res(C)
        cw, cwb = C.sb([128, 4, 4], F32, "cw")
        cbias, cbb = C.sb([128, 4], F32, "cb")
        gateb, gtb = C.sb([128, 8], F32, "gateb")
        pscale, pscb = C.sb([128, 256], F32, "pscale")
        wblkf, wfb = C.sb([128, 2, 128], F32, "wblkf")
        wblk, wkb = C.sb([128, 2, 128], BF16, "wblk")
        invc, invb = C.sb([128, 2, 2, 512], F32, "invc")
        tri, trib = C.sb([128, 128], BF16, "tri")
        sel, selb = C.sb([128, 128], BF16, "sel")
        gbr = Rot([C.sb([128, 4, 16], BF16, "gbb") for _ in range(2)])
        Xm = [C.sb([128, 515], F32, "Xm") for _ in range(4)]
        Xp = [C.sb([128, 528], F32, "Xp") for _ in range(2)]
        acc = Rot([C.sb([128, 512], F32, "acc") for _ in range(2)])
        stg = Rot([C.sb([128, 512], BF16, "stg") for _ in range(4)])
        ksg = [C.sb([128, 512], BF16, "ksg") for _ in range(2)]
        ssum = [C.sb([128, 528], F32, "ssum") for _ in range(4)]
        ptmp = Rot([C.sb([128, 512], F32, "ptmp") for _ in range(2)])
        dmT = [C.sb([128, 512], BF16, "dmT") for _ in range(2)]
        mvst = Rot([C.sb([128, 4, 65], BF16, "mvst") for _ in range(2)])
        dvst = Rot([C.sb([128, 4, 65], BF16, "dvst") for _ in range(2)])
        fvst = Rot([C.sb([128, 4, 65], BF16, "fvst") for _ in range(2)])
        ogst = Rot([C.sb([128, 256], BF16, "ogst") for _ in range(2)])
        kgst = Rot([C.sb([128, 256], BF16, "kgst") for _ in range(2)])
        ybst = Rot([C.sb([128, 256], BF16, "ybst") for _ in range(2)])
        agst = Rot([C.sb([128, 4, 8], F32, "agst") for _ in range(2)])
        ebst = Rot([C.sb([128, 4, 4], F32, "ebst") for _ in range(2)])
        gsr = Rot([C.sb([128, 4, 32], F32, "gs") for _ in range(2)])
        pgen = Rot([C.ps([128, 512], F32, "pgen") for _ in range(4)])
        pgp = C.ps([128, 512], F32, "pgp")
        ptk = C.ps([128, 1024], BF16, "ptk")

        load_w_bf16(S, win, winb, W["w_in"][l], 8, INW)
        S.dma("sp", gain[:], bcast_row(W["mix_norm"][l], 128), writes=[gb])
        S.dma("sp", gateb[:], bcast_row(W["mlstm_gate_b"][l], 128), writes=[gtb])
        S.dma("sp", pscale[:], bcast_row(W["pool_scale"][l], 128), writes=[pscb])
        for c in range(4):
            for j in range(4):
                S.dma("sp", cw[:, c, j:j + 1], col1(W["mlstm_conv_w"][l, j, c * 128:(c + 1) * 128]), writes=[cwb])
            S.dma("sp", cbias[:, c:c + 1], col1(W["mlstm_conv_b"][l, c * 128:(c + 1) * 128]), writes=[cbb])
        S.op("dve", lambda e: e.memset(wblkf[:], 0.0), writes=[wfb])
        for g in range(4):
            r0 = (g % 2) * 64
            S.dma("sp", wblkf[r0:r0 + 64, g // 2, r0:r0 + 64], W["pool_w"][l, g], writes=[wfb])
        S.op("dve", lambda e: e.tensor_copy(out=wblk[:], in_=wblkf[:]), reads=[wfb], writes=[wkb])
        S.dma("sp", invc[:], CS["c_invc"], writes=[invb])
        S.dma("pool", tri[:], CS["c_tri"], writes=[trib])
        S.dma("pool", sel[:], CS["c_sel"], writes=[selb])
        for r in (mvst, dvst, fvst):
            for t_, b_ in r.items:
                S.op("dve", lambda e: e.memset(t_[:], 1.0), writes=[b_])

        if MIXSTAGE <= 0:
            S.barrier()
            return
        fm_specs = [("m", 0, 0), ("m", 1, 128), ("m", 2, 256), ("m", 3, 384), ("p", 0, 1032), ("p", 1, 1160),
                    ("d", 4, 1288), ("d", 5, 1416), ("d", 6, 1544), ("d", 7, 1672),
                    ("d", 8, 2056), ("d", 9, 2184), ("d", 10, 2312), ("d", 11, 2440)]
        dscale = {4: 0.125, 5: 0.125, 6: 1.0, 7: 1.0, 8: 32 ** -0.5, 9: 32 ** -0.5, 10: 1.0, 11: 1.0}

        blocks = [(sq, bb) for sq in range(NSEQ) for bb in range(NB)]
        norm_block(S, C, h, 0, 4, gain, gb, ident, ibuf, uTs[0][0], uTs[0][1], res, lq="pool")
        for bi, (seq, b) in enumerate(blocks):
            if True:
                uT, uTb = uTs[bi % 2]
                us_next = None
                if bi + 1 < len(blocks):
                    nsq, nbb = blocks[bi + 1]
                    us_next = norm_pre(S, C, h, nsq * T + nbb * 512, 4, gain, gb, res, lq="pool")
                tok0 = seq * T + b * 512
                tsl = slice(b * 512, (b + 1) * 512)
                if b == 0:
                    for c in range(4):
                        S.op("dve", lambda e: e.memset(Xm[c][0][:, 0:3], 0.0), writes=[Xm[c][1]])
                    for c in range(2):
                        S.op("dve", lambda e: e.memset(Xp[c][0][:, 0:16], 0.0), writes=[Xp[c][1]])
                gs, _ = gsr.next()
                gbt, _ = gbr.next()
                pg, _ = pgp
                gsbs = [Buf() for _ in range(4)]
                gbbs = [Buf() for _ in range(4)]
                pgbs = [pgp[1]] * 4
                agts = [agst.next()]

                gsb_, gbtb, pgb = gsbs[0], gbbs[0], pgbs[0]
                ag_t, ag_b = agts[0]
                PV4 = pg[:, 0:64].rearrange("p (i c) -> p i c", c=16)

                def gateA():
                    for i in range(4):
                        tl = slice(i * 128, (i + 1) * 128)
                        for k in range(8):
                            S.op("pe", lambda e: e.matmul(pg[:, i * 16:i * 16 + 8], lhsT=uT[:, k, tl], rhs=win[:, k, 1024:1032],
                                                          start=(k == 0), stop=(k == 7)), reads=[uTb, winb], writes=[pgb])
                    S.op("dve", lambda e: e.tensor_tensor(out=gs[:, :, 0:8], in0=PV4[:, :, 0:8],
                                                          in1=gateb[:].unsqueeze(1).broadcast_to([128, 4, 8]), op=ALU.add),
                         reads=[pgb, gtb], writes=[gsb_])
                    S.op("act", lambda e: e.activation(out=gs[:, :, 8:12], in_=gs[:, :, 4:8], func=AF.Sigmoid),
                         reads=[gsb_], writes=[gsb_])
                    S.op("act", lambda e: e.activation(out=gs[:, :, 12:16], in_=gs[:, :, 8:12], func=AF.Ln),
                         reads=[gsb_], writes=[gsb_])
                    S.op("dve", lambda e: e.tensor_copy(out=gbt[:, :, 0:4], in_=gs[:, :, 12:16]), reads=[gsb_], writes=[gbtb])
                    S.op("dve", lambda e: e.tensor_copy(out=gs[:, :, 28:32], in_=gbt[:, :, 0:4]), reads=[gbtb], writes=[gsb_])
                    S.op("dve", lambda e: e.tensor_tensor(out=gbt[:, :, 4:8], in0=gs[:, :, 12:16], in1=gs[:, :, 28:32],
                                                          op=ALU.subtract), reads=[gsb_, gbtb], writes=[gbtb])

                def gateB():
                    for i in range(4):
                        S.op("pe", lambda e: e.matmul(pg[:, i * 16 + 8:i * 16 + 12], lhsT=tri[:], rhs=gbt[:, i, 0:4],
                                                      start=True, stop=False), reads=[trib, gbtb], writes=[pgb])
                        S.op("pe", lambda e: e.matmul(pg[:, i * 16 + 8:i * 16 + 12], lhsT=tri[:], rhs=gbt[:, i, 4:8],
                                                      start=False, stop=True), reads=[trib, gbtb], writes=[pgb])
                    S.op("act", lambda e: e.copy(out=gs[:, :, 16:20], in_=PV4[:, :, 8:12]), reads=[pgb], writes=[gsb_])
                    S.op("act", lambda e: e.activation(out=ag_t[:, :, 0:4], in_=gs[:, :, 16:20], func=AF.Exp),
                         reads=[gsb_], writes=[ag_b])
                    S.op("dve", lambda e: e.tensor_tensor(out=gs[:, :, 20:24], in0=gs[:, :, 0:4], in1=gs[:, :, 16:20],
                                                          op=ALU.subtract), reads=[gsb_], writes=[gsb_])
                    S.op("act", lambda e: e.activation(out=ag_t[:, :, 4:8], in_=gs[:, :, 20:24], func=AF.Exp),
                         reads=[gsb_], writes=[ag_b])
                    S.op("dve", lambda e: e.tensor_scalar(out=gs[:, :, 24:28], in0=ag_t[:, :, 4:8], scalar1=0.125, scalar2=None,
                                                          op0=ALU.mult), reads=[ag_b], writes=[gsb_])
                    S.op("dve", lambda e: e.tensor_copy(out=gbt[:, :, 8:12], in_=gs[:, :, 16:20]), reads=[gsb_], writes=[gbtb])
                    S.op("dve", lambda e: e.tensor_copy(out=gs[:, :, 28:32], in_=gbt[:, :, 8:12]), reads=[gbtb], writes=[gsb_])
                    S.op("dve", lambda e: e.tensor_tensor(out=gbt[:, :, 12:16], in0=gs[:, :, 16:20], in1=gs[:, :, 28:32],
                                                          op=ALU.subtract), reads=[gsb_, gbtb], writes=[gbtb])

                def gateC():
                    for i in range(4):
                        S.op("pe", lambda e: e.matmul(pg[:, i * 16 + 12:i * 16 + 16], lhsT=sel[:], rhs=gbt[:, i, 8:12],
                                                      start=True, stop=False), reads=[selb, gbtb], writes=[pgb])
                        S.op("pe", lambda e: e.matmul(pg[:, i * 16 + 12:i * 16 + 16], lhsT=sel[:], rhs=gbt[:, i, 12:16],
                                                      start=False, stop=True), reads=[selb, gbtb], writes=[pgb])
                    eb_t, eb_b = ebst.next()
                    S.op("act", lambda e: e.activation(out=eb_t[:], in_=PV4[:, :, 12:16], func=AF.Exp),
                         reads=[pgb], writes=[eb_b])
                    S.dma("sp", SC["ag"][seq, :, b * 4:(b + 1) * 4, :], ag_t[:], reads=[ag_b])
                    S.dma("sp", SC["eb"][seq, :, b * 4:(b + 1) * 4, :], eb_t[:], reads=[eb_b])

                for fi, (kind, ci, col) in enumerate(fm_specs):
                    if fi == 0:
                        gateA()
                    elif fi == 4:
                        gateB()
                    elif fi == 8:
                        gateC()
                    p, pb = pgen.next()
                    for k in range(8):
                        S.op("pe", lambda e: e.matmul(p[:], lhsT=win[:, k, col:col + 128], rhs=uT[:, k, :],
                                                      start=(k == 0), stop=(k == 7)), reads=[winb, uTb], writes=[pb])
                    if kind == "m":
                        X, Xb = Xm[ci]
                        S.op("act", lambda e: e.copy(out=X[:, 3:515], in_=p[:]), reads=[pb], writes=[Xb])
                        a_, ab_ = acc.next()
                        S.op("dve", lambda e: e.tensor_scalar(out=a_[:], in0=X[:, 3:515], scalar1=cw[:, ci, 3:4],
                                                              scalar2=cbias[:, ci:ci + 1], op0=ALU.mult, op1=ALU.add),
                             reads=[Xb, cwb, cbb], writes=[ab_])
                        for j in range(3):
                            S.op("dve", lambda e: e.scalar_tensor_tensor(out=a_[:], in0=X[:, j:j + 512],
                                                                         scalar=cw[:, ci, j:j + 1], in1=a_[:],
                                                                         op0=ALU.mult, op1=ALU.add),
                                 reads=[Xb, cwb, ab_], writes=[ab_])
                        S.op("dve", lambda e: e.tensor_copy(out=X[:, 0:3], in_=X[:, 512:515]), reads=[Xb], writes=[Xb])
                        if ci < 2:
                            s_, sb_ = stg.next()
                        else:
                            s_, sb_ = ksg[ci - 2]
                        S.op("act", lambda e: e.activation(out=s_[:], in_=a_[:], func=AF.Silu), reads=[ab_], writes=[sb_])
                        S.dma("sp", SC["qkT"][seq, ci * 128:(ci + 1) * 128, tsl], s_[:], reads=[sb_])
                        if ci >= 2:
                            pk, pkb = ptk
                            for i in range(4):
                                o0 = i * 256 + (ci - 2) * 128
                                S.op("pe", lambda e: e.transpose(out=pk[:, o0:o0 + 128], in_=s_[:, i * 128:(i + 1) * 128],
                                                                 identity=ident[:]), reads=[sb_, ibuf], writes=[pkb])
                    elif kind == "p":
                        X, Xb = Xp[ci]
                        S.op("act", lambda e: e.copy(out=X[:, 16:528], in_=p[:]), reads=[pb], writes=[Xb])
                        prev, prevb = X, Xb
                        sh = 1
                        nlev = 2 if ci == 0 else 4
                        levels = []
                        for lev in range(nlev):
                            s_, sb_ = ssum[lev]
                            lo = 2 * sh - 1
                            S.op("dve", lambda e: e.tensor_tensor(out=s_[:, lo:528], in0=prev[:, lo:528],
                                                                  in1=prev[:, lo - sh:528 - sh], op=ALU.add),
                                 reads=[prevb], writes=[sb_])
                            levels.append((s_, sb_))
                            prev, prevb = s_, sb_
                            sh *= 2
                        d_, db_ = dmT[ci]
                        for half in range(2):
                            s_, sb_ = levels[(0 if ci == 0 else 2) + half]
                            rs = slice(half * 64, half * 64 + 64)
                            if b == 0:
                                t_, tb_ = ptmp.next()
                                S.op("dve", lambda e: e.tensor_tensor(out=t_[rs, :], in0=s_[rs, 16:528],
                                                                      in1=invc[rs, 0, ci, :], op=ALU.mult),
                                     reads=[sb_, invb], writes=[tb_])
                                S.op("dve", lambda e: e.tensor_tensor(out=d_[rs, :], in0=t_[rs, :], in1=X[rs, 16:528],
                                                                      op=ALU.subtract), reads=[tb_, Xb], writes=[db_])
                            else:
                                S.op("dve", lambda e: e.scalar_tensor_tensor(out=d_[rs, :], in0=s_[rs, 16:528],
                                                                             scalar=invc[rs, 1, ci, 0:1], in1=X[rs, 16:528],
                                                                             op0=ALU.mult, op1=ALU.subtract),
                                     reads=[sb_, invb, Xb], writes=[db_])
                        S.op("dve", lambda e: e.tensor_copy(out=X[:, 0:16], in_=X[:, 512:528]), reads=[Xb], writes=[Xb])
                    else:
                        s_, sb_ = stg.next()
                        S.op("act", lambda e: e.activation(out=s_[:], in_=p[:], func=AF.Copy, scale=float(dscale[ci])),
                             reads=[pb], writes=[sb_])
                        S.dma("sp", SC["qkT"][seq, ci * 128:(ci + 1) * 128, tsl], s_[:], reads=[sb_])
                for i in range(4):
                    if i == 2 and us_next is not None:
                        norm_post(S, us_next, ident, ibuf, uTs[(bi + 1) % 2][0], uTs[(bi + 1) % 2][1], res)
                    tl = slice(i * 128, (i + 1) * 128)
                    rows = slice(b * 512 + i * 128, b * 512 + (i + 1) * 128)
                    G = gs[:, i, :]
                    k_, kb_ = kgst.next()
                    pk, pkb = ptk
                    S.op("dve", lambda e: e.tensor_tensor(
                        out=k_[:].rearrange("p (h d) -> p h d", h=4),
                        in0=pk[:, i * 256:(i + 1) * 256].rearrange("p (h d) -> p h d", h=4),
                        in1=G[:, 24:28].unsqueeze(2).broadcast_to([128, 4, 64]), op=ALU.mult),
                        reads=[pkb, gsbs[0]], writes=[kb_])
                    S.dma("sp", SC["kg"][seq, rows, :], k_[:], reads=[kb_])
                    pp, ppb = pgen.next()
                    for cc in range(2):
                        S.op("pe", lambda e: e.matmul(pp[:, cc * 128:(cc + 1) * 128], lhsT=dmT[cc][0][:, tl],
                                                      rhs=wblk[:, cc, :], start=True, stop=True),
                             reads=[dmT[cc][1], wkb], writes=[ppb])
                    y_, yb_ = ybst.next()
                    S.op("dve", lambda e: e.tensor_tensor(out=y_[:], in0=pp[:, 0:256], in1=pscale[:], op=ALU.mult),
                         reads=[ppb, pscb], writes=[yb_])
                    S.dma("sp", SC["y"][seq, rows, 256:512], y_[:], reads=[yb_])
                    p1, p1b = pgen.next()
                    for k in range(8):
                        S.op("pe", lambda e: e.matmul(p1[:], lhsT=uT[:, k, tl], rhs=win[:, k, 512:1024],
                                                      start=(k == 0), stop=(k == 7)), reads=[uTb, winb], writes=[p1b])
                    v_, vb_ = mvst.next()
                    S.op("act", lambda e: e.copy(out=v_[:, :, 0:64], in_=p1[:, 0:256].rearrange("p (h d) -> p h d", h=4)),
                         reads=[p1b], writes=[vb_])
                    o_, ob_ = ogst.next()
                    S.op("act", lambda e: e.activation(out=o_[:], in_=p1[:, 256:512], func=AF.Sigmoid),
                         reads=[p1b], writes=[ob_])
                    S.dma("sp", SC["mv1"][seq, rows, :], v_[:].rearrange("p h d -> p (h d)"), reads=[vb_])
                    S.dma("sp", SC["og"][seq, rows, :], o_[:], reads=[ob_])
                    p2, p2b = pgen.next()
                    for gi, col in ((0, 1800), (1, 2568)):
                        for k in range(8):
                            S.op("pe", lambda e: e.matmul(p2[:, gi * 256:(gi + 1) * 256], lhsT=uT[:, k, tl],
                                                          rhs=win[:, k, col:col + 256], start=(k == 0), stop=(k == 7)),
                                 reads=[uTb, winb], writes=[p2b])
                    dv_, dvb_ = dvst.next()
                    fv_, fvb_ = fvst.next()
                    S.op("act", lambda e: e.copy(out=dv_[:, :, 0:64], in_=p2[:, 0:256].rearrange("p (h d) -> p h d", h=4)),
                         reads=[p2b], writes=[dvb_])
                    S.op("act", lambda e: e.copy(out=fv_[:, :, 0:64], in_=p2[:, 256:512].rearrange("p (h d) -> p h d", h=4)),
                         reads=[p2b], writes=[fvb_])
                    S.dma("sp", SC["dv1"][seq, rows, :], dv_[:].rearrange("p h d -> p (h d)"), reads=[dvb_])
                    S.dma("sp", SC["fv1"][seq, rows, :], fv_[:].rearrange("p h d -> p (h d)"), reads=[fvb_])
        S.barrier()


def phase_mlstm(S, W, CS, SC, l, seq, T):
    NCH = T // 128
    with ExitStack() as st:
        C = Ctx(S, st)
        qk, qkb = C.sb([128, 4, T], BF16, "mqk")
        v1, v1b = C.sb([128, NCH, 260], BF16, "mv1")
        kg, kgb = C.sb([128, NCH, 256], BF16, "mkg")
        og, ogb = C.sb([128, NCH, 256], BF16, "mog")
        ag, agb = C.sb([128, NCH, 8], F32, "mag")
        eb, ebb = C.sb([128, NCH, 4], F32, "meb")
        ebp, ebpb = C.sb([128, NCH, 2], F32, "mebp")
        maskT, mkb = C.sb([128, 128], F32, "maskT")
        ng, ngb = C.sb([128, 256], F32, "ng")
        Cf, Cfb = C.sb([128, 2, 130], F32, "Cf")
        Cb, Cbb = C.sb([128, 2, 130], BF16, "Cb")
        tmpU = Rot([C.sb([128, 2, 130], F32, "tmpU") for _ in range(2)])
        PT = Rot([C.sb([128, 128], BF16, "PT") for _ in range(4)])
        nd = Rot([C.sb([128, 2, 4, 65], F32, "nd") for _ in range(2)])
        hh = Rot([C.sb([128, 2, 4, 64], F32, "hh") for _ in range(2)])
        sq = Rot([C.sb([128, 2, 4, 64], F32, "sq") for _ in range(2)])
        stt = Rot([C.sb([128, 2, 32], F32, "stt") for _ in range(2)])
        yst = Rot([C.sb([128, 2, 256], BF16, "yst") for _ in range(2)])
        psc = Rot([C.ps([128, 512], F32, "psc") for _ in range(2)])
        pU = Rot([C.ps([128, 512], F32, "pU") for _ in range(2)])
        po = Rot([C.ps([128, 2, 512], F32, "po") for _ in range(2)])

        for c in range(4):
            S.dma("sp", qk[:, c, :], SC["qkT"][seq, c * 128:(c + 1) * 128, :], writes=[qkb])
        load_tok(S, v1, v1b, SC["mv1"][seq], NCH)
        load_tok(S, kg, kgb, SC["kg"][seq], NCH)
        load_tok(S, og, ogb, SC["og"][seq], NCH)
        S.dma("sp", ag[:], SC["ag"][seq], writes=[agb])
        S.dma("sp", eb[:], SC["eb"][seq], writes=[ebb])
        S.dma("sp", maskT[:], CS["c_maskT"], writes=[mkb])
        S.dma("sp", ng[:], bcast_row(W["mlstm_norm"][l], 128), writes=[ngb])
        for pr in range(2):
            S.op("dve", lambda e: e.tensor_copy(out=ebp[0:64, :, pr], in_=eb[0:64, :, 2 * pr]), reads=[ebb], writes=[ebpb])
            S.op("dve", lambda e: e.tensor_copy(out=ebp[64:128, :, pr], in_=eb[64:128, :, 2 * pr + 1]),
                 reads=[ebb], writes=[ebpb])
        S.op("dve", lambda e: e.memset(Cf[:], 0.0), writes=[Cfb])
        S.op("dve", lambda e: e.memset(Cb[:], 0.0), writes=[Cbb])

        for c in range(NCH):
            if MLSTAGE <= 0:
                break
            cols = slice(c * 128, (c + 1) * 128)
            psA, psB = psc.items
            pts = []
            for hd in range(4):
                pr, hh_ = hd // 2, hd % 2
                rs = slice(hh_ * 64, hh_ * 64 + 64)
                ps_, psb_ = (psA, psB)[hh_]
                S.op("pe", lambda e: e.matmul(ps_[:, pr * 128:(pr + 1) * 128], lhsT=qk[rs, 2 + pr, cols], rhs=qk[rs, pr, cols],
                                              start=True, stop=True), reads=[qkb], writes=[psb_])
            if MLSTAGE <= 1:
                continue
            pu_, pub_ = pU.next()
            for pr in range(2):
                S.op("pe", lambda e: e.matmul(pu_[:, pr * 130:(pr + 1) * 130], lhsT=kg[:, c, pr * 128:(pr + 1) * 128],
                                              rhs=v1[:, c, pr * 130:(pr + 1) * 130], start=True, stop=True),
                     reads=[kgb, v1b], writes=[pub_])
            for hd in range(4):
                p_, pb_ = PT.next()
                ps_, psb_ = (psA, psB)[hd % 2]
                S.op("dve", lambda e: e.scalar_tensor_tensor(out=p_[:], in0=ps_[:, (hd // 2) * 128:(hd // 2 + 1) * 128],
                                                             scalar=ag[:, c, 4 + hd:5 + hd], in1=maskT[:],
                                                             op0=ALU.mult, op1=ALU.mult),
                     reads=[psb_, agb, mkb], writes=[pb_])
                pts.append((p_, pb_))
            if MLSTAGE <= 2:
                continue
            if c % 2 == 0:
                po2_, pob_ = po.next()
            po_ = po2_[:, c % 2, :]
            for pr in range(2):
                S.op("pe", lambda e: e.matmul(po_[:, pr * 130:(pr + 1) * 130], lhsT=qk[:, pr, cols], rhs=Cb[:, pr, :],
                                              start=True, stop=False), reads=[qkb, Cbb], writes=[pob_])
                for hh_ in range(2):
                    hd = 2 * pr + hh_
                    p_, pb_ = pts[hd]
                    S.op("pe", lambda e: e.matmul(po_[:, hd * 65:(hd + 1) * 65], lhsT=p_[:], rhs=v1[:, c, hd * 65:(hd + 1) * 65],
                                                  start=False, stop=True), reads=[pb_, v1b], writes=[pob_])
            if MLSTAGE <= 3:
                continue
            tu, tub = tmpU.next()
            for pr in range(2):
                S.op("act", lambda e: e.activation(out=tu[:, pr, :], in_=pu_[:, pr * 130:(pr + 1) * 130], func=AF.Identity,
                                                   scale=ebp[:, c, pr:pr + 1]), reads=[pub_, ebpb], writes=[tub])
                S.op("dve", lambda e: e.scalar_tensor_tensor(out=Cf[:, pr, :], in0=Cf[:, pr, :], scalar=ebp[:, c, pr:pr + 1],
                                                             in1=tu[:, pr, :], op0=ALU.mult, op1=ALU.add),
                     reads=[Cfb, ebpb, tub], writes=[Cfb])
                S.op("act", lambda e: e.copy(out=Cb[0:64, pr, 0:65], in_=Cf[0:64, pr, 0:65]), reads=[Cfb], writes=[Cbb])
                S.op("act", lambda e: e.copy(out=Cb[64:128, pr, 65:130], in_=Cf[64:128, pr, 65:130]), reads=[Cfb], writes=[Cbb])
            if MLSTAGE <= 4:
                continue
            if c % 2 == 0:
                continue
            c0 = c - 1
            n_, nb_ = nd.next()
            S.op("dve", lambda e: e.tensor_tensor(out=n_[:], in0=po2_[:, :, 0:260].rearrange("p n (h d) -> p n h d", h=4),
                                                  in1=ag[:, c0:c0 + 2, 0:4].unsqueeze(3).broadcast_to([128, 2, 4, 65]), op=ALU.mult),
                 reads=[pob_, agb], writes=[nb_])
            s_, sb_ = stt.next()
            S.op("act", lambda e: e.activation(out=s_[:, :, 0:4].unsqueeze(3), in_=n_[:, :, :, 64:65], func=AF.Abs),
                 reads=[nb_], writes=[sb_])
            S.op("dve", lambda e: e.tensor_scalar(out=s_[:, :, 0:4], in0=s_[:, :, 0:4], scalar1=1.0, scalar2=None,
                                                  op0=ALU.max), reads=[sb_], writes=[sb_])
            S.op("dve", lambda e: e.reciprocal(out=s_[:, :, 4:8], in_=s_[:, :, 0:4]), reads=[sb_], writes=[sb_])
            h_, hb_ = hh.next()
            S.op("dve", lambda e: e.tensor_tensor(out=h_[:], in0=n_[:, :, :, 0:64],
                                                  in1=s_[:, :, 4:8].unsqueeze(3).broadcast_to([128, 2, 4, 64]), op=ALU.mult),
                 reads=[nb_, sb_], writes=[hb_])
            q_, qb_ = sq.next()
            S.op("act", lambda e: e.activation(out=q_[:], in_=h_[:], func=AF.Square), reads=[hb_], writes=[qb_])
            S.op("dve", lambda e: e.tensor_reduce(out=s_[:, :, 8:12], in_=h_[:], axis=AX.X, op=ALU.add), reads=[hb_], writes=[sb_])
            S.op("dve", lambda e: e.tensor_reduce(out=s_[:, :, 12:16], in_=q_[:], axis=AX.X, op=ALU.add), reads=[qb_], writes=[sb_])
            S.op("dve", lambda e: e.tensor_scalar(out=s_[:, :, 16:20], in0=s_[:, :, 8:12], scalar1=1.0 / 64, scalar2=None,
                                                  op0=ALU.mult), reads=[sb_], writes=[sb_])
            S.op("dve", lambda e: e.tensor_tensor(out=s_[:, :, 20:24], in0=s_[:, :, 16:20], in1=s_[:, :, 16:20], op=ALU.mult),
                 reads=[sb_], writes=[sb_])
            S.op("dve", lambda e: e.scalar_tensor_tensor(out=s_[:, :, 24:28], in0=s_[:, :, 12:16], scalar=1.0 / 64,
                                                         in1=s_[:, :, 20:24], op0=ALU.mult, op1=ALU.subtract),
                 reads=[sb_], writes=[sb_])
            S.op("dve", lambda e: e.tensor_scalar(out=s_[:, :, 24:28], in0=s_[:, :, 24:28], scalar1=0.0, scalar2=RMS_EPS,
                                                  op0=ALU.max, op1=ALU.add), reads=[sb_], writes=[sb_])
            S.op("act", lambda e: e.activation(out=s_[:, :, 28:32], in_=s_[:, :, 24:28], func=AF.Sqrt), reads=[sb_], writes=[sb_])
            S.op("dve", lambda e: e.reciprocal(out=s_[:, :, 28:32], in_=s_[:, :, 28:32]), reads=[sb_], writes=[sb_])
            S.op("dve", lambda e: e.tensor_tensor(out=h_[:], in0=h_[:],
                                                  in1=s_[:, :, 16:20].unsqueeze(3).broadcast_to([128, 2, 4, 64]),
                                                  op=ALU.subtract), reads=[hb_, sb_], writes=[hb_])
            S.op("dve", lambda e: e.tensor_tensor(out=h_[:], in0=h_[:],
                                                  in1=s_[:, :, 28:32].unsqueeze(3).broadcast_to([128, 2, 4, 64]),
                                                  op=ALU.mult), reads=[hb_, sb_], writes=[hb_])
            hf = h_[:].rearrange("p n h d -> p n (h d)")
            S.op("dve", lambda e: e.tensor_tensor(out=hf, in0=hf, in1=ng[:].unsqueeze(1).broadcast_to([128, 2, 256]), op=ALU.mult),
                 reads=[hb_, ngb], writes=[hb_])
            y_, yb_ = yst.next()
            S.op("dve", lambda e: e.tensor_tensor(out=y_[:], in0=hf, in1=og[:, c0:c0 + 2, :], op=ALU.mult),
                 reads=[hb_, ogb], writes=[yb_])
            S.dma("pool", SC["y"][seq, c0 * 128:(c0 + 2) * 128, 0:256].rearrange("(n p) c -> p n c", p=128), y_[:], reads=[yb_])
        S.barrier()


def phase_dil(S, W, CS, SC, l, seq, T, jrev, jb):
    with ExitStack() as st:
        C = Ctx(S, st)
        qk, qkb = C.sb([128, 4, T], BF16, "dqk")
        Rd, Rdb = C.sb([128, 3, 4, 256], BF16, "Rd")
        vt = [C.sb([128, 260], BF16, "dvt") for _ in range(6)]
        PT = Rot([C.sb([128, 256], BF16, "dPT") for _ in range(6)])
        PTr = Rot([C.sb([128, 256], BF16, "dPTr") for _ in range(4)])
        Ed, Edb = C.sb([128, 3, 4, 256], BF16, "Ed")
        ost = Rot([C.sb([128, 260], F32, "dost") for _ in range(3)])
        pscs = [Rot([C.ps([128, 512], F32, "dpsc") for _ in range(2)]) for _ in range(2)]
        po = Rot([C.ps([128, 512], F32, "dpo") for _ in range(2)])
        for c in range(4):
            S.dma("sp", qk[:, c, :], SC["qkT"][seq, (4 + c) * 128:(5 + c) * 128, :], writes=[qkb])
        gv = SC["gvec"]
        for p in range(3):
            for hd in range(4):
                src = bass.AP(gv.tensor, gv.offset + hd * NG + p * 384, [[1, 128], [1, 256]])
                S.dma("sp", Rd[:, p, hd, :], src, writes=[Rdb])
        for p in range(3):
            for hd in range(4):
                ps_, psb_ = pscs[hd % 2].next()
                S.op("pe", lambda e: e.matmul(ps_[:, 0:256], lhsT=jrev[:], rhs=Rd[:, p, hd, :], start=True, stop=True),
                     reads=[jb, Rdb], writes=[psb_])
                S.op("act", lambda e: e.activation(out=Ed[:, p, hd, :], in_=ps_[:, 0:256], func=AF.Exp),
                     reads=[psb_], writes=[Edb])
        for p, (w, d) in enumerate(DIL):
            L = T // d
            ntl = L // 128
            for r in range(d):
                grp = {}

                def dscores(i, hd):
                    t0 = r + d * 128 * i
                    ks = [i, i - 1] if i > 0 else [i]
                    nk = len(ks)
                    if hd == 0:
                        for ii in ([0, 1, 2] if i == 0 else [i + 2]):
                            if ii < ntl:
                                v_, vb_ = vt[ii % 6]
                                tt = r + d * 128 * ii
                                S.dma("sp", v_[:], SC["dv1"][seq, tt:tt + d * 127 + 1:d, :], writes=[vb_])
                    ch, rs = hd // 2, slice((hd % 2) * 64, (hd % 2) * 64 + 64)
                    ps_, psb_ = pscs[hd % 2].next()
                    qsl = qk[rs, ch, t0:t0 + d * 127 + 1:d]
                    for jj, j in enumerate(ks):
                        k0 = r + d * 128 * j
                        S.op("pe", lambda e: e.matmul(ps_[:, jj * 128:(jj + 1) * 128], lhsT=qk[rs, 2 + ch, k0:k0 + d * 127 + 1:d],
                                                      rhs=qsl, start=True, stop=True), reads=[qkb], writes=[psb_])
                    pr_, prb_ = PTr.next()
                    S.op("act", lambda e: e.activation(out=pr_[:, 0:nk * 128], in_=ps_[:, 0:nk * 128], func=AF.Exp),
                         reads=[psb_], writes=[prb_])
                    p_, pb_ = PT.next()
                    S.op("dve", lambda e: e.tensor_tensor(out=p_[:, 0:nk * 128], in0=pr_[:, 0:nk * 128],
                                                          in1=Ed[:, p, hd, 0:nk * 128], op=ALU.mult),
                         reads=[prb_, Edb], writes=[pb_])
                    return p_, pb_

                def dpv(i, hd, p_, pb_):
                    t0 = r + d * 128 * i
                    ks = [i, i - 1] if i > 0 else [i]
                    nk = len(ks)
                    if hd == 0:
                        grp[i] = po.next()
                    po_, pob_ = grp[i]
                    for jj, j in enumerate(ks):
                        vj, vjb = vt[j % 6]
                        S.op("pe", lambda e: e.matmul(po_[:, hd * 65:(hd + 1) * 65], lhsT=p_[:, jj * 128:(jj + 1) * 128],
                                                      rhs=vj[:, hd * 65:(hd + 1) * 65], start=(jj == 0), stop=(jj == nk - 1)),
                             reads=[pb_, vjb], writes=[pob_])
                    if hd == 3:
                        o_, ob_ = ost.next()
                        S.op("dve", lambda e: e.tensor_copy(out=o_[:], in_=po_[:, 0:260]), reads=[pob_], writes=[ob_])
                        S.dma("pool", SC["dacc"][seq, p, t0:t0 + d * 127 + 1:d, :], o_[:], reads=[ob_])
                        del grp[i]

                units = [(i, hd) for i in range(ntl) for hd in range(4)]
                LA = 3
                pend = {}
                nxt = 0
                for u in range(len(units)):
                    while nxt < len(units) and nxt <= u + LA:
                        pend[nxt] = dscores(*units[nxt])
                        nxt += 1
                    dpv(*units[u], *pend.pop(u))
        S.barrier()
        ld = Rot([C.sb([128, 3, 260], F32, "dld") for _ in range(2)])
        yst = Rot([C.sb([128, 256], BF16, "dyst") for _ in range(2)])
        rd = Rot([C.sb([128, 4], F32, "drd") for _ in range(2)])
        for n in range(T // 128):
            rows = slice(n * 128, (n + 1) * 128)
            a_, ab_ = ld.next()
            S.dma("sp", a_[:], SC["dacc"][seq, :, rows, :].rearrange("t p c -> p t c"), writes=[ab_])
            S.op("dve", lambda e: e.tensor_tensor(out=a_[:, 0, :], in0=a_[:, 0, :], in1=a_[:, 1, :], op=ALU.add),
                 reads=[ab_], writes=[ab_])
            S.op("dve", lambda e: e.tensor_tensor(out=a_[:, 0, :], in0=a_[:, 0, :], in1=a_[:, 2, :], op=ALU.add),
                 reads=[ab_], writes=[ab_])
            r_, rb_ = rd.next()
            av = a_[:, 0, :].rearrange("p (h d) -> p h d", h=4)
            S.op("dve", lambda e: e.reciprocal(out=r_[:].unsqueeze(2), in_=av[:, :, 64:65]), reads=[ab_], writes=[rb_])
            y_, yb_ = yst.next()
            S.op("dve", lambda e: e.tensor_tensor(out=y_[:].rearrange("p (h d) -> p h d", h=4), in0=av[:, :, 0:64],
                                                  in1=r_[:].unsqueeze(2).broadcast_to([128, 4, 64]), op=ALU.mult),
                 reads=[ab_, rb_], writes=[yb_])
            S.dma("pool", SC["y"][seq, rows, 512:768], y_[:], reads=[yb_])
        S.barrier()


def phase_diff(S, W, CS, SC, l, seq, T, jrev, jb):
    NCH = T // 128
    lam_init = 0.8 - 0.6 * math.exp(-0.3 * l)
    with ExitStack() as st:
        C = Ctx(S, st)
        qk, qkb = C.sb([64, 8, T], BF16, "fqk")
        v1, v1b = C.sb([128, NCH, 260], BF16, "fv1")
        Rf, Rfb = C.sb([128, 4, 3072], BF16, "Rf")
        Ef, Efb = C.sb([128, 4, 3072], BF16, "Ef")
        PTr = Rot([C.sb([128, 512], BF16, "fPTr") for _ in range(6)])
        lv, lvb = C.sb([1, 160], F32, "lv")
        ones1, o1b = C.sb([1, 128], F32, "ones1")
        nlam, nlb = C.sb([128, 1], F32, "nlam")
        subg, sgb = C.sb([128, 64], F32, "subg")
        zl, zlb = C.sb([1, 128], BF16, "zl")
        zr, zrb = C.sb([1, 260], BF16, "zr")
        PT = Rot([C.sb([128, 512], BF16, "fPT") for _ in range(10)])
        om = [C.sb([128, 4, 65], F32, "om") for _ in range(2)]
        om2 = [C.sb([128, 4, 64], F32, "om2") for _ in range(2)]
        rdn = Rot([C.sb([128, 8], F32, "rdn") for _ in range(4)])
        od = Rot([C.sb([128, 4, 64], F32, "od") for _ in range(2)])
        sq = Rot([C.sb([128, 4, 64], F32, "fsq") for _ in range(2)])
        yst = Rot([C.sb([128, 4, 64], BF16, "fyst") for _ in range(2)])
        pscs = [Rot([C.ps([128, 512], F32, "fpsc") for _ in range(2)]) for _ in range(2)]
        po = Rot([C.ps([128, 512], F32, "fpo") for _ in range(4)])
        plam = po.items[0]

        for c in range(8):
            S.dma("sp", qk[:, c, :], SC["qkT"][seq, 1024 + c * 64:1024 + (c + 1) * 64, :], writes=[qkb])
        load_tok(S, v1, v1b, SC["fv1"][seq], NCH)
        gv = SC["gvec"]
        for hd in range(4):
            for part in range(2):
                src = bass.AP(gv.tensor, gv.offset + (4 + hd) * NG + 1152 + part * 1536, [[1, 128], [1, 1536]])
                S.dma("sp", Rf[:, hd, part * 1536:(part + 1) * 1536], src, writes=[Rfb])
        for hd in range(4):
            for n in range(6):
                ps_, psb_ = pscs[n % 2].next()
                S.op("pe", lambda e: e.matmul(ps_[:], lhsT=jrev[:], rhs=Rf[:, hd, n * 512:(n + 1) * 512], start=True, stop=True),
                     reads=[jb, Rfb], writes=[psb_])
                S.op("act", lambda e: e.activation(out=Ef[:, hd, n * 512:(n + 1) * 512], in_=ps_[:], func=AF.Exp),
                     reads=[psb_], writes=[Efb])
        S.dma("sp", lv[:, 0:128], W["diff_lambda"][l].rearrange("(o a) b -> o (a b)", o=1), writes=[lvb])
        S.op("dve", lambda e: e.tensor_tensor(out=lv[:, 128:160], in0=lv[:, 0:32], in1=lv[:, 32:64], op=ALU.mult),
             reads=[lvb], writes=[lvb])
        S.op("dve", lambda e: e.tensor_reduce(out=lv[:, 0:1], in_=lv[:, 128:160], axis=AX.X, op=ALU.add), reads=[lvb], writes=[lvb])
        S.op("dve", lambda e: e.tensor_tensor(out=lv[:, 128:160], in0=lv[:, 64:96], in1=lv[:, 96:128], op=ALU.mult),
             reads=[lvb], writes=[lvb])
        S.op("dve", lambda e: e.tensor_reduce(out=lv[:, 1:2], in_=lv[:, 128:160], axis=AX.X, op=ALU.add), reads=[lvb], writes=[lvb])
        S.op("act", lambda e: e.activation(out=lv[:, 2:4], in_=lv[:, 0:2], func=AF.Exp), reads=[lvb], writes=[lvb])
        S.op("dve", lambda e: e.scalar_tensor_tensor(out=lv[:, 4:5], in0=lv[:, 3:4], scalar=-float(lam_init), in1=lv[:, 2:3],
                                                     op0=ALU.add, op1=ALU.subtract), reads=[lvb], writes=[lvb])
        S.op("dve", lambda e: e.memset(ones1[:], 1.0), writes=[o1b])
        pl, plb = plam
        S.op("pe", lambda e: e.matmul(pl[:, 0:1], lhsT=ones1[:], rhs=lv[:, 4:5], start=True, stop=True),
             reads=[o1b, lvb], writes=[plb])
        S.op("act", lambda e: e.copy(out=nlam[:], in_=pl[:, 0:1]), reads=[plb], writes=[nlb])
        S.dma("sp", subg[:], bcast_row(W["diff_subln"][l], 128), writes=[sgb])
        S.op("dve", lambda e: e.tensor_scalar(out=subg[:], in0=subg[:], scalar1=float(1.0 - lam_init), scalar2=None,
                                              op0=ALU.mult), reads=[sgb], writes=[sgb])
        S.op("dve", lambda e: e.memset(zl[:], 0.0), writes=[zlb])
        S.op("dve", lambda e: e.memset(zr[:], 0.0), writes=[zrb])

        for i in range(T // 512):
            for hd in range(4):
                pos = [po.next(), po.next()]
                for m in range(2):
                    S.op("pe", lambda e: e.matmul(pos[m][0][:, 0:260], lhsT=zl[:], rhs=zr[:], start=True, stop=False),
                         reads=[zlb, zrb], writes=[pos[m][1]])
                nk = 4 * i + 4

                def scores(j):
                    jj = j - 4 * i
                    q0 = max(jj, 0) * 128
                    o = 512 * i - 128 * j
                    x0 = min(o, 2176) + 384 + q0
                    pss = []
                    for m in range(2):
                        rs = slice(32 * m, 32 * m + 32)
                        ps_, psb_ = pscs[m].next()
                        S.op("pe", lambda e: e.matmul(ps_[:, q0:512], lhsT=qk[rs, 4 + hd, j * 128:(j + 1) * 128],
                                                      rhs=qk[rs, hd, i * 512 + q0:(i + 1) * 512], start=True, stop=True),
                             reads=[qkb], writes=[psb_])
                        pss.append((ps_, psb_))
                    prs = []
                    for m in range(2):
                        ps_, psb_ = pss[m]
                        pr_, prb_ = PTr.next()
                        S.op("act", lambda e: e.activation(out=pr_[:, q0:512], in_=ps_[:, q0:512], func=AF.Exp),
                             reads=[psb_], writes=[prb_])
                        prs.append((pr_, prb_))
                    outs = []
                    for m in range(2):
                        pr_, prb_ = prs[m]
                        p_, pb_ = PT.next()
                        S.op("dve", lambda e: e.tensor_tensor(out=p_[:, q0:512], in0=pr_[:, q0:512],
                                                              in1=Ef[:, hd, x0:x0 + 512 - q0], op=ALU.mult),
                             reads=[prb_, Efb], writes=[pb_])
                        outs.append((p_, pb_))
                    return outs, q0

                def pv(j, outs, q0):
                    for m in range(2):
                        p_, pb_ = outs[m]
                        po_, pob_ = pos[m]
                        for s in range(q0 // 128, 4):
                            S.op("pe", lambda e: e.matmul(po_[:, s * 65:(s + 1) * 65], lhsT=p_[:, s * 128:(s + 1) * 128],
                                                          rhs=v1[:, j, hd * 65:(hd + 1) * 65], start=False,
                                                          stop=(j == 4 * i + s)), reads=[pb_, v1b], writes=[pob_])

                LA = 2
                pend = {}
                nxt = 0
                for j in range(nk):
                    while nxt < nk and nxt <= j + LA:
                        pend[nxt] = scores(nxt)
                        nxt += 1
                    pv(j, *pend.pop(j))
                for m in range(2):
                    po_, pob_ = pos[m]
                    o_, ob_ = om[m]
                    S.op("act", lambda e: e.copy(out=o_[:], in_=po_[:, 0:260].rearrange("p (s d) -> p s d", s=4)),
                         reads=[pob_], writes=[ob_])
                    r_, rb_ = rdn.next()
                    S.op("dve", lambda e: e.reciprocal(out=r_[:, 0:4].unsqueeze(2), in_=o_[:, :, 64:65]), reads=[ob_], writes=[rb_])
                    o2_, o2b_ = om2[m]
                    S.op("dve", lambda e: e.tensor_tensor(out=o2_[:], in0=o_[:, :, 0:64],
                                                          in1=r_[:, 0:4].unsqueeze(2).broadcast_to([128, 4, 64]), op=ALU.mult),
                         reads=[ob_, rb_], writes=[o2b_])
                d_, db_ = od.next()
                S.op("dve", lambda e: e.scalar_tensor_tensor(out=d_[:], in0=om2[1][0][:], scalar=nlam[:, 0:1], in1=om2[0][0][:],
                                                             op0=ALU.mult, op1=ALU.add),
                     reads=[om2[0][1], om2[1][1], nlb], writes=[db_])
                q_, qb_ = sq.next()
                S.op("act", lambda e: e.activation(out=q_[:], in_=d_[:], func=AF.Square), reads=[db_], writes=[qb_])
                r_, rb_ = rdn.next()
                S.op("dve", lambda e: e.tensor_reduce(out=r_[:, 0:4], in_=q_[:], axis=AX.X, op=ALU.add), reads=[qb_], writes=[rb_])
                S.op("dve", lambda e: e.tensor_scalar(out=r_[:, 0:4], in0=r_[:, 0:4], scalar1=1.0 / 64, scalar2=1e-5,
                                                      op0=ALU.mult, op1=ALU.add), reads=[rb_], writes=[rb_])
                S.op("act", lambda e: e.activation(out=r_[:, 4:8], in_=r_[:, 0:4], func=AF.Sqrt), reads=[rb_], writes=[rb_])
                S.op("dve", lambda e: e.reciprocal(out=r_[:, 4:8], in_=r_[:, 4:8]), reads=[rb_], writes=[rb_])
                S.op("dve", lambda e: e.tensor_tensor(out=d_[:], in0=d_[:], in1=r_[:, 4:8].unsqueeze(2).broadcast_to([128, 4, 64]),
                                                      op=ALU.mult), reads=[db_, rb_], writes=[db_])
                y_, yb_ = yst.next()
                S.op("dve", lambda e: e.tensor_tensor(out=y_[:], in0=d_[:], in1=subg[:].unsqueeze(1).broadcast_to([128, 4, 64]),
                                                      op=ALU.mult), reads=[db_, sgb], writes=[yb_])
                S.dma("pool", SC["y"][seq, i * 512:(i + 1) * 512, 768 + hd * 64:832 + hd * 64].rearrange("(s p) c -> p s c", p=128),
                      y_[:], reads=[yb_])
        S.barrier()


def phase_outproj(S, W, SC, l, NSEQ, T, h, ident, ibuf):
    with ExitStack() as st:
        C = Ctx(S, st)
        wo, wob = C.sb([128, 8, D], BF16, "wout")
        load_w_bf16(S, wo, wob, W["w_out"][l], 8, D)
        yt = Rot([C.sb([128, D], BF16, "yt") for _ in range(2)])
        yT = Rot([C.sb([128, 8, 128], BF16, "yT") for _ in range(2)])
        ht = Rot([C.sb([128, D], F32, "oht") for _ in range(3)])
        ptp = Rot([C.ps([128, 1024], BF16, "optp") for _ in range(2)])
        pso = Rot([C.ps([128, 512], F32, "opso") for _ in range(4)])
        for seq in range(NSEQ):
            for n in range(T // 128):
                rows = slice(n * 128, (n + 1) * 128)
                hrows = slice(seq * T + n * 128, seq * T + (n + 1) * 128)
                y_, yb_ = yt.next()
                S.dma("sp", y_[:], SC["y"][seq, rows, :], writes=[yb_])
                h_, hb_ = ht.next()
                S.dma("sp", h_[:], h[hrows, :], writes=[hb_])
                pt, ptb = ptp.next()
                for c in range(8):
                    S.op("pe", lambda e: e.transpose(out=pt[:, c * 128:(c + 1) * 128], in_=y_[:, c * 128:(c + 1) * 128],
                                                     identity=ident[:]), reads=[yb_, ibuf], writes=[ptb])
                yT_, yTb_ = yT.next()
                S.op("act", lambda e: e.copy(out=yT_[:, 0:4, :], in_=pt[:, 0:512].rearrange("p (c t) -> p c t", c=4)),
                     reads=[ptb], writes=[yTb_])
                S.op("dve", lambda e: e.tensor_copy(out=yT_[:, 4:8, :], in_=pt[:, 512:1024].rearrange("p (c t) -> p c t", c=4)),
                     reads=[ptb], writes=[yTb_])
                for nn in range(2):
                    p_, pb_ = pso.next()
                    for k in range(8):
                        S.op("pe", lambda e: e.matmul(p_[:], lhsT=yT_[:, k, :], rhs=wo[:, k, nn * 512:(nn + 1) * 512],
                                                      start=(k == 0), stop=(k == 7)), reads=[yTb_, wob], writes=[pb_])
                    S.op("dve", lambda e: e.tensor_tensor(out=h_[:, nn * 512:(nn + 1) * 512], in0=p_[:],
                                                          in1=h_[:, nn * 512:(nn + 1) * 512], op=ALU.add),
                         reads=[pb_, hb_], writes=[hb_])
                S.dma("pool", h[hrows, :], h_[:], reads=[hb_])
        S.barrier()


def phase_xattn(S, W, l, NSEQ, T, MEM, h, mem, ident, ibuf, xw):
    NB = T // 512
    NMT = MEM // 128
    with ExitStack() as st:
        C = Ctx(S, st)
        (wq, wqb), (wkv, wkvb), (wo, wob) = xw
        xg, xgb = C.sb([128, D], F32, "xg")
        mg, mgb = C.sb([128, D], F32, "mg")
        res = norm_res(C)
        uTs = [C.sb([128, 8, 512], BF16, "xuT") for _ in range(2)]
        mT, mTb = C.sb([128, 8, MEM], BF16, "mT")
        KT, KTb = C.sb([128, 8, MEM], BF16, "KT")
        V1, V1b = C.sb([128, NMT, 4, 257], BF16, "V1")
        qT, qTb = C.sb([128, 8, 512], BF16, "qT")
        PT = Rot([C.sb([128, 512], BF16, "xPT") for _ in range(6)])
        osb, osbb = C.sb([128, 4, D], BF16, "osb")
        oT, oTb = C.sb([128, 8, 512], BF16, "oT")
        rd = Rot([C.sb([128, 1], F32, "xrd") for _ in range(4)])
        ht = Rot([C.sb([128, D], F32, "xht") for _ in range(8)])
        oTbs = [Buf() for _ in range(4)]
        osbs = [Buf() for _ in range(4)]
        psq = Rot([C.ps([128, 512], F32, "psq") for _ in range(2)])
        pov = Rot([C.ps([128, 512], F32, "pov") for _ in range(4)])
        pwo = pov
        S.dma("sp", xg[:], bcast_row(W["xattn_norm"][l], 128), writes=[xgb])
        S.dma("sp", mg[:], bcast_row(W["mem_norm"][l], 128), writes=[mgb])
        S.op("dve", lambda e: e.memset(V1[:], 1.0), writes=[V1b])
        for seq in range(NSEQ):
            norm_block(S, C, mem, seq * MEM, NMT, mg, mgb, ident, ibuf, mT, mTb, res)
            for c in range(8):
                p_, pb_ = psq.next()
                for k in range(8):
                    S.op("pe", lambda e: e.matmul(p_[:, 0:MEM], lhsT=wkv[:, k, c * 128:(c + 1) * 128], rhs=mT[:, k, :],
                                                  start=(k == 0), stop=(k == 7)), reads=[wkvb, mTb], writes=[pb_])
                S.op("act", lambda e: e.copy(out=KT[:, c, :], in_=p_[:, 0:MEM]), reads=[pb_], writes=[KTb])
            for j in range(NMT):
                for nn in range(2):
                    p_, pb_ = psq.next()
                    for k in range(8):
                        S.op("pe", lambda e: e.matmul(p_[:], lhsT=mT[:, k, j * 128:(j + 1) * 128],
                                                      rhs=wkv[:, k, D + nn * 512:D + (nn + 1) * 512],
                                                      start=(k == 0), stop=(k == 7)), reads=[wkvb, mTb], writes=[pb_])
                    S.op("act", lambda e: e.copy(out=V1[:, j, 2 * nn:2 * nn + 2, 0:256],
                                                 in_=p_[:].rearrange("p (h d) -> p h d", h=2)),
                         reads=[pb_], writes=[V1b])
            norm_block(S, C, h, seq * T, 4, xg, xgb, ident, ibuf, uTs[0][0], uTs[0][1], res)
            for b in range(NB):
                uT, uTb = uTs[b % 2]
                tok0 = seq * T + b * 512
                hts = [ht.next() for _ in range(4)]
                for s4 in range(4):
                    S.dma("sp", hts[s4][0][:], h[tok0 + s4 * 128:tok0 + (s4 + 1) * 128, :], writes=[hts[s4][1]])
                for c in range(8):
                    p_, pb_ = psq.next()
                    for k in range(8):
                        S.op("pe", lambda e: e.matmul(p_[:], lhsT=wq[:, k, c * 128:(c + 1) * 128], rhs=uT[:, k, :],
                                                      start=(k == 0), stop=(k == 7)), reads=[wqb, uTb], writes=[pb_])
                    S.op("act", lambda e: e.activation(out=qT[:, c, :], in_=p_[:], func=AF.Copy, scale=1.0 / 16),
                         reads=[pb_], writes=[qTb])
                def xscores(hd):
                    pts = []
                    for j in range(NMT):
                        p_, pb_ = psq.next()
                        for cc in range(2):
                            S.op("pe", lambda e: e.matmul(p_[:], lhsT=KT[:, 2 * hd + cc, j * 128:(j + 1) * 128],
                                                          rhs=qT[:, 2 * hd + cc, :], start=(cc == 0), stop=(cc == 1)),
                                 reads=[KTb, qTb], writes=[pb_])
                        t_, tb_ = PT.next()
                        S.op("act", lambda e: e.activation(out=t_[:], in_=p_[:], func=AF.Exp), reads=[pb_], writes=[tb_])
                        pts.append((t_, tb_))
                    return pts

                def xpv(hd, pts):
                    for s in range(4):
                        p_, pb_ = pov.next()
                        for j in range(NMT):
                            t_, tb_ = pts[j]
                            S.op("pe", lambda e: e.matmul(p_[:, 0:257], lhsT=t_[:, s * 128:(s + 1) * 128], rhs=V1[:, j, hd, :],
                                                          start=(j == 0), stop=(j == NMT - 1)), reads=[tb_, V1b], writes=[pb_])
                        r_, rb_ = rd.next()
                        S.op("dve", lambda e: e.reciprocal(out=r_[:], in_=p_[:, 256:257]), reads=[pb_], writes=[rb_])
                        if s % 2 == 0:
                            S.op("act", lambda e: e.activation(out=osb[:, s, hd * 256:(hd + 1) * 256], in_=p_[:, 0:256],
                                                               func=AF.Identity, scale=r_[:, 0:1]),
                                 reads=[pb_, rb_], writes=[osbs[s]])
                        else:
                            S.op("dve", lambda e: e.tensor_scalar(out=osb[:, s, hd * 256:(hd + 1) * 256], in0=p_[:, 0:256],
                                                                  scalar1=r_[:, 0:1], scalar2=None, op0=ALU.mult),
                                 reads=[pb_, rb_], writes=[osbs[s]])

                prevp = xscores(0)
                for hd in range(1, 4):
                    curp = xscores(hd)
                    xpv(hd - 1, prevp)
                    prevp = curp
                xpv(3, prevp)
                if b + 1 < NB:
                    norm_block(S, C, h, tok0 + 512, 4, xg, xgb, ident, ibuf, uTs[(b + 1) % 2][0], uTs[(b + 1) % 2][1], res)
                for s in range(4):
                    pt, ptb = res["ptp"][s % 2]
                    for c in range(8):
                        S.op("pe", lambda e: e.transpose(out=pt[:, c * 128:(c + 1) * 128], in_=osb[:, s, c * 128:(c + 1) * 128],
                                                         identity=ident[:]), reads=[osbs[s], ibuf], writes=[ptb])
                    S.op("act", lambda e: e.copy(out=oT[:, 0:4, s * 128:(s + 1) * 128],
                                                 in_=pt[:, 0:512].rearrange("p (c t) -> p c t", c=4)), reads=[ptb], writes=[oTbs[s]])
                    S.op("dve", lambda e: e.tensor_copy(out=oT[:, 4:8, s * 128:(s + 1) * 128],
                                                        in_=pt[:, 512:1024].rearrange("p (c t) -> p c t", c=4)),
                         reads=[ptb], writes=[oTbs[s]])
                for s in range(4):
                    hrows = slice(tok0 + s * 128, tok0 + (s + 1) * 128)
                    h_, hb_ = hts[s]
                    for nn in range(2):
                        p_, pb_ = pwo.next()
                        for k in range(8):
                            S.op("pe", lambda e: e.matmul(p_[:], lhsT=oT[:, k, s * 128:(s + 1) * 128],
                                                          rhs=wo[:, k, nn * 512:(nn + 1) * 512], start=(k == 0), stop=(k == 7)),
                                 reads=[oTbs[s], wob], writes=[pb_])
                        S.op("dve", lambda e: e.tensor_tensor(out=h_[:, nn * 512:(nn + 1) * 512], in0=p_[:],
                                                              in1=h_[:, nn * 512:(nn + 1) * 512], op=ALU.add),
                             reads=[pb_, hb_], writes=[hb_])
                    S.dma("pool", h[hrows, :], h_[:], reads=[hb_])
        S.barrier()


def phase_final(S, W, ntok, h, out):
    with ExitStack() as st:
        C = Ctx(S, st)
        gain, gb = C.sb([128, D], F32, "fgain")
        S.dma("sp", gain[:], bcast_row(W["final_norm"], 128), writes=[gb])
        ht = Rot([C.sb([128, D], F32, "fh") for _ in range(3)])
        jk = Rot([C.sb([128, D], BF16, "fjk") for _ in range(2)])
        ss = Rot([C.sb([128, 4], F32, "fss") for _ in range(3)])
        for n in range(ntok // 128):
            rows = slice(n * 128, (n + 1) * 128)
            h_, hb_ = ht.next()
            j_, jb_ = jk.next()
            s_, sb_ = ss.next()
            S.dma("sp", h_[:], h[rows, :], writes=[hb_])
            S.op("act", lambda e: e.activation(out=j_[:], in_=h_[:], func=AF.Square, accum_out=s_[:, 0:1]),
                 reads=[hb_], writes=[jb_, sb_])
            S.op("dve", lambda e: e.tensor_scalar(out=s_[:, 1:2], in0=s_[:, 0:1], scalar1=1.0 / D, scalar2=RMS_EPS,
                                                  op0=ALU.mult, op1=ALU.add), reads=[sb_], writes=[sb_])
            S.op("act", lambda e: e.activation(out=s_[:, 2:3], in_=s_[:, 1:2], func=AF.Sqrt), reads=[sb_], writes=[sb_])
            S.op("dve", lambda e: e.reciprocal(out=s_[:, 3:4], in_=s_[:, 2:3]), reads=[sb_], writes=[sb_])
            S.op("dve", lambda e: e.scalar_tensor_tensor(out=h_[:], in0=h_[:], scalar=s_[:, 3:4], in1=gain[:],
                                                         op0=ALU.mult, op1=ALU.mult), reads=[hb_, sb_, gb], writes=[hb_])
            S.dma("pool", out[rows, :], h_[:], reads=[hb_])
        S.barrier()


WEIGHT_SHAPES = {
    "t5_bias": (32, 8), "ffn1_norm": (4, D), "ffn1_w_gate": (4, D, DFF), "ffn1_w_up": (4, D, DFF),
    "ffn1_w_down": (4, DFF, D), "mix_norm": (4, D), "w_in": (4, D, INW), "mlstm_conv_w": (4, 4, 512),
    "mlstm_conv_b": (4, 512), "mlstm_gate_b": (4, 8), "mlstm_norm": (4, 256), "pool_w": (4, 4, 64, 64),
    "pool_scale": (4, 256), "diff_lambda": (4, 4, 32), "diff_subln": (4, 64), "w_out": (4, D, D),
    "xattn_norm": (4, D), "mem_norm": (4, D), "xattn_wq": (4, D, D), "xattn_wkv": (4, D, 2 * D),
    "xattn_wo": (4, D, D), "ffn2_norm": (4, D), "ffn2_w_gate": (4, D, DFF), "ffn2_w_up": (4, D, DFF),
    "ffn2_w_down": (4, DFF, D), "final_norm": (D,),
}

ALL_PHASES = ("ffn1", "mixproj", "mlstm", "dil", "diff", "outproj", "xattn", "ffn2", "final")


def build(T=4096, NSEQ=2, DEPTH=4, MEM=256, phases=ALL_PHASES, debug=()):
    nc = bass.Bass("TRN2", target_bir_lowering=False)
    ntok = T * NSEQ
    x = nc.dram_tensor("x", [ntok, D], F32, kind="ExternalInput").ap()
    mem = nc.dram_tensor("mem", [NSEQ * MEM, D], F32, kind="ExternalInput").ap()
    W = {k: nc.dram_tensor(k, list(s), F32, kind="ExternalInput").ap() for k, s in WEIGHT_SHAPES.items()}
    CS = {k: nc.dram_tensor(k, list(s), F32, kind="ExternalInput").ap() for k, s in CONST_SHAPES.items()}
    out = nc.dram_tensor("out", [ntok, D], F32, kind="ExternalOutput").ap()

    def scratch(name, shape, dt):
        kind = "ExternalOutput" if name in debug else "Internal"
        return nc.dram_tensor("s_" + name, list(shape), dt, kind=kind).ap()

    SC = {
        "qkT": scratch("qkT", [NSEQ, 12 * 128, T], BF16),
        "kg": scratch("kg", [NSEQ, T, 256], BF16),
        "mv1": scratch("mv1", [NSEQ, T, 260], BF16),
        "og": scratch("og", [NSEQ, T, 256], BF16),
        "ag": scratch("ag", [NSEQ, 128, T // 128, 8], F32),
        "eb": scratch("eb", [NSEQ, 128, T // 128, 4], F32),
        "dv1": scratch("dv1", [NSEQ, T, 260], BF16),
        "fv1": scratch("fv1", [NSEQ, T, 260], BF16),
        "dacc": scratch("dacc", [NSEQ, 3, T, 260], F32),
        "y": scratch("y", [NSEQ, T, D], BF16),
        "gvec": scratch("gvec", [8, NG], BF16),
        "dbg": scratch("dbg", [128, 64], F32),
    }
    h = out
    with ExitStack() as st:
        S = Sched(nc, st)
        C0 = Ctx(S, st)
        ident, ibuf, jrev, jb = setup_consts(S, C0, W, CS, SC)
        for l in range(DEPTH):
            if "ffn1" in phases:
                phase_ffn(S, W, "ffn1", l, x if l == 0 else h, h, ntok, ident, ibuf)
            if "mixproj" in phases:
                phase_mixproj(S, W, CS, SC, l, NSEQ, T, h, ident, ibuf)
            for seq in range(NSEQ):
                if "mlstm" in phases:
                    phase_mlstm(S, W, CS, SC, l, seq, T)
                if "dil" in phases:
                    phase_dil(S, W, CS, SC, l, seq, T, jrev, jb)
                if "diff" in phases:
                    phase_diff(S, W, CS, SC, l, seq, T, jrev, jb)
            with ExitStack() as xst:
                Cx = Ctx(S, xst)
                xw = None
                if "xattn" in phases:
                    xw = (Cx.sb([128, 8, D], BF16, "wq"), Cx.sb([128, 8, 2 * D], BF16, "wkv"), Cx.sb([128, 8, D], BF16, "wo"))
                    load_w_bf16(S, xw[0][0], xw[0][1], W["xattn_wq"][l], 8, D)
                    load_w_bf16(S, xw[1][0], xw[1][1], W["xattn_wkv"][l], 8, 2 * D)
                    load_w_bf16(S, xw[2][0], xw[2][1], W["xattn_wo"][l], 8, D)
                if "outproj" in phases:
                    phase_outproj(S, W, SC, l, NSEQ, T, h, ident, ibuf)
                if "xattn" in phases:
                    phase_xattn(S, W, l, NSEQ, T, MEM, h, mem, ident, ibuf, xw)
            if "ffn2" in phases:
                phase_ffn(S, W, "ffn2", l, h, h, ntok, ident, ibuf)
        if "final" in phases:
            phase_final(S, W, ntok, h, out)
        S.barrier()
        print("instructions:", S.ninst, {e: c for e, c in S.cnt.items()})
    return nc


_CONSTS = None


def kernel(**inputs):
    global _CONSTS
    if _CONSTS is None:
        _CONSTS = host_consts()
    x = np.ascontiguousarray(inputs["x"], dtype=np.float32)
    mem = np.ascontiguousarray(inputs["mem"], dtype=np.float32)
    B, T, _ = x.shape
    MEM = mem.shape[1]
    ncores = 8
    nseq = B // ncores
    nc = build(T=T, NSEQ=nseq, DEPTH=4, MEM=MEM)
    base = {k: np.ascontiguousarray(inputs[k], dtype=np.float32) for k in WEIGHT_SHAPES}
    base.update(_CONSTS)
    in_maps = []
    for c in range(ncores):
        m = dict(base)
        m["x"] = x[c * nseq:(c + 1) * nseq].reshape(nseq * T, D)
        m["mem"] = mem[c * nseq:(c + 1) * nseq].reshape(nseq * MEM, D)
        in_maps.append(m)
    res = run_bass_kernel_spmd(nc, in_maps, core_ids=list(range(ncores)))
    outs = [np.asarray(r["out"], dtype=np.float32).reshape(nseq, T, D) for r in res.results]
    return np.concatenate(outs, axis=0)
```
